# Optimizing a Trainium2 kernel written in Bass

```python
import math
import jax, jax.numpy as jnp
from jax import lax
import numpy as np

D_MODEL = 1024
BATCH = 2
SEQ = 8192
DEPTH = 4

GRID_W = 64
CTX_LEN = 256
N_HEADS = 4
HEAD_DIM = 64
V_DIM = 2 * HEAD_DIM
QK_WIDTH = N_HEADS * 2 * HEAD_DIM
ATTN_WIDTH = N_HEADS * V_DIM
SSM_WIDTH = D_MODEL // 4
SSM_P = 16
SSM_G = SSM_WIDTH // SSM_P
SSM_N = 64
FNET_WIDTH = D_MODEL // 4
FNET_G = 4
FNET_C = FNET_WIDTH // FNET_G
MIX_WIDTH = ATTN_WIDTH + SSM_WIDTH + FNET_WIDTH
IN_WIDTH = 2 * QK_WIDTH + ATTN_WIDTH + SSM_WIDTH + FNET_WIDTH
D_FF = 4 * D_MODEL
Q_BLOCK = 128
ROPE_BASE = 10000.0
EPS = 1e-6

kernel_name = 'hybrid_diffattn_s5_fnet_dit_block'


def rms_norm(x, g):
    xf = x.astype(jnp.float32)
    y = xf * lax.rsqrt(jnp.mean(xf * xf, axis=-1, keepdims=True) + EPS)
    return (y * g.astype(jnp.float32)).astype(x.dtype)


def axial_rope_tables(n_tokens):
    rows = n_tokens // GRID_W
    t = jnp.arange(rows * GRID_W)
    row = (t // GRID_W).astype(jnp.float32)
    col = (t % GRID_W).astype(jnp.float32)
    n_freq = HEAD_DIM // 4
    inv = jnp.power(ROPE_BASE, -jnp.arange(n_freq, dtype=jnp.float32) / n_freq)
    ang = jnp.concatenate([row[:, None] * inv, col[:, None] * inv], axis=-1)
    return jnp.cos(ang), jnp.sin(ang)


def apply_rope(x, cos, sin):
    half = x.shape[-1] // 2
    x1 = x[..., :half].astype(jnp.float32)
    x2 = x[..., half:].astype(jnp.float32)
    cos = cos[None, :, None, None, :]
    sin = sin[None, :, None, None, :]
    return jnp.concatenate([x1 * cos - x2 * sin, x2 * cos + x1 * sin], axis=-1).astype(x.dtype)


def diff_attention(q, k, v, lam):
    b, lq, h, _, dh = q.shape
    nb = lq // Q_BLOCK
    qb = jnp.moveaxis(q.reshape(b, nb, Q_BLOCK, h, 2, dh), 1, 0)
    scale = dh ** -0.5

    def one_block(qblk):
        s = jnp.einsum('bqhmd,bkhmd->bmhqk', qblk, k).astype(jnp.float32) * scale
        p = jax.nn.softmax(s, axis=-1)
        w = p[:, 0] - lam * p[:, 1]
        return jnp.einsum('bhqk,bkhd->bqhd', w.astype(v.dtype), v)

    out = lax.map(one_block, qb)
    return jnp.moveaxis(out, 0, 1).reshape(b, lq, h, v.shape[-1])


def zoh_discretise(a_re, a_im, log_dt, b_re, b_im):
    f32 = jnp.float32
    a_re = a_re.astype(f32)
    a_im = a_im.astype(f32)
    dt = jnp.exp(log_dt.astype(f32))[:, None]
    mag = jnp.exp(dt * a_re)
    ab_re = mag * jnp.cos(dt * a_im)
    ab_im = mag * jnp.sin(dt * a_im)
    den = a_re * a_re + a_im * a_im
    n_re = ab_re - 1.0
    f_re = (n_re * a_re + ab_im * a_im) / den
    f_im = (ab_im * a_re - n_re * a_im) / den
    b_re = b_re.astype(f32)
    b_im = b_im.astype(f32)
    bb_re = f_re[..., None] * b_re - f_im[..., None] * b_im
    bb_im = f_re[..., None] * b_im + f_im[..., None] * b_re
    return ab_re, ab_im, bb_re, bb_im


def _affine_combine(e1, e2):
    a1r, a1i, b1r, b1i = e1
    a2r, a2i, b2r, b2i = e2
    return (a2r * a1r - a2i * a1i,
            a2r * a1i + a2i * a1r,
            a2r * b1r - a2i * b1i + b2r,
            a2r * b1i + a2i * b1r + b2i)


def s5_scan(u, disc, s0, reverse):
    ab_re, ab_im, bb_re, bb_im = disc
    b, l, _ = u.shape
    uf = u.astype(jnp.float32).reshape(b, l, SSM_G, SSM_P)
    if reverse:
        uf = uf[:, ::-1]
    bu_re = jnp.einsum('blgp,gnp->blgn', uf, bb_re)
    bu_im = jnp.einsum('blgp,gnp->blgn', uf, bb_im)
    if s0 is not None:
        s0_re, s0_im = s0
        bu_re = bu_re.at[:, 0].add(ab_re * s0_re - ab_im * s0_im)
        bu_im = bu_im.at[:, 0].add(ab_re * s0_im + ab_im * s0_re)
    a_re = jnp.broadcast_to(ab_re, bu_re.shape)
    a_im = jnp.broadcast_to(ab_im, bu_im.shape)
    _, _, s_re, s_im = lax.associative_scan(_affine_combine, (a_re, a_im, bu_re, bu_im), axis=1)
    return s_re, s_im


def s5_readout(states, c_re, c_im, reverse):
    s_re, s_im = states
    y = (jnp.einsum('blgn,gpn->blgp', s_re, c_re.astype(jnp.float32))
         - jnp.einsum('blgn,gpn->blgp', s_im, c_im.astype(jnp.float32)))
    if reverse:
        y = y[:, ::-1]
    b, l = y.shape[:2]
    return y.reshape(b, l, SSM_WIDTH)


def s5_glu(y, p):
    h = jax.nn.gelu(y)
    return h * jax.nn.sigmoid(h @ p['w_glu'] + p['b_glu'])


def s5_mixer(u_x, u_c, p, ctx_out):
    d_skip = p['ssm_d'].astype(jnp.float32)
    y_x = d_skip * u_x.astype(jnp.float32)
    y_c = d_skip * u_c.astype(jnp.float32) if ctx_out else None
    for direction in range(2):
        rev = direction == 1
        disc = zoh_discretise(p['ssm_a_re'][direction], p['ssm_a_im'][direction],
                              p['ssm_log_dt'][direction], p['ssm_b_re'][direction],
                              p['ssm_b_im'][direction])
        st_c = s5_scan(u_c, disc, None, rev)
        st_x = s5_scan(u_x, disc, (st_c[0][:, -1], st_c[1][:, -1]), rev)
        y_x = y_x + s5_readout(st_x, p['ssm_c_re'][direction], p['ssm_c_im'][direction], rev)
        if ctx_out:
            y_c = y_c + s5_readout(st_c, p['ssm_c_re'][direction], p['ssm_c_im'][direction], rev)
    out_x = s5_glu(y_x.astype(u_x.dtype), p)
    out_c = s5_glu(y_c.astype(u_c.dtype), p) if ctx_out else None
    return out_x, out_c


def fourier_mix(h, w_fnet):
    b, l, _ = h.shape
    hf = h.astype(jnp.float32).reshape(b, l, FNET_G, FNET_C)
    z = jnp.fft.fftn(hf, axes=(1, 3), norm='ortho').real.astype(h.dtype)
    return jnp.einsum('blgc,gcd->blgd', z, w_fnet).reshape(b, l, FNET_WIDTH)


def sq_relu_mlp(h, p):
    return jnp.square(jax.nn.relu(h @ p['w_ff1'])) @ p['w_ff2']


def split_proj(proj):
    b, l, _ = proj.shape
    o1 = QK_WIDTH
    o2 = 2 * QK_WIDTH
    o3 = o2 + ATTN_WIDTH
    o4 = o3 + SSM_WIDTH
    q = proj[..., :o1].reshape(b, l, N_HEADS, 2, HEAD_DIM)
    k = proj[..., o1:o2].reshape(b, l, N_HEADS, 2, HEAD_DIM)
    v = proj[..., o2:o3].reshape(b, l, N_HEADS, V_DIM)
    return q, k, v, proj[..., o3:o4], proj[..., o4:]


def trunk_layer(x, ctx, c_act, cctx_act, rope, p, layer_idx, ctx_out):
    cos, sin = rope
    b, l, _ = x.shape
    bc, lc, _ = ctx.shape
    mod_x = (c_act @ p['w_mod'] + p['b_mod'])[:, None, :]
    mod_c = (cctx_act @ p['w_mod'] + p['b_mod'])[None, None, :]
    sh1x, sc1x, g1x, sh2x, sc2x, g2x = jnp.split(mod_x, 6, axis=-1)
    sh1c, sc1c, g1c, sh2c, sc2c, g2c = jnp.split(mod_c, 6, axis=-1)

    hx = rms_norm(x, p['g_norm1']) * (1.0 + sc1x) + sh1x
    hc = rms_norm(ctx, p['g_norm1']) * (1.0 + sc1c) + sh1c
    qx, kx, vx, ux, fx = split_proj(hx @ p['w_in'])
    qc, kc, vc, uc, fc = split_proj(hc @ p['w_in'])

    f32 = jnp.float32
    lam_init = 0.8 - 0.6 * math.exp(-0.3 * layer_idx)
    lam = (jnp.exp(jnp.sum(p['lam_q1'].astype(f32) * p['lam_k1'].astype(f32)))
           - jnp.exp(jnp.sum(p['lam_q2'].astype(f32) * p['lam_k2'].astype(f32))) + lam_init)
    qx = apply_rope(rms_norm(qx, p['g_qnorm']), cos, sin)
    kx = apply_rope(rms_norm(kx, p['g_knorm']), cos, sin)
    kc = rms_norm(kc, p['g_knorm'])
    k_all = jnp.concatenate([kc, kx], axis=1)
    v_all = jnp.concatenate([vc, vx], axis=1)
    ax = rms_norm(diff_attention(qx, k_all, v_all, lam), p['g_subln']) * (1.0 - lam_init)

    sx, sc = s5_mixer(ux, uc, p, ctx_out)

    mix_x = jnp.concatenate([ax.reshape(b, l, ATTN_WIDTH), sx, fourier_mix(fx, p['w_fnet'])], axis=-1)
    x = x + g1x * (mix_x @ p['w_out'])

    x = x + g2x * sq_relu_mlp(rms_norm(x, p['g_norm2']) * (1.0 + sc2x) + sh2x, p)

    if ctx_out:
        qc = rms_norm(qc, p['g_qnorm'])
        ac = rms_norm(diff_attention(qc, kc, vc, lam), p['g_subln']) * (1.0 - lam_init)
        mix_c = jnp.concatenate([ac.reshape(bc, lc, ATTN_WIDTH), sc, fourier_mix(fc, p['w_fnet'])], axis=-1)
        ctx = ctx + g1c * (mix_c @ p['w_out'])
        ctx = ctx + g2c * sq_relu_mlp(rms_norm(ctx, p['g_norm2']) * (1.0 + sc2c) + sh2c, p)
    return x, ctx


def setup_inputs(seed: int = 0) -> dict:
    key = jax.random.key(seed)
    ks = jax.random.split(key, 32)
    f32 = jnp.float32

    def nrm(k, shape, s):
        return jax.random.normal(k, shape, f32) * s

    n_idx = jnp.arange(SSM_N, dtype=f32)
    return {
        'x': nrm(ks[0], (BATCH, SEQ, D_MODEL), 1.0),
        'c': nrm(ks[1], (BATCH, D_MODEL), 1.0),
        'ctx': nrm(ks[2], (BATCH, CTX_LEN, D_MODEL), 1.0),
        'c_ctx': nrm(ks[3], (D_MODEL,), 1.0),
        'w_mod': nrm(ks[4], (DEPTH, D_MODEL, 6 * D_MODEL), 0.5 * D_MODEL ** -0.5),
        'b_mod': nrm(ks[5], (DEPTH, 6 * D_MODEL), 0.02),
        'g_norm1': 1.0 + nrm(ks[6], (DEPTH, D_MODEL), 0.02),
        'w_in': nrm(ks[7], (DEPTH, D_MODEL, IN_WIDTH), D_MODEL ** -0.5),
        'g_qnorm': 1.0 + nrm(ks[8], (DEPTH, HEAD_DIM), 0.02),
        'g_knorm': 1.0 + nrm(ks[9], (DEPTH, HEAD_DIM), 0.02),
        'lam_q1': nrm(ks[10], (DEPTH, HEAD_DIM), 0.1),
        'lam_k1': nrm(ks[11], (DEPTH, HEAD_DIM), 0.1),
        'lam_q2': nrm(ks[12], (DEPTH, HEAD_DIM), 0.1),
        'lam_k2': nrm(ks[13], (DEPTH, HEAD_DIM), 0.1),
        'g_subln': 1.0 + nrm(ks[14], (DEPTH, V_DIM), 0.02),
        'ssm_a_re': -0.5 + nrm(ks[15], (DEPTH, 2, SSM_G, SSM_N), 0.01),
        'ssm_a_im': jnp.pi * n_idx + nrm(ks[16], (DEPTH, 2, SSM_G, SSM_N), 0.01),
        'ssm_log_dt': jax.random.uniform(ks[17], (DEPTH, 2, SSM_G), f32,
                                         minval=math.log(1e-3), maxval=math.log(1e-1)),
        'ssm_b_re': nrm(ks[18], (DEPTH, 2, SSM_G, SSM_N, SSM_P), (2 * SSM_P) ** -0.5),
        'ssm_b_im': nrm(ks[19], (DEPTH, 2, SSM_G, SSM_N, SSM_P), (2 * SSM_P) ** -0.5),
        'ssm_c_re': nrm(ks[20], (DEPTH, 2, SSM_G, SSM_P, SSM_N), 0.5),
        'ssm_c_im': nrm(ks[21], (DEPTH, 2, SSM_G, SSM_P, SSM_N), 0.5),
        'ssm_d': nrm(ks[22], (DEPTH, SSM_WIDTH), 0.5),
        'w_glu': nrm(ks[23], (DEPTH, SSM_WIDTH, SSM_WIDTH), SSM_WIDTH ** -0.5),
        'b_glu': nrm(ks[24], (DEPTH, SSM_WIDTH), 0.02),
        'w_fnet': nrm(ks[25], (DEPTH, FNET_G, FNET_C, FNET_C), FNET_C ** -0.5),
        'w_out': nrm(ks[26], (DEPTH, MIX_WIDTH, D_MODEL), MIX_WIDTH ** -0.5),
        'g_norm2': 1.0 + nrm(ks[27], (DEPTH, D_MODEL), 0.02),
        'w_ff1': nrm(ks[28], (DEPTH, D_MODEL, D_FF), D_MODEL ** -0.5),
        'w_ff2': nrm(ks[29], (DEPTH, D_FF, D_MODEL), D_FF ** -0.5),
    }


def reference(x, c, ctx, c_ctx, w_mod, b_mod, g_norm1, w_in, g_qnorm, g_knorm,
              lam_q1, lam_k1, lam_q2, lam_k2, g_subln, ssm_a_re, ssm_a_im, ssm_log_dt,
              ssm_b_re, ssm_b_im, ssm_c_re, ssm_c_im, ssm_d, w_glu, b_glu, w_fnet,
              w_out, g_norm2, w_ff1, w_ff2):
    rope = axial_rope_tables(x.shape[1])
    c_act = jax.nn.silu(c)
    cctx_act = jax.nn.silu(c_ctx)
    for i in range(DEPTH):
        p = {
            'w_mod': w_mod[i], 'b_mod': b_mod[i], 'g_norm1': g_norm1[i], 'w_in': w_in[i],
            'g_qnorm': g_qnorm[i], 'g_knorm': g_knorm[i],
            'lam_q1': lam_q1[i], 'lam_k1': lam_k1[i], 'lam_q2': lam_q2[i], 'lam_k2': lam_k2[i],
            'g_subln': g_subln[i],
            'ssm_a_re': ssm_a_re[i], 'ssm_a_im': ssm_a_im[i], 'ssm_log_dt': ssm_log_dt[i],
            'ssm_b_re': ssm_b_re[i], 'ssm_b_im': ssm_b_im[i],
            'ssm_c_re': ssm_c_re[i], 'ssm_c_im': ssm_c_im[i], 'ssm_d': ssm_d[i],
            'w_glu': w_glu[i], 'b_glu': b_glu[i], 'w_fnet': w_fnet[i], 'w_out': w_out[i],
            'g_norm2': g_norm2[i], 'w_ff1': w_ff1[i], 'w_ff2': w_ff2[i],
        }
        x, ctx = trunk_layer(x, ctx, c_act, cctx_act, rope, p, i, i < DEPTH - 1)
    return x
```

```python
import math
import contextlib
import numpy as np
import ml_dtypes
import concourse.bass as bass
import concourse.mybir as mybir
from concourse.bass_utils import run_bass_kernel_spmd

F32 = mybir.dt.float32
BF16 = mybir.dt.bfloat16
I32 = mybir.dt.int32
AF = mybir.ActivationFunctionType
ALU = mybir.AluOpType
NPBF = ml_dtypes.bfloat16


class Buf:
    __slots__ = ("name", "w", "r", "psum")

    def __init__(self, name, psum=False):
        self.name = name
        self.w = None
        self.r = {}
        self.psum = psum


class V:
    __slots__ = ("ap", "bufs")

    def __init__(self, ap, bufs):
        self.ap = ap
        self.bufs = bufs


class T:
    def __init__(self, handle, name, shape, subaxis=None, psum=False):
        self.h = handle
        self.name = name
        self.shape = list(shape)
        self.subaxis = subaxis
        n = shape[subaxis] if subaxis is not None else 1
        self.bufs = [Buf(f"{name}.{i}", psum) for i in range(n)]

    def __getitem__(self, idx):
        if not isinstance(idx, tuple):
            idx = (idx,)
        ap = self.h[idx]
        bufs = self.bufs
        if self.subaxis is not None and len(idx) > self.subaxis:
            ix = idx[self.subaxis]
            if isinstance(ix, int):
                bufs = [self.bufs[ix]]
            elif isinstance(ix, slice):
                rng = range(*ix.indices(self.shape[self.subaxis]))
                bufs = [self.bufs[i] for i in rng]
        return V(ap, bufs)

    def view(self, ap, sub=None):
        bufs = self.bufs if sub is None else [self.bufs[i] for i in sub]
        return V(ap, bufs)


class Sched:
    ENGS = ("pe", "act", "dve", "pool", "sp")

    def __init__(self, n_dma_sems=8):
        self.prog = {e: [] for e in self.ENGS}
        self.cnt = {}
        self.seen = {e: {} for e in self.ENGS}
        self.n_dma_sems = n_dma_sems
        self.dma_rr = {e: 0 for e in self.ENGS}
        self.ninst = 0

    def semkeys(self):
        keys = list(self.ENGS)
        for e in ("sp", "act", "pool"):
            for i in range(self.n_dma_sems):
                keys.append(f"d_{e}{i}")
        return keys

    def _wait(self, e, dep):
        k, n = dep
        if self.seen[e].get(k, 0) >= n:
            return
        self.seen[e][k] = n
        self.prog[e].append(("wait", k, n))

    def _deps(self, e, reads, writes):
        deps = []
        for b in reads:
            if b.w is not None:
                deps.append(b.w)
            if b.psum:
                for k, n in b.r.items():
                    if k != e:
                        deps.append((k, n))
        for b in writes:
            if b.w is not None:
                deps.append(b.w)
            for k, n in b.r.items():
                deps.append((k, n))
        for d in deps:
            if d[0] == "pe" and e == "pe":
                continue
            self._wait(e, d)

    def _mark(self, key, n, reads, writes):
        for b in reads:
            if b.r.get(key, 0) < n:
                b.r[key] = n
        for b in writes:
            b.w = (key, n)
            b.r = {}

    def op(self, e, fn, reads=(), writes=()):
        self._deps(e, reads, writes)
        n = self.cnt.get(e, 0) + 1
        self.cnt[e] = n
        self.prog[e].append(("op", fn, e, 1))
        self._mark(e, n, reads, writes)
        self.ninst += 1

    def dma(self, e, fn, reads=(), writes=()):
        i = self.dma_rr[e]
        self.dma_rr[e] = (i + 1) % self.n_dma_sems
        key = f"d_{e}{i}"
        uses = self.cnt.get(key, 0)
        if uses > 0:
            self._wait(e, (key, uses))
        self._deps(e, reads, writes)
        n = uses + 16
        self.cnt[key] = n
        self.prog[e].append(("op", fn, key, 16))
        self._mark(key, n, reads, writes)
        self.ninst += 1

    def wait_all(self, e):
        for k, n in list(self.cnt.items()):
            self._wait(e, (k, n))

    def barrier(self):
        for e in self.ENGS:
            self.wait_all(e)

    def emit(self, block, sems):
        engobj = {"pe": "tensor", "act": "scalar", "dve": "vector", "pool": "gpsimd", "sp": "sync"}

        def mk(e):
            def body(eng):
                for item in self.prog[e]:
                    if item[0] == "wait":
                        eng.wait_ge(sems[item[1]], item[2])
                    else:
                        _, fn, key, inc = item
                        fn(eng).then_inc(sems[key], inc)
            return body

        for e in self.ENGS:
            getattr(block, engobj[e])(mk(e))


class KB:
    def __init__(self):
        self.nc = bass.Bass("TRN2", target_bir_lowering=False)
        self.S = Sched()
        self.st = contextlib.ExitStack()
        self.dq = 0
        self.sb_off = (self.nc.sbuf_base + 63) // 64 * 64
        self.sb_top = self.nc.sbuf_top
        self.nalloc = 0

    @contextlib.contextmanager
    def scope(self):
        mark = self.sb_off
        yield
        self.S.barrier()
        self.sb_off = mark

    def dram(self, name, shape, dt, kind, subaxis=None):
        h = self.nc.dram_tensor(name, list(shape), dt, kind=kind).ap()
        return T(h, name, shape, subaxis)

    def sb(self, name, shape, dt, subaxis=None):
        nbytes = int(np.prod(shape[1:])) * mybir.dt.size(dt)
        nbytes = (nbytes + 63) // 64 * 64
        assert self.sb_off + nbytes <= self.sb_top, f"SBUF overflow allocating {name}: {self.sb_off}+{nbytes}>{self.sb_top}"
        self.nalloc += 1
        h = self.nc.alloc_sbuf_tensor_at(f"{name}_{self.nalloc}", list(shape), dt, offset=self.sb_off)
        self.sb_off += nbytes
        return T(h, name, shape, subaxis)

    def ps(self, name, shape=None, dt=F32):
        shape = [128, 512] if dt == F32 else [128, 1024]
        h = self.st.enter_context(self.nc.psum_tensor(name, list(shape), dt))
        return T(h, name, shape, None, psum=True)

    @staticmethod
    def _rb(*vs):
        out = []
        for v in vs:
            if isinstance(v, V):
                out.extend(v.bufs)
        return out

    @staticmethod
    def _a(v):
        return v.ap if isinstance(v, V) else v

    def dma(self, out, in_, q=None):
        if q is None:
            q = "sp"
            self.dq += 1
        self.S.dma(q, lambda e: e.dma_start(out=out.ap, in_=in_.ap), reads=in_.bufs, writes=out.bufs)

    def mm(self, out, lhsT, rhs, start, stop):
        self.S.op("pe", lambda e: e.matmul(out.ap, lhsT=lhsT.ap, rhs=rhs.ap, start=start, stop=stop),
                  reads=self._rb(lhsT, rhs), writes=out.bufs)

    def act(self, out, in_, func, bias=None, scale=None, eng="act"):
        kw = {}
        if bias is not None:
            kw["bias"] = self._a(bias)
        if scale is not None:
            kw["scale"] = self._a(scale)
        self.S.op(eng, lambda e: e.activation(out=out.ap, in_=in_.ap, func=func, **kw),
                  reads=self._rb(in_, bias, scale), writes=out.bufs)

    def copy(self, eng, out, in_):
        if eng == "act":
            self.S.op(eng, lambda e: e.copy(out=out.ap, in_=in_.ap), reads=in_.bufs, writes=out.bufs)
        else:
            self.S.op(eng, lambda e: e.tensor_copy(out=out.ap, in_=in_.ap), reads=in_.bufs, writes=out.bufs)

    def tt(self, eng, out, in0, in1, op):
        self.S.op(eng, lambda e: e.tensor_tensor(out=out.ap, in0=in0.ap, in1=in1.ap, op=op),
                  reads=self._rb(in0, in1), writes=out.bufs)

    def ts(self, eng, out, in0, s1, op0, s2=None, op1=None):
        if op1 is None:
            self.S.op(eng, lambda e: e.tensor_single_scalar(out=out.ap, in_=in0.ap, scalar=self._a(s1), op=op0),
                      reads=self._rb(in0, s1), writes=out.bufs)
        else:
            self.S.op(eng, lambda e: e.tensor_scalar(out=out.ap, in0=in0.ap, scalar1=self._a(s1), scalar2=self._a(s2),
                                                     op0=op0, op1=op1),
                      reads=self._rb(in0, s1, s2), writes=out.bufs)

    def stt(self, eng, out, in0, scalar, in1, op0, op1):
        self.S.op(eng, lambda e: e.scalar_tensor_tensor(out=out.ap, in0=in0.ap, scalar=self._a(scalar), in1=in1.ap,
                                                        op0=op0, op1=op1),
                  reads=self._rb(in0, scalar, in1), writes=out.bufs)

    def recip(self, out, in_):
        self.S.op("dve", lambda e: e.reciprocal(out=out.ap, in_=in_.ap), reads=in_.bufs, writes=out.bufs)

    def memset(self, eng, out, val):
        self.S.op(eng, lambda e: e.memset(out.ap, val), writes=out.bufs)

    def scan(self, out, d0, d1, init):
        self.S.op("dve", lambda e: e.tensor_tensor_scan(out=out.ap, data0=d0.ap, data1=d1.ap, initial=self._a(init),
                                                        op0=ALU.mult, op1=ALU.add),
                  reads=self._rb(d0, d1, init), writes=out.bufs)

    def finish(self):
        self.S.wait_all("sp")
        sems = {k: self.st.enter_context(self.nc.semaphore(k)) for k in self.S.semkeys()}
        block = self.st.enter_context(self.nc.Block())
        self.S.emit(block, sems)
        self.st.close()
        return self.nc


def rev(v):
    return V(v.ap[:, ::-1], v.bufs)


TOK = 2304
BLOCKS = [(0, 512, 0), (512, 512, 0), (1024, 512, 0), (1536, 512, 0), (2048, 256, 1)]
EPS = 1e-6


def build_M():
    k = KB()
    cT = k.dram("cT", [128, 8, 2], F32, "ExternalInput")
    w_mod = k.dram("w_mod", [1024, 6144], F32, "ExternalInput")
    bmodT = k.dram("bmodT", [128, 48], F32, "ExternalInput")
    gn = k.dram("gn", [128, 16], F32, "ExternalInput")
    modT = k.dram("modT", [128, 48, 2], F32, "ExternalOutput")

    c_sb = k.sb("c_sb", [128, 8, 2], F32)
    cact = k.sb("cact", [128, 8, 2], F32)
    bm_sb = k.sb("bm_sb", [128, 48], F32)
    gn_sb = k.sb("gn_sb", [128, 16], F32)
    mod_sb = k.sb("mod_sb", [128, 48, 2], F32)
    wch = [k.sb(f"wch{i}", [128, 8, 512], F32) for i in range(3)]
    pss = [k.ps(f"psm{i}", [128, 2]) for i in range(4)]

    k.dma(c_sb[:, :, :], cT[:, :, :], q="sp")
    k.dma(bm_sb[:, :], bmodT[:, :], q="sp")
    k.dma(gn_sb[:, :], gn[:, :], q="sp")
    k.act(cact[:, :, :], c_sb[:, :, :], AF.Silu)
    wv = w_mod.h.rearrange("(k p) c -> p k c", p=128)
    for c in range(12):
        w = wch[c % 3]
        for kk in range(8):
            k.dma(w[:, kk, :], w_mod.view(wv[:, kk, c * 512:(c + 1) * 512]), q=("sp", "act")[kk % 2])
        for f in range(4):
            ft = c * 4 + f
            ps = pss[ft % 4]
            for kk in range(8):
                k.mm(ps[:, 0:2], w[:, kk, f * 128:(f + 1) * 128], cact[:, kk, :], kk == 0, kk == 7)
            k.ts("dve", mod_sb[:, ft, :], ps[:, 0:2], bm_sb[:, ft:ft + 1], ALU.add)
    for s, g0 in ((1, 0), (4, 8)):
        for n in range(2):
            k.stt("dve", mod_sb[:, s * 8:(s + 1) * 8, n], mod_sb[:, s * 8:(s + 1) * 8, n], 1.0,
                  gn_sb[:, g0:g0 + 8], ALU.add, ALU.mult)
    k.dma(modT[:, :, :], mod_sb[:, :, :], q="sp")
    return k.finish()


def rmsnorm_mod(k, x_sb, W, n, mod_sb, s_sh, s_gm, h_bf, xsq, ps_ss, rstd, tmp, ones_bf, cst):
    for kk in range(8):
        k.act(xsq[:, kk, 0:W], x_sb[:, kk, 0:W], AF.Square)
    for kk in range(8):
        k.mm(ps_ss[:, 0:W], ones_bf, xsq[:, kk, 0:W], kk == 0, kk == 7)
    k.act(rstd[:, 0:W], ps_ss[:, 0:W], AF.Sqrt, bias=cst[:, 0:1], scale=1.0 / 1024)
    k.recip(rstd[:, 0:W], rstd[:, 0:W])
    for kk in range(8):
        t = tmp[kk % 2]
        k.stt("dve", t[:, 0:W], x_sb[:, kk, 0:W], mod_sb[:, s_gm * 8 + kk, n:n + 1], rstd[:, 0:W], ALU.mult, ALU.mult)
        k.act(h_bf[:, kk, 0:W], t[:, 0:W], AF.Identity, bias=mod_sb[:, s_sh * 8 + kk, n:n + 1], scale=1.0)


def build_A(nb=5, upto=9):
    k = KB()
    xT = k.dram("xT", [1024, TOK], F32, "ExternalInput")
    modT = k.dram("modT", [128, 48, 2], F32, "ExternalInput")
    w_in = k.dram("w_in", [1024, 2048], F32, "ExternalInput")
    gqk = k.dram("gqk", [128, 2], F32, "ExternalInput")
    rope = k.dram("rope", [128, 2, TOK], F32, "ExternalInput")
    cmat = k.dram("cmat", [128, 3, 128], BF16, "ExternalInput")
    cs = k.dram("cs", [128, 2, 512], BF16, "ExternalInput")
    dcol = k.dram("dcol", [128, 2], F32, "ExternalInput")
    QT = k.dram("QT", [4, 128, TOK], BF16, "ExternalOutput")
    KT = k.dram("KT", [4, 128, TOK], BF16, "ExternalOutput")
    Vd = k.dram("V", [TOK, 512], BF16, "ExternalOutput")
    UT = k.dram("UT", [256, TOK], BF16, "ExternalOutput")
    PQ = k.dram("PQ", [TOK, 512], BF16, "ExternalOutput")
    DUT = k.dram("DUT", [256, TOK], F32, "ExternalOutput")

    w_bf = k.sb("w_bf", [128, 8, 2048], BF16)
    stage = [k.sb(f"stage{i}", [128, 8, 256], F32) for i in range(2)]
    mod_sb = k.sb("mod_sb", [128, 48, 2], F32)
    gqk_sb = k.sb("gqk_sb", [128, 2], F32)
    dcol_sb = k.sb("dcol_sb", [128, 2], F32)
    cmat_sb = k.sb("cmat_sb", [128, 3, 128], BF16)
    cs_sb = k.sb("cs_sb", [128, 2, 512], BF16)
    cst = k.sb("cst", [128, 2], F32)
    x_sb = [k.sb(f"x_sb{i}", [128, 8, 512], F32) for i in range(2)]
    rope_sb = [k.sb(f"rope_sb{i}", [128, 2, 512], F32) for i in range(2)]
    xsq = k.sb("xsq", [128, 8, 512], BF16)
    h_bf = [k.sb(f"h_bf{i}", [128, 8, 512], BF16) for i in range(2)]
    rstd = k.sb("rstd", [128, 512], F32)
    tmp = [k.sb(f"tmp{i}", [128, 512], F32) for i in range(2)]
    qg = [k.sb(f"qg{i}", [128, 512], BF16) for i in range(2)]
    sq = [k.sb(f"sq{i}", [128, 512], BF16) for i in range(2)]
    rs = [k.sb(f"rs{i}", [128, 512], F32) for i in range(2)]
    t1 = [k.sb(f"t1{i}", [128, 512], F32) for i in range(2)]
    t2 = [k.sb(f"t2{i}", [128, 512], F32) for i in range(2)]
    oq = [k.sb(f"oq{i}", [128, 512], BF16) for i in range(3)]
    vtok = [k.sb(f"vtok{i}", [128, 512], BF16) for i in range(2)]
    ftl = k.sb("ftl", [128, 2, 512], BF16)
    pqs = [k.sb(f"pqs{i}", [128, 512], BF16) for i in range(2)]
    ubf = [k.sb(f"ubf{i}", [128, 512], BF16) for i in range(2)]
    duf = [k.sb(f"duf{i}", [128, 512], F32) for i in range(2)]

    ps_ss = k.ps("ps_ss", [128, 512])
    ps_main = [k.ps(f"ps_main{i}", [128, 512]) for i in range(2)]
    ps_ss2 = k.ps("ps_ss2", [128, 512])
    ps_rot = k.ps("ps_rot", [128, 512])
    ps_v = k.ps("ps_v", [128, 512])
    ps_pq = k.ps("ps_pq", [128, 512])

    k.memset("dve", cst[:, 0:1], EPS)
    k.dma(mod_sb[:, :, :], modT[:, :, :], q="sp")
    k.dma(gqk_sb[:, :], gqk[:, :], q="sp")
    k.dma(dcol_sb[:, :], dcol[:, :], q="sp")
    k.dma(cmat_sb[:, :, :], cmat[:, :, :], q="sp")
    k.dma(cs_sb[:, :, :], cs[:, :, :], q="sp")
    wv = w_in.h.rearrange("(k p) c -> p k c", p=128)
    for c in range(8):
        sg = stage[c % 2]
        for kk in range(8):
            k.dma(sg[:, kk, :], w_in.view(wv[:, kk, c * 256:(c + 1) * 256]), q=("sp", "act")[kk % 2])
        k.copy(("dve", "act")[c % 2], w_bf[:, :, c * 256:(c + 1) * 256], sg[:, :, :])
    ones_bf = cmat_sb[:, 0, :]
    blk64 = cmat_sb[:, 1, :]
    RTm = cmat_sb[:, 2, :]
    xv = xT.h.rearrange("(k p) t -> p k t", p=128)
    mmi = 0
    qi = 0
    for bi, (c0, W, n) in enumerate(BLOCKS[:nb]):
        xs = x_sb[bi % 2]
        rp = rope_sb[bi % 2]
        hb = h_bf[bi % 2]
        for kk in range(8):
            k.dma(xs[:, kk, 0:W], xT.view(xv[:, kk, c0:c0 + W]), q="sp")
        k.dma(rp[:, :, 0:W], rope[:, :, c0:c0 + W], q="sp")
        if upto < 1: continue
        rmsnorm_mod(k, xs, W, n, mod_sb, 0, 1, hb, xsq, ps_ss, rstd, tmp, ones_bf, cst)
        if upto < 2: continue
        for idx in range(8):
            isq = idx < 4
            col = idx * 128
            pm = ps_main[mmi % 2]; mmi += 1
            for kk in range(8):
                k.mm(pm[:, 0:W], w_bf[:, kk, col:col + 128], hb[:, kk, 0:W], kk == 0, kk == 7)
            a = qi % 2; qi += 1
            gcol = gqk_sb[:, 0:1] if isq else gqk_sb[:, 1:2]
            k.act(qg[a][:, 0:W], pm[:, 0:W], AF.Identity, scale=gcol)
            k.act(sq[a][:, 0:W], pm[:, 0:W], AF.Square)
            k.mm(ps_ss2[:, 0:W], blk64, sq[a][:, 0:W], True, True)
            k.mm(ps_rot[:, 0:W], RTm, qg[a][:, 0:W], True, True)
            k.act(rs[a][:, 0:W], ps_ss2[:, 0:W], AF.Sqrt, bias=cst[:, 0:1], scale=1.0 / 64)
            k.recip(rs[a][:, 0:W], rs[a][:, 0:W])
            k.tt("dve", t1[a][:, 0:W], qg[a][:, 0:W], rp[:, 0, 0:W], ALU.mult)
            k.tt("dve", t2[a][:, 0:W], ps_rot[:, 0:W], rp[:, 1, 0:W], ALU.mult)
            k.tt("dve", t1[a][:, 0:W], t1[a][:, 0:W], t2[a][:, 0:W], ALU.add)
            o = oq[idx % 3]
            k.stt("dve", o[:, 0:W], t1[a][:, 0:W], 0.125 if isq else 1.0, rs[a][:, 0:W], ALU.mult, ALU.mult)
            dst = QT if isq else KT
            k.dma(dst[idx % 4, :, c0:c0 + W], o[:, 0:W], q="sp")
        if upto < 3.0: continue
        for m in range(2):
            pm = ps_main[mmi % 2]; mmi += 1
            col = 1792 + m * 128
            for kk in range(8):
                k.mm(pm[:, 0:W], w_bf[:, kk, col:col + 128], hb[:, kk, 0:W], kk == 0, kk == 7)
            k.act(ftl[:, m, 0:W], pm[:, 0:W], AF.Identity)
        if upto < 3.2: continue
        for m in range(2):
            pm = ps_main[mmi % 2]; mmi += 1
            col = 1536 + m * 128
            for kk in range(8):
                k.mm(pm[:, 0:W], w_bf[:, kk, col:col + 128], hb[:, kk, 0:W], kk == 0, kk == 7)
            k.act(ubf[m][:, 0:W], pm[:, 0:W], AF.Identity)
            if upto < 3.4: continue
            k.ts("dve", duf[m][:, 0:W], pm[:, 0:W], dcol_sb[:, m:m + 1], ALU.mult)
            if upto < 3.6: continue
            k.dma(UT[m * 128:(m + 1) * 128, c0:c0 + W], ubf[m][:, 0:W], q="sp")
            k.dma(DUT[m * 128:(m + 1) * 128, c0:c0 + W], duf[m][:, 0:W], q="sp")
        if upto < 4: continue
        for tt in range(W // 128):
            tsl = slice(tt * 128, (tt + 1) * 128)
            for kk in range(8):
                k.mm(ps_v[:, :], hb[:, kk, tsl], w_bf[:, kk, 1024:1536], kk == 0, kk == 7)
            vt = vtok[tt % 2]
            k.act(vt[:, :], ps_v[:, :], AF.Identity)
            k.dma(Vd[c0 + tt * 128:c0 + (tt + 1) * 128, :], vt[:, :], q="sp")
            for m in range(2):
                k.mm(ps_pq[:, :], ftl[:, m, tsl], cs_sb[:, m, :], m == 0, m == 1)
            pt = pqs[tt % 2]
            k.copy("dve", pt[:, :], ps_pq[:, :])
            k.dma(PQ[c0 + tt * 128:c0 + (tt + 1) * 128, :], pt[:, :], q="sp")
    return k.finish()


KEYT = [(s, t) for s in range(4) for t in range(16)] + [(0, 16), (0, 17)]
NY = 8448


def build_B(do_attn=True, do_s5=True, do_fft=True, nheads=4, nqb=5):
    k = KB()
    QT = k.dram("QT", [4, 128, TOK], BF16, "ExternalInput")
    KTall = k.dram("KTall", [4, 4, 128, TOK], BF16, "ExternalInput")
    Vall = k.dram("Vall", [4, TOK, 512], BF16, "ExternalInput")
    lamv = k.dram("lamv", [64, 4], F32, "ExternalInput")
    lconst = k.dram("lconst", [128, 2], F32, "ExternalInput")
    gsub = k.dram("gsub", [128, 1], F32, "ExternalInput")
    cmat = k.dram("cmat", [128, 3, 128], BF16, "ExternalInput")
    mixA = k.dram("mixA", [512, TOK], BF16, "ExternalOutput")
    UTall = k.dram("UTall", [4, 64, TOK], BF16, "ExternalInput")
    s5p = k.dram("s5p", [128, 4, 3], F32, "ExternalInput")
    BTd = k.dram("BT", [64, 4, 2, 128], F32, "ExternalInput")
    CTd = k.dram("CT", [128, 4, 2, 64], F32, "ExternalInput")
    tauAB = k.dram("tauAB", [128, 2, 512], F32, "ExternalInput")
    Yd = k.dram("Y", [64, NY], F32, "ExternalOutput")
    PQall = k.dram("PQall", [4, TOK, 512], BF16, "ExternalInput")
    krow = k.dram("krow", [128, TOK], F32, "ExternalInput")
    lcol = k.dram("lcol", [128, 66], F32, "ExternalInput")
    wfbd = k.dram("wfbd", [128, 2, 128], F32, "ExternalInput")
    mixF = k.dram("mixF", [256, TOK], BF16, "ExternalOutput")

    cmat_sb = k.sb("cmat_sb", [128, 3, 128], BF16)
    cst = k.sb("cst", [128, 4], F32)
    k.dma(cmat_sb[:, :, :], cmat[:, :, :])
    k.memset("dve", cst[:, 0:1], EPS)
    k.memset("dve", cst[:, 1:2], 1.0)
    ones_bf = cmat_sb[:, 0, :]
    pss = [k.ps(f"ps{i}") for i in range(8)]

    if do_attn:
      with k.scope():
          lam_sb = k.sb("lam_sb", [64, 4], F32)
          lprod = k.sb("lprod", [64, 2], F32)
          lc_sb = k.sb("lc_sb", [128, 2], F32)
          gs_sb = k.sb("gs_sb", [128, 1], F32)
          ones_f = k.sb("ones_f", [64, 128], F32)
          lam_e = k.sb("lam_e", [128, 2], F32)
          neglam = k.sb("neglam", [128, 1], F32)
          gfin = k.sb("gfin", [128, 1], F32)
          k.dma(lam_sb[:, :], lamv[:, :])
          k.dma(lc_sb[:, :], lconst[:, :])
          k.dma(gs_sb[:, :], gsub[:, :])
          k.memset("dve", ones_f[:, :], 1.0)
          k.tt("dve", lprod[:, 0:1], lam_sb[:, 0:1], lam_sb[:, 1:2], ALU.mult)
          k.tt("dve", lprod[:, 1:2], lam_sb[:, 2:3], lam_sb[:, 3:4], ALU.mult)
          k.mm(pss[0][:, 0:2], ones_f[:, :], lprod[:, :], True, True)
          k.act(lam_e[:, :], pss[0][:, 0:2], AF.Exp)
          k.tt("dve", neglam[:, :], lam_e[:, 1:2], lam_e[:, 0:1], ALU.subtract)
          k.tt("dve", neglam[:, :], neglam[:, :], lc_sb[:, 0:1], ALU.subtract)
          k.tt("dve", gfin[:, :], gs_sb[:, :], lc_sb[:, 1:2], ALU.mult)

          KT_sb = [k.sb(f"KT_sb{i}", [128, 4, TOK], BF16) for i in range(2)]
          V_sb = [k.sb(f"V_sb{i}", [128, 4, 18, 128], BF16) for i in range(2)]
          QT_sb = [k.sb(f"QT_sb{i}", [128, TOK], BF16) for i in range(2)]
          PT = [[k.sb(f"PT{m}{i}", [128, 512], BF16) for i in range(3)] for m in range(2)]
          rr = [k.sb(f"rr{m}", [128, 512], F32) for m in range(2)]
          o0 = k.sb("o0", [128, 512], F32)
          o1 = k.sb("o1", [128, 512], F32)
          osq = k.sb("osq", [128, 512], BF16)
          ors = k.sb("ors", [128, 512], F32)
          oout = [k.sb(f"oout{i}", [128, 512], BF16) for i in range(2)]
          ps_s = [[pss[0], pss[1]], [pss[2], pss[3]]]
          ps_o = [pss[4], pss[5]]
          ps_r = [pss[6], pss[7]]
          it = 0
          for h in range(nheads):
              kt = KT_sb[h % 2]; vs = V_sb[h % 2]; qs = QT_sb[h % 2]
              for s in range(4):
                  k.dma(kt[:, s, :], KTall[s, h, :, :], q="sp")
                  vv = Vall.h[s].rearrange("(t p) c -> p t c", p=128)
                  for half in range(2):
                      k.dma(vs[:, s, half * 9:(half + 1) * 9, :],
                            Vall.view(vv[:, half * 9:(half + 1) * 9, h * 128:(h + 1) * 128]), q="sp")
              k.dma(qs[:, :], QT[h, :, :], q="sp")
              for qb, (c0, W, n) in enumerate(BLOCKS[:nqb]):
                  keys = KEYT if n == 0 else KEYT[64:]
                  for ki, (s, t) in enumerate(keys):
                      first = ki == 0
                      last = ki == len(keys) - 1
                      for m in range(2):
                          pS = ps_s[m][it % 2]
                          k.mm(pS[:, 0:W], kt[m * 64:(m + 1) * 64, s, t * 128:(t + 1) * 128],
                               qs[m * 64:(m + 1) * 64, c0:c0 + W], True, True)
                      for m in range(2):
                          pS = ps_s[m][it % 2]
                          pt = PT[m][it % 3]
                          k.act(pt[:, 0:W], pS[:, 0:W], AF.Exp)
                          k.mm(ps_o[m][:, 0:W], vs[:, s, t, :], pt[:, 0:W], first, last)
                          k.mm(ps_r[m][:, 0:W], ones_bf, pt[:, 0:W], first, last)
                      it += 1
                  for m in range(2):
                      k.recip(rr[m][:, 0:W], ps_r[m][:, 0:W])
                  k.tt("dve", o0[:, 0:W], ps_o[0][:, 0:W], rr[0][:, 0:W], ALU.mult)
                  k.tt("dve", o1[:, 0:W], ps_o[1][:, 0:W], rr[1][:, 0:W], ALU.mult)
                  k.stt("dve", o0[:, 0:W], o1[:, 0:W], neglam[:, 0:1], o0[:, 0:W], ALU.mult, ALU.add)
                  k.act(osq[:, 0:W], o0[:, 0:W], AF.Square)
                  pe_ = ps_s[0][it % 2]
                  k.mm(pe_[:, 0:W], ones_bf, osq[:, 0:W], True, True)
                  k.act(ors[:, 0:W], pe_[:, 0:W], AF.Sqrt, bias=cst[:, 0:1], scale=1.0 / 128)
                  k.recip(ors[:, 0:W], ors[:, 0:W])
                  oo = oout[(h * 5 + qb) % 2]
                  k.stt("dve", oo[:, 0:W], o0[:, 0:W], gfin[:, 0:1], ors[:, 0:W], ALU.mult, ALU.mult)
                  k.dma(mixA[h * 128:(h + 1) * 128, c0:c0 + W], oo[:, 0:W], q="sp")

    if do_fft:
      with k.scope():
          TWO_PI = 2.0 * np.pi
          PQ_sb = k.sb("PQ_sb", [128, 4, 18, 512], BF16)
          for s in range(4):
              pv = PQall.h[s].rearrange("(t p) c -> p t c", p=128)
              for half in range(2):
                  k.dma(PQ_sb[:, s, half * 9:(half + 1) * 9, :], PQall.view(pv[:, half * 9:(half + 1) * 9, :]), q="sp")
          wf_f = k.sb("wf_f", [128, 2, 128], F32)
          wf_b = k.sb("wf_b", [128, 2, 128], BF16)
          k.dma(wf_f[:, :, :], wfbd[:, :, :])
          k.copy("dve", wf_b[:, :, :], wf_f[:, :, :])
          kr_sb = k.sb("kr_sb", [128, TOK], F32)
          lc_sb2 = k.sb("lc_sb2", [128, 66], F32)
          hp = k.sb("hp", [128, 1], F32)
          k.dma(kr_sb[:, :], krow[:, :])
          k.dma(lc_sb2[:, :], lcol[:, :])
          k.memset("dve", hp[:, :], np.pi / 2)
          gt = [k.sb(f"gt{i}", [128, 512], F32) for i in range(2)]
          gi = [k.sb(f"gi{i}", [128, 512], I32) for i in range(2)]
          gr = [k.sb(f"gr{i}", [128, 512], F32) for i in range(2)]
          ga = [k.sb(f"ga{i}", [128, 512], F32) for i in range(2)]
          tcos = [k.sb(f"tcos{i}", [128, 512], BF16) for i in range(3)]
          tsin = [k.sb(f"tsin{i}", [128, 512], BF16) for i in range(3)]
          zbf = [k.sb(f"zbf{i}", [128, 512], BF16) for i in range(2)]
          fo = [k.sb(f"fo{i}", [128, 512], BF16) for i in range(2)]
          gi_ = 0
          zi = 0
          for kb in range(5):
              acc = [pss[0], pss[1]]
              if kb < 4:
                  W, c0, lts, scale = 512, kb * 512, [(lt, lt // 16, lt % 16) for lt in range(64)], 1.0 / np.sqrt(8192.0)
              else:
                  W, c0, lts, scale = 256, 2048, [(64, 0, 16), (65, 0, 17)], 1.0 / 16.0
              for li_, (lt, s, t) in enumerate(lts):
                  a = gi_ % 2; b3 = gi_ % 3; gi_ += 1
                  k.ts("dve", gt[a][:, 0:W], kr_sb[:, c0:c0 + W], lc_sb2[:, lt:lt + 1], ALU.mult)
                  k.copy("dve", gi[a][:, 0:W], gt[a][:, 0:W])
                  k.copy("dve", gr[a][:, 0:W], gi[a][:, 0:W])
                  k.tt("dve", gt[a][:, 0:W], gt[a][:, 0:W], gr[a][:, 0:W], ALU.subtract)
                  k.act(ga[a][:, 0:W], gt[a][:, 0:W], AF.Abs)
                  k.act(tsin[b3][:, 0:W], gt[a][:, 0:W], AF.Sin, scale=-TWO_PI)
                  k.act(tcos[b3][:, 0:W], ga[a][:, 0:W], AF.Sin, bias=hp[:, 0:1], scale=-TWO_PI)
                  first = li_ == 0
                  last = li_ == len(lts) - 1
                  for m in range(2):
                      k.mm(acc[m][:, 0:W], PQ_sb[:, s, t, m * 128:(m + 1) * 128], tcos[b3][:, 0:W], first, False)
                      k.mm(acc[m][:, 0:W], PQ_sb[:, s, t, 256 + m * 128:256 + (m + 1) * 128], tsin[b3][:, 0:W], False, last)
              for m in range(2):
                  zb = zbf[zi % 2]
                  k.act(zb[:, 0:W], acc[m][:, 0:W], AF.Identity, scale=float(scale))
                  pw = pss[2 + zi % 2]
                  k.mm(pw[:, 0:W], wf_b[:, m, :], zb[:, 0:W], True, True)
                  f_ = fo[zi % 2]
                  k.copy("dve", f_[:, 0:W], pw[:, 0:W])
                  k.dma(mixF[m * 128:(m + 1) * 128, c0:c0 + W], f_[:, 0:W], q="sp")
                  zi += 1

    if do_s5:
      with k.scope():
          TW = 2.0 * np.pi
          u_sb = k.sb("u_sb", [64, 4, TOK], BF16)
          for s in range(4):
              k.dma(u_sb[:, s, :], UTall[s, 0:64, :], q="sp")
          p_sb = k.sb("p_sb", [128, 4, 3], F32)
          BT_f = k.sb("BT_f", [64, 4, 2, 128], F32)
          BT_b = k.sb("BT_b", [64, 4, 2, 128], BF16)
          CT_f = k.sb("CT_f", [128, 4, 2, 64], F32)
          CT_b = k.sb("CT_b", [128, 4, 2, 64], BF16)
          tau = k.sb("tau", [128, 2, 512], F32)
          k.dma(p_sb[:, :, :], s5p[:, :, :])
          k.dma(BT_f[:, :, :, :], BTd[:, :, :, :])
          k.dma(CT_f[:, :, :, :], CTd[:, :, :, :])
          k.dma(tau[:, :, :], tauAB[:, :, :])
          k.copy("dve", BT_b[:, :, :, :], BT_f[:, :, :, :])
          k.copy("dve", CT_b[:, :, 0, :], CT_f[:, :, 0, :])
          k.ts("dve", CT_b[:, :, 1, :], CT_f[:, :, 1, :], -1.0, ALU.mult)
          sc = k.sb("sc", [128, 4, 16], F32)
          Cc = [k.sb(f"Cc{i}", [128, 512], F32) for i in range(4)]
          Sn = [k.sb(f"Sn{i}", [128, 512], F32) for i in range(4)]
          R1r = [k.sb(f"R1r{i}", [128, 512], F32) for i in range(4)]
          R1i = [k.sb(f"R1i{i}", [128, 512], F32) for i in range(4)]
          magT = [k.sb(f"magT{i}", [128, 512], F32) for i in range(4)]
          ET = k.sb("ET", [128, 4, 2, 4], F32)
          tb = [k.sb(f"tb{i}", [128, 512], F32) for i in range(3)]
          ti = k.sb("ti", [128, 512], I32)
          onesT = k.sb("onesT", [128, 512], F32)
          k.memset("dve", onesT[:, :], 1.0)

          def frac_sin(out, turns):
              k.copy("dve", ti[:, :], turns)
              k.copy("dve", tb[2][:, :], ti[:, :])
              k.tt("dve", tb[2][:, :], turns, tb[2][:, :], ALU.subtract)
              k.act(out, tb[2][:, :], AF.Sin, scale=TW)

          for dq in range(4):
              c = lambda i: sc[:, dq, i:i + 1]
              ldt, are, aim = p_sb[:, dq, 0:1], p_sb[:, dq, 1:2], p_sb[:, dq, 2:3]
              k.act(c(0), ldt, AF.Exp)
              k.tt("dve", c(1), c(0), are, ALU.mult)
              k.act(c(2), c(1), AF.Exp)
              k.tt("dve", c(3), c(0), aim, ALU.mult)
              k.ts("dve", c(3), c(3), 1.0 / TW, ALU.mult)
              k.ts("dve", c(4), c(3), 32.0, ALU.mult)
              k.copy("dve", ti[:, 0:1], c(4))
              k.copy("dve", c(5), ti[:, 0:1])
              k.tt("dve", c(4), c(4), c(5), ALU.subtract)
              k.ts("dve", tb[0][:, :], tau[:, 0, :], c(4), ALU.mult)
              k.stt("dve", tb[0][:, :], tau[:, 1, :], c(3), tb[0][:, :], ALU.mult, ALU.add)
              frac_sin(Sn[dq][:, :], tb[0][:, :])
              k.ts("dve", tb[1][:, :], tb[0][:, :], 0.25, ALU.add)
              frac_sin(Cc[dq][:, :], tb[1][:, :])
              k.tt("dve", c(6), c(2), Cc[dq][:, 1:2], ALU.mult)
              k.tt("dve", c(7), c(2), Sn[dq][:, 1:2], ALU.mult)
              k.ts("dve", c(8), c(6), -1.0, ALU.add)
              k.tt("dve", c(9), are, are, ALU.mult)
              k.stt("dve", c(9), aim, aim, c(9), ALU.mult, ALU.add)
              k.recip(c(9), c(9))
              k.tt("dve", c(10), c(8), are, ALU.mult)
              k.stt("dve", c(10), c(7), aim, c(10), ALU.mult, ALU.add)
              k.tt("dve", c(10), c(10), c(9), ALU.mult)
              k.tt("dve", c(11), c(7), are, ALU.mult)
              k.tt("dve", c(12), c(8), aim, ALU.mult)
              k.tt("dve", c(11), c(11), c(12), ALU.subtract)
              k.tt("dve", c(11), c(11), c(9), ALU.mult)
              k.ts("dve", c(12), c(10), -1.0, ALU.mult)
              k.ts("dve", R1r[dq][:, :], Cc[dq][:, :], c(10), ALU.mult)
              k.stt("dve", R1r[dq][:, :], Sn[dq][:, :], c(11), R1r[dq][:, :], ALU.mult, ALU.add)
              k.ts("dve", R1i[dq][:, :], Cc[dq][:, :], c(11), ALU.mult)
              k.stt("dve", R1i[dq][:, :], Sn[dq][:, :], c(12), R1i[dq][:, :], ALU.mult, ALU.add)
              k.ts("dve", magT[dq][:, :], onesT[:, :], c(2), ALU.mult)
              for ti_, T in enumerate((256, 512)):
                  er, ei, eni = ET[:, dq, ti_, 0:1], ET[:, dq, ti_, 1:2], ET[:, dq, ti_, 2:3]
                  k.tt("dve", c(13), Cc[dq][:, T - 1:T], Cc[dq][:, 1:2], ALU.mult)
                  k.tt("dve", c(14), Sn[dq][:, T - 1:T], Sn[dq][:, 1:2], ALU.mult)
                  k.tt("dve", er, c(13), c(14), ALU.subtract)
                  k.tt("dve", c(13), Cc[dq][:, T - 1:T], Sn[dq][:, 1:2], ALU.mult)
                  k.tt("dve", c(14), Sn[dq][:, T - 1:T], Cc[dq][:, 1:2], ALU.mult)
                  k.tt("dve", ei, c(13), c(14), ALU.add)
                  k.ts("dve", eni, ei, -1.0, ALU.mult)

          Y_sb = k.sb("Y_sb", [64, NY], F32)
          zin = k.sb("zin", [128, 4, 2], F32)
          k.memset("dve", zin[:, :, :], 0.0)
          wre = [k.sb(f"wre{i}", [128, 512], F32) for i in range(2)]
          wim = [k.sb(f"wim{i}", [128, 512], F32) for i in range(2)]
          ta = [k.sb(f"ta{i}", [128, 512], F32) for i in range(2)]
          zre = [k.sb(f"zre{i}", [128, 512], F32) for i in range(2)]
          zim = [k.sb(f"zim{i}", [128, 512], F32) for i in range(2)]
          pa = [k.sb(f"pa{i}", [128, 512], F32) for i in range(2)]
          pb = [k.sb(f"pb{i}", [128, 512], F32) for i in range(2)]
          sre = [k.sb(f"sre{i}", [128, 512], BF16) for i in range(2)]
          sim_ = [k.sb(f"sim{i}", [128, 512], BF16) for i in range(2)]
          tcar = k.sb("tcar", [128, 2], F32)
          ps_br = [pss[4], pss[5]]
          ps_bi = [pss[6], pss[7]]
          ps_y = [pss[2], pss[3]]
          it = 0
          chunks_f = [(0, 2048, 256, 8192)] + [(c // 4, (c % 4) * 512, 512, c * 512) for c in range(16)]
          chunks_b = [(0, 2048, 256, 8192)] + [(c // 4, (c % 4) * 512, 512, c * 512) for c in reversed(range(16))]
          for d in range(2):
              chunks = chunks_f if d == 0 else chunks_b
              for ci_, (s, c0, T, y0) in enumerate(chunks):
                  ti_ = 0 if T == 256 else 1
                  uu = u_sb[0:64, s, c0:c0 + T]
                  if d == 1:
                      uu = rev(uu)
                  py = ps_y[ci_ % 2]
                  for q in range(2):
                      dq = d * 2 + q
                      a = it % 2; it += 1
                      br, bi_ = ps_br[a], ps_bi[a]
                      k.mm(br[:, 0:T], BT_b[:, dq, 0, :], uu, True, True)
                      k.mm(bi_[:, 0:T], BT_b[:, dq, 1, :], uu, True, True)
                      k.tt("dve", wre[a][:, 0:T], br[:, 0:T], R1r[dq][:, 0:T], ALU.mult)
                      k.tt("dve", ta[a][:, 0:T], bi_[:, 0:T], R1i[dq][:, 0:T], ALU.mult)
                      k.tt("pool", wre[a][:, 0:T], wre[a][:, 0:T], ta[a][:, 0:T], ALU.subtract)
                      k.tt("dve", wim[a][:, 0:T], bi_[:, 0:T], R1r[dq][:, 0:T], ALU.mult)
                      k.tt("dve", ta[a][:, 0:T], br[:, 0:T], R1i[dq][:, 0:T], ALU.mult)
                      k.tt("pool", wim[a][:, 0:T], wim[a][:, 0:T], ta[a][:, 0:T], ALU.add)
                      k.scan(zre[a][:, 0:T], magT[dq][:, 0:T], wre[a][:, 0:T], zin[:, dq, 0:1])
                      k.scan(zim[a][:, 0:T], magT[dq][:, 0:T], wim[a][:, 0:T], zin[:, dq, 1:2])
                      er, ei, eni = ET[:, dq, ti_, 0:1], ET[:, dq, ti_, 1:2], ET[:, dq, ti_, 2:3]
                      k.ts("dve", tcar[:, 0:1], zre[a][:, T - 1:T], er, ALU.mult)
                      k.stt("dve", zin[:, dq, 0:1], zim[a][:, T - 1:T], eni, tcar[:, 0:1], ALU.mult, ALU.add)
                      k.ts("dve", tcar[:, 1:2], zim[a][:, T - 1:T], er, ALU.mult)
                      k.stt("dve", zin[:, dq, 1:2], zre[a][:, T - 1:T], ei, tcar[:, 1:2], ALU.mult, ALU.add)
                      k.tt("pool", pa[a][:, 0:T], zre[a][:, 0:T], Cc[dq][:, 0:T], ALU.mult)
                      k.tt("pool", pb[a][:, 0:T], zim[a][:, 0:T], Sn[dq][:, 0:T], ALU.mult)
                      k.tt("pool", sre[a][:, 0:T], pa[a][:, 0:T], pb[a][:, 0:T], ALU.subtract)
                      k.tt("pool", pa[a][:, 0:T], zim[a][:, 0:T], Cc[dq][:, 0:T], ALU.mult)
                      k.tt("pool", pb[a][:, 0:T], zre[a][:, 0:T], Sn[dq][:, 0:T], ALU.mult)
                      k.tt("pool", sim_[a][:, 0:T], pa[a][:, 0:T], pb[a][:, 0:T], ALU.add)
                      k.mm(py[0:64, 0:T], CT_b[:, dq, 0, :], sre[a][:, 0:T], q == 0, False)
                      k.mm(py[0:64, 0:T], CT_b[:, dq, 1, :], sim_[a][:, 0:T], False, q == 1)
                  if d == 0:
                      k.copy("act", Y_sb[:, y0:y0 + T], py[0:64, 0:T])
                  else:
                      yv = rev(Y_sb[:, y0:y0 + T])
                      k.tt("dve", yv, py[0:64, 0:T], yv, ALU.add)
          for c in range(4):
              k.dma(Yd[:, c * 2112:(c + 1) * 2112], Y_sb[:, c * 2112:(c + 1) * 2112], q="sp")
    return k.finish()


CBLOCKS = [(c * 256, 256, 0) for c in range(8)] + [(2048, 256, 1)]


def build_C():
    k = KB()
    xT = k.dram("xT", [1024, TOK], F32, "ExternalInput")
    modT = k.dram("modT", [128, 48, 2], F32, "ExternalInput")
    mixA = k.dram("mixA", [512, TOK], BF16, "ExternalInput")
    mixF = k.dram("mixF", [256, TOK], BF16, "ExternalInput")
    Ysel = k.dram("Ysel", [4, 64, TOK], F32, "ExternalInput")
    DUT = k.dram("DUT", [256, TOK], F32, "ExternalInput")
    w_glu = k.dram("w_glu", [256, 256], F32, "ExternalInput")
    bglu = k.dram("bglu", [128, 2], F32, "ExternalInput")
    w_out = k.dram("w_out", [1024, 1024], F32, "ExternalInput")
    w_ff1 = k.dram("w_ff1", [1024, 4096], F32, "ExternalInput")
    w_ff2 = k.dram("w_ff2", [4096, 1024], F32, "ExternalInput")
    cmat = k.dram("cmat", [128, 3, 128], BF16, "ExternalInput")
    xo = k.dram("xTo", [1024, TOK], F32, "ExternalOutput")

    W = 256
    wo_bf = k.sb("wo_bf", [128, 8, 1024], BF16)
    wg_bf = k.sb("wg_bf", [128, 2, 256], BF16)
    w1_bf = k.sb("w1_bf", [128, 8, 4096], BF16)
    w2_bf = k.sb("w2_bf", [128, 32, 1024], BF16)
    stage = [k.sb(f"stage{i}", [128, 1024], F32) for i in range(2)]
    mod_sb = k.sb("mod_sb", [128, 48, 2], F32)
    bg_sb = k.sb("bg_sb", [128, 2], F32)
    cmat_sb = k.sb("cmat_sb", [128, 3, 128], BF16)
    cst = k.sb("cst", [128, 2], F32)
    x_sb = k.sb("x_sb", [128, 8, W], F32)
    mix_bf = k.sb("mix_bf", [128, 8, W], BF16, subaxis=1)
    h2_bf = k.sb("h2_bf", [128, 8, W], BF16)
    xsq = k.sb("xsq", [128, 8, W], BF16)
    act_bf = k.sb("act_bf", [128, 32, W], BF16, subaxis=1)
    rstd = k.sb("rstd", [128, W], F32)
    tmp = [k.sb(f"tmp{i}", [128, W], F32) for i in range(2)]
    y_sb = [k.sb(f"y_sb{i}", [128, W], F32) for i in range(2)]
    du_sb = [k.sb(f"du_sb{i}", [128, W], F32) for i in range(2)]
    u1 = [k.sb(f"u1{i}", [128, W], F32) for i in range(2)]
    sg = [k.sb(f"sg{i}", [128, W], F32) for i in range(2)]
    hg = [k.sb(f"hg{i}", [128, W], F32) for i in range(2)]
    hgb = k.sb("hgb", [128, 2, W], BF16)
    rl = [k.sb(f"rl{i}", [128, W], F32) for i in range(2)]
    ps_ss = k.ps("ps_ss")
    ps_g = k.ps("ps_g")
    pm = [k.ps(f"pm{i}") for i in range(4)]

    k.memset("dve", cst[:, 0:1], EPS)
    k.dma(mod_sb[:, :, :], modT[:, :, :])
    k.dma(bg_sb[:, :], bglu[:, :])
    k.dma(cmat_sb[:, :, :], cmat[:, :, :])
    ones_bf = cmat_sb[:, 0, :]
    si = 0

    def load_cast(dst, src):
        nonlocal si
        sg_ = stage[si % 2]
        ncol = dst.ap.shape[-1]
        k.dma(sg_[:, 0:ncol], src, q=("sp", "act")[si % 2])
        k.copy(("dve", "act")[si % 2], dst, sg_[:, 0:ncol])
        si += 1

    gv = w_glu.h.rearrange("(k p) c -> p k c", p=128)
    for kk in range(2):
        load_cast(wg_bf[:, kk, :], w_glu.view(gv[:, kk, :]))
    ov = w_out.h.rearrange("(k p) c -> p k c", p=128)
    for kk in range(8):
        load_cast(wo_bf[:, kk, :], w_out.view(ov[:, kk, :]))
    v1 = w_ff1.h.rearrange("(k p) c -> p k c", p=128)
    for kk in range(8):
        for cc in range(4):
            load_cast(w1_bf[:, kk, cc * 1024:(cc + 1) * 1024], w_ff1.view(v1[:, kk, cc * 1024:(cc + 1) * 1024]))
    v2 = w_ff2.h.rearrange("(k p) c -> p k c", p=128)
    for kk in range(32):
        load_cast(w2_bf[:, kk, :], w_ff2.view(v2[:, kk, :]))

    xv = xT.h.rearrange("(k p) t -> p k t", p=128)
    xov = xo.h.rearrange("(k p) t -> p k t", p=128)
    pi = 0
    for bi, (c0, _, n) in enumerate(CBLOCKS):
        k.dma(x_sb[:, :, :], xT.view(xv[:, :, c0:c0 + W]), q="sp")
        for h in range(4):
            k.dma(mix_bf[:, h, :], mixA[h * 128:(h + 1) * 128, c0:c0 + W], q="sp")
        for m in range(2):
            k.dma(mix_bf[:, 6 + m, :], mixF[m * 128:(m + 1) * 128, c0:c0 + W], q="sp")
        for m in range(2):
            k.dma(y_sb[m][0:64, :], Ysel[2 * m, :, c0:c0 + W], q="sp")
            k.dma(y_sb[m][64:128, :], Ysel[2 * m + 1, :, c0:c0 + W], q="sp")
            k.dma(du_sb[m][:, :], DUT[m * 128:(m + 1) * 128, c0:c0 + W], q="sp")
            k.tt("dve", y_sb[m][:, :], y_sb[m][:, :], du_sb[m][:, :], ALU.add)
            k.tt("dve", u1[m][:, :], y_sb[m][:, :], y_sb[m][:, :], ALU.mult)
            k.ts("dve", u1[m][:, :], u1[m][:, :], 0.044715, ALU.mult, 1.0, ALU.add)
            k.tt("dve", u1[m][:, :], u1[m][:, :], y_sb[m][:, :], ALU.mult)
            k.act(sg[m][:, :], u1[m][:, :], AF.Sigmoid, scale=1.5957691216057308)
            k.tt("dve", hg[m][:, :], y_sb[m][:, :], sg[m][:, :], ALU.mult)
            k.act(hgb[:, m, :], hg[m][:, :], AF.Identity)
        for m in range(2):
            for kt in range(2):
                k.mm(ps_g[:, 0:W], wg_bf[:, kt, m * 128:(m + 1) * 128], hgb[:, kt, :], kt == 0, kt == 1)
            k.act(sg[m][:, :], ps_g[:, 0:W], AF.Sigmoid, bias=bg_sb[:, m:m + 1], scale=1.0)
            k.tt("dve", mix_bf[:, 4 + m, :], hg[m][:, :], sg[m][:, :], ALU.mult)
        for dt in range(8):
            p_ = pm[pi % 4]; pi += 1
            for kt in range(8):
                k.mm(p_[:, 0:W], wo_bf[:, kt, dt * 128:(dt + 1) * 128], mix_bf[:, kt, :], kt == 0, kt == 7)
            k.stt("dve", x_sb[:, dt, :], p_[:, 0:W], mod_sb[:, 2 * 8 + dt, n:n + 1], x_sb[:, dt, :], ALU.mult, ALU.add)
        rmsnorm_mod(k, x_sb, W, n, mod_sb, 3, 4, h2_bf, xsq, ps_ss, rstd, tmp, ones_bf, cst)
        for ft in range(32):
            p_ = pm[pi % 4]; pi += 1
            for kt in range(8):
                k.mm(p_[:, 0:W], w1_bf[:, kt, ft * 128:(ft + 1) * 128], h2_bf[:, kt, :], kt == 0, kt == 7)
            r_ = rl[ft % 2]
            k.act(r_[:, :], p_[:, 0:W], AF.Relu)
            k.tt("dve", act_bf[:, ft, :], r_[:, :], r_[:, :], ALU.mult)
        for dt in range(8):
            p_ = pm[pi % 4]; pi += 1
            for kt in range(32):
                k.mm(p_[:, 0:W], w2_bf[:, kt, dt * 128:(dt + 1) * 128], act_bf[:, kt, :], kt == 0, kt == 31)
            k.stt("dve", x_sb[:, dt, :], p_[:, 0:W], mod_sb[:, 5 * 8 + dt, n:n + 1], x_sb[:, dt, :], ALU.mult, ALU.add)
        k.dma(xo.view(xov[:, :, c0:c0 + W]), x_sb[:, :, :], q="sp")
    return k.finish()


L_SEQ = 8192
CTX = 256
TOKC = 2304


def const_cmat():
    ones = np.ones((128, 128), np.float32)
    p = np.arange(128)
    blk = (p[:, None] // 64 == p[None, :] // 64).astype(np.float32)
    RT = np.zeros((128, 128), np.float32)
    for m in range(128):
        if m % 64 < 32:
            RT[m + 32, m] = -1.0
        else:
            RT[m - 32, m] = 1.0
    return np.stack([ones, blk, RT], axis=1).astype(NPBF)


def const_cs():
    CS = np.zeros((256, 512), np.float64)
    c = np.arange(64)[:, None]
    d = np.arange(64)[None, :]
    ang = 2 * np.pi * c * d / 64.0
    for g in range(4):
        CS[g * 64:(g + 1) * 64, g * 64:(g + 1) * 64] = np.cos(ang) / 8.0
        CS[g * 64:(g + 1) * 64, 256 + g * 64:256 + (g + 1) * 64] = np.sin(ang) / 8.0
    return CS.reshape(2, 128, 512).transpose(1, 0, 2).astype(NPBF)


def const_rope(j):
    t = np.arange(j * 2048, (j + 1) * 2048)
    row = (t // 64).astype(np.float32)
    col = (t % 64).astype(np.float32)
    inv = np.power(np.float32(10000.0), -np.arange(16, dtype=np.float32) / np.float32(16)).astype(np.float32)
    ang = np.concatenate([row[:, None] * inv, col[:, None] * inv], axis=-1).astype(np.float32)
    cos = np.cos(ang).astype(np.float32)
    sin = np.sin(ang).astype(np.float32)
    out = np.zeros((128, 2, TOKC), np.float32)
    pidx = (np.arange(128) % 64) % 32
    out[:, 0, :2048] = cos[:, pidx].T
    out[:, 1, :2048] = sin[:, pidx].T
    out[:, 0, 2048:] = 1.0
    return out


def fm(v, ntile):
    return np.ascontiguousarray(np.asarray(v).reshape(ntile, 128).T)


def const_tauAB():
    tau = np.arange(512)
    out = np.zeros((128, 2, 512), np.float32)
    out[:, 0, :] = (tau // 32)[None, :]
    out[:, 1, :] = (tau % 32)[None, :]
    return out


_DFT_CACHE = {}


def const_dft(j):
    if j in _DFT_CACHE:
        return _DFT_CACHE[j]
    L = 8192
    l = np.arange(L, dtype=np.int64).reshape(64, 128)
    k = (2048 * j + np.arange(2048, dtype=np.int64)).reshape(4, 512)
    kl = (l[None, :, :, None] * k[:, None, None, :]) % L
    ang = kl.astype(np.float64) * (2 * np.pi / L)
    sc = 1.0 / np.sqrt(L)
    out = np.empty((4, 64, 128, 1024), NPBF)
    out[..., :512] = (np.cos(ang) * sc).astype(NPBF)
    out[..., 512:] = (-np.sin(ang) * sc).astype(NPBF)
    _DFT_CACHE[j] = out
    return out


def const_dftc():
    L = 256
    l = np.arange(L, dtype=np.int64).reshape(2, 128)
    k = np.arange(L, dtype=np.int64)
    ang = ((l[:, :, None] * k[None, None, :]) % L).astype(np.float64) * (2 * np.pi / L)
    out = np.empty((2, 128, 512), NPBF)
    out[..., :256] = (np.cos(ang) / 16.0).astype(NPBF)
    out[..., 256:] = (-np.sin(ang) / 16.0).astype(NPBF)
    return out


def s5_layout(inp, li, j):
    s5p = np.zeros((128, 4, 3), np.float32)
    BT = np.zeros((64, 4, 2, 128), np.float32)
    CT = np.zeros((128, 4, 2, 64), np.float32)
    for d in range(2):
        for q in range(2):
            dq = d * 2 + q
            for gl in range(2):
                g = 4 * j + 2 * q + gl
                ps = slice(gl * 64, (gl + 1) * 64)
                s5p[ps, dq, 0] = inp["ssm_log_dt"][li, d, g]
                s5p[ps, dq, 1] = inp["ssm_a_re"][li, d, g]
                s5p[ps, dq, 2] = inp["ssm_a_im"][li, d, g]
                chs = slice((2 * q + gl) * 16, (2 * q + gl + 1) * 16)
                BT[chs, dq, 0, ps] = inp["ssm_b_re"][li, d, g].T
                BT[chs, dq, 1, ps] = inp["ssm_b_im"][li, d, g].T
                CT[ps, dq, 0, chs] = inp["ssm_c_re"][li, d, g].T
                CT[ps, dq, 1, chs] = inp["ssm_c_im"][li, d, g].T
    return s5p, BT, CT


def wfbd_layout(w_fnet_l):
    out = np.zeros((128, 2, 128), np.float32)
    for m in range(2):
        for gl in range(2):
            out[gl * 64:(gl + 1) * 64, m, gl * 64:(gl + 1) * 64] = w_fnet_l[2 * m + gl]
    return out


def const_krow(j):
    out = np.zeros((128, TOKC), np.float32)
    out[:, :2048] = ((2048 * j + np.arange(2048)) / 8192.0).astype(np.float32)[None, :]
    out[:, 2048:] = (np.arange(256) / 256.0).astype(np.float32)[None, :]
    return out


def const_lcol():
    out = np.zeros((128, 66), np.float32)
    p = np.arange(128)
    for lt in range(64):
        out[:, lt] = 128 * lt + p
    out[:, 64] = p
    out[:, 65] = 128 + p
    return out


_NC_CACHE = {}


def _prog(name, builder):
    if name not in _NC_CACHE:
        _NC_CACHE[name] = builder()
    return _NC_CACHE[name]


def _launch(name, builder, in_maps):
    nc = _prog(name, builder)
    res = run_bass_kernel_spmd(nc, in_maps, core_ids=list(range(8)))
    return [{k: np.asarray(v) for k, v in r.items()} for r in res.results]


def kernel(**inputs):
    inp = {k: np.asarray(v) for k, v in inputs.items()}
    x, c, ctx, c_ctx = inp["x"], inp["c"], inp["ctx"], inp["c_ctx"]
    depth = inp["w_in"].shape[0]
    f32 = np.float32
    cm = const_cmat(); csm = const_cs(); tab = const_tauAB(); lc = const_lcol()
    in_maps = []
    for core in range(8):
        b, j = core // 4, core % 4
        cp = np.stack([c[b], c_ctx], -1).astype(f32)
        in_maps.append({
            "cT": np.ascontiguousarray(cp.reshape(8, 128, 2).transpose(1, 0, 2)),
            "w_mod": inp["w_mod"][j], "bmodT": fm(inp["b_mod"][j], 48),
            "gn": np.concatenate([fm(inp["g_norm1"][j], 8), fm(inp["g_norm2"][j], 8)], axis=1),
        })
    modT = [r["modT"] for r in _launch("M", build_M, in_maps)]
    xT = []
    for core in range(8):
        b, j = core // 4, core % 4
        xT.append(np.ascontiguousarray(np.concatenate([x[b, j * 2048:(j + 1) * 2048].T, ctx[b].T], axis=1)))
    ropes = [const_rope(j) for j in range(4)]
    krows = [const_krow(j) for j in range(4)]
    for li in range(depth):
        lam_init = 0.8 - 0.6 * math.exp(-0.3 * li)
        in_maps = []
        for core in range(8):
            b, j = core // 4, core % 4
            in_maps.append({
                "xT": xT[core], "modT": modT[b * 4 + li], "w_in": inp["w_in"][li],
                "gqk": np.stack([np.tile(inp["g_qnorm"][li], 2), np.tile(inp["g_knorm"][li], 2)], -1).astype(f32),
                "rope": ropes[j], "cmat": cm, "cs": csm, "dcol": fm(inp["ssm_d"][li], 2),
            })
        outA = _launch("A", build_A, in_maps)
        in_maps = []
        for core in range(8):
            b, j = core // 4, core % 4
            grp = [outA[b * 4 + s] for s in range(4)]
            s5p, BT, CT = s5_layout(inp, li, j)
            in_maps.append({
                "QT": outA[core]["QT"], "KTall": np.stack([g["KT"] for g in grp]),
                "Vall": np.stack([g["V"] for g in grp]),
                "lamv": np.stack([inp["lam_q1"][li], inp["lam_k1"][li], inp["lam_q2"][li], inp["lam_k2"][li]],
                                 -1).astype(f32),
                "lconst": np.tile(np.array([[lam_init, 1 - lam_init]], f32), (128, 1)),
                "gsub": inp["g_subln"][li].reshape(128, 1).astype(f32), "cmat": cm,
                "UTall": np.stack([g["UT"][64 * j:64 * (j + 1)] for g in grp]),
                "s5p": s5p, "BT": BT, "CT": CT, "tauAB": tab,
                "PQall": np.stack([g["PQ"] for g in grp]), "krow": krows[j], "lcol": lc,
                "wfbd": wfbd_layout(inp["w_fnet"][li]),
            })
        outB = _launch("B", build_B, in_maps)
        in_maps = []
        for core in range(8):
            b, j = core // 4, core % 4
            Ys = [outB[b * 4 + s]["Y"] for s in range(4)]
            ysel = np.stack([np.concatenate([y[:, 2048 * j:2048 * (j + 1)], y[:, 8192:]], 1) for y in Ys])
            in_maps.append({
                "xT": xT[core], "modT": modT[b * 4 + li], "mixA": outB[core]["mixA"], "mixF": outB[core]["mixF"],
                "Ysel": ysel, "DUT": outA[core]["DUT"], "w_glu": inp["w_glu"][li], "bglu": fm(inp["b_glu"][li], 2),
                "w_out": inp["w_out"][li], "w_ff1": inp["w_ff1"][li], "w_ff2": inp["w_ff2"][li], "cmat": cm,
            })
        outC = _launch("C", build_C, in_maps)
        xT = [r["xTo"] for r in outC]
    out = np.empty(x.shape, f32)
    for core in range(8):
        b, j = core // 4, core % 4
        out[b, j * 2048:(j + 1) * 2048, :] = xT[core][:, :2048].T
    return out
```

```python
import math
import contextlib
import numpy as np
import ml_dtypes
import concourse.bass as bass
import concourse.mybir as mybir
from concourse.bass_utils import run_bass_kernel_spmd

F32 = mybir.dt.float32
BF16 = mybir.dt.bfloat16
I32 = mybir.dt.int32
AF = mybir.ActivationFunctionType
ALU = mybir.AluOpType
NPBF = ml_dtypes.bfloat16


class Buf:
    __slots__ = ("name", "w", "r", "psum")

    def __init__(self, name, psum=False):
        self.name = name
        self.w = None
        self.r = {}
        self.psum = psum


class V:
    __slots__ = ("ap", "bufs")

    def __init__(self, ap, bufs):
        self.ap = ap
        self.bufs = bufs


class T:
    def __init__(self, handle, name, shape, subaxis=None, psum=False):
        self.h = handle
        self.name = name
        self.shape = list(shape)
        self.subaxis = subaxis
        n = shape[subaxis] if subaxis is not None else 1
        self.bufs = [Buf(f"{name}.{i}", psum) for i in range(n)]

    def __getitem__(self, idx):
        if not isinstance(idx, tuple):
            idx = (idx,)
        ap = self.h[idx]
        bufs = self.bufs
        if self.subaxis is not None and len(idx) > self.subaxis:
            ix = idx[self.subaxis]
            if isinstance(ix, int):
                bufs = [self.bufs[ix]]
            elif isinstance(ix, slice):
                rng = range(*ix.indices(self.shape[self.subaxis]))
                bufs = [self.bufs[i] for i in rng]
        return V(ap, bufs)

    def view(self, ap, sub=None):
        bufs = self.bufs if sub is None else [self.bufs[i] for i in sub]
        return V(ap, bufs)


class Sched:
    ENGS = ("pe", "act", "dve", "pool", "sp")

    def __init__(self, n_dma_sems=8):
        self.prog = {e: [] for e in self.ENGS}
        self.cnt = {}
        self.seen = {e: {} for e in self.ENGS}
        self.n_dma_sems = n_dma_sems
        self.dma_rr = {e: 0 for e in self.ENGS}
        self.ninst = 0

    def semkeys(self):
        keys = list(self.ENGS)
        for e in ("sp", "act", "pool"):
            for i in range(self.n_dma_sems):
                keys.append(f"d_{e}{i}")
        return keys

    def _wait(self, e, dep):
        k, n = dep
        if self.seen[e].get(k, 0) >= n:
            return
        self.seen[e][k] = n
        self.prog[e].append(("wait", k, n))

    def _deps(self, e, reads, writes):
        deps = []
        for b in reads:
            if b.w is not None:
                deps.append(b.w)
            if b.psum:
                for k, n in b.r.items():
                    if k != e:
                        deps.append((k, n))
        for b in writes:
            if b.w is not None:
                deps.append(b.w)
            for k, n in b.r.items():
                deps.append((k, n))
        for d in deps:
            if d[0] == "pe" and e == "pe":
                continue
            self._wait(e, d)

    def _mark(self, key, n, reads, writes):
        for b in reads:
            if b.r.get(key, 0) < n:
                b.r[key] = n
        for b in writes:
            b.w = (key, n)
            b.r = {}

    def op(self, e, fn, reads=(), writes=(), sig=True):
        self._deps(e, reads, writes)
        n = self.cnt.get(e, 0) + 1
        if sig:
            self.cnt[e] = n
            self.prog[e].append(("op", fn, e, 1))
        else:
            self.prog[e].append(("op", fn, None, 0))
        self._mark(e, n, reads, writes)
        self.ninst += 1

    def dma(self, e, fn, reads=(), writes=()):
        i = self.dma_rr[e]
        self.dma_rr[e] = (i + 1) % self.n_dma_sems
        key = f"d_{e}{i}"
        uses = self.cnt.get(key, 0)
        if uses > 0:
            self._wait(e, (key, uses))
        self._deps(e, reads, writes)
        n = uses + 16
        self.cnt[key] = n
        self.prog[e].append(("op", fn, key, 16))
        self._mark(key, n, reads, writes)
        self.ninst += 1

    def wait_all(self, e):
        for k, n in list(self.cnt.items()):
            self._wait(e, (k, n))

    def barrier(self):
        for e in self.ENGS:
            self.wait_all(e)

    def emit(self, block, sems):
        engobj = {"pe": "tensor", "act": "scalar", "dve": "vector", "pool": "gpsimd", "sp": "sync"}

        def mk(e):
            def body(eng):
                for item in self.prog[e]:
                    if item[0] == "wait":
                        eng.wait_ge(sems[item[1]], item[2])
                    else:
                        _, fn, key, inc = item
                        if key is None:
                            fn(eng)
                        else:
                            fn(eng).then_inc(sems[key], inc)
            return body

        for e in self.ENGS:
            getattr(block, engobj[e])(mk(e))


class KB:
    def __init__(self):
        self.nc = bass.Bass("TRN2", target_bir_lowering=False)
        self.S = Sched()
        self.st = contextlib.ExitStack()
        self.dq = 0
        self.sb_off = (self.nc.sbuf_base + 63) // 64 * 64
        self.sb_top = self.nc.sbuf_top
        self.nalloc = 0

    @contextlib.contextmanager
    def scope(self):
        mark = self.sb_off
        yield
        self.S.barrier()
        self.sb_off = mark

    def dram(self, name, shape, dt, kind, subaxis=None):
        h = self.nc.dram_tensor(name, list(shape), dt, kind=kind).ap()
        return T(h, name, shape, subaxis)

    def sb(self, name, shape, dt, subaxis=None):
        nbytes = int(np.prod(shape[1:])) * mybir.dt.size(dt)
        nbytes = (nbytes + 63) // 64 * 64
        assert self.sb_off + nbytes <= self.sb_top, f"SBUF overflow allocating {name}: {self.sb_off}+{nbytes}>{self.sb_top}"
        self.nalloc += 1
        h = self.nc.alloc_sbuf_tensor_at(f"{name}_{self.nalloc}", list(shape), dt, offset=self.sb_off)
        self.sb_off += nbytes
        return T(h, name, shape, subaxis)

    def ps(self, name, shape=None, dt=F32):
        shape = [128, 512] if dt == F32 else [128, 1024]
        h = self.st.enter_context(self.nc.psum_tensor(name, list(shape), dt))
        return T(h, name, shape, None, psum=True)

    @staticmethod
    def _rb(*vs):
        out = []
        for v in vs:
            if isinstance(v, V):
                out.extend(v.bufs)
        return out

    @staticmethod
    def _a(v):
        return v.ap if isinstance(v, V) else v

    def dma(self, out, in_, q=None):
        if q is None:
            q = "sp"
            self.dq += 1
        self.S.dma(q, lambda e: e.dma_start(out=out.ap, in_=in_.ap), reads=in_.bufs, writes=out.bufs)

    def mm(self, out, lhsT, rhs, start, stop, sig=None):
        self.S.op("pe", lambda e: e.matmul(out.ap, lhsT=lhsT.ap, rhs=rhs.ap, start=start, stop=stop),
                  reads=self._rb(lhsT, rhs), writes=out.bufs, sig=(stop if sig is None else sig))

    def act(self, out, in_, func, bias=None, scale=None, eng="act"):
        kw = {}
        if bias is not None:
            kw["bias"] = self._a(bias)
        if scale is not None:
            kw["scale"] = self._a(scale)
        self.S.op(eng, lambda e: e.activation(out=out.ap, in_=in_.ap, func=func, **kw),
                  reads=self._rb(in_, bias, scale), writes=out.bufs)

    def copy(self, eng, out, in_):
        if eng == "act":
            self.S.op(eng, lambda e: e.copy(out=out.ap, in_=in_.ap), reads=in_.bufs, writes=out.bufs)
        else:
            self.S.op(eng, lambda e: e.tensor_copy(out=out.ap, in_=in_.ap), reads=in_.bufs, writes=out.bufs)

    def tt(self, eng, out, in0, in1, op):
        self.S.op(eng, lambda e: e.tensor_tensor(out=out.ap, in0=in0.ap, in1=in1.ap, op=op),
                  reads=self._rb(in0, in1), writes=out.bufs)

    def ts(self, eng, out, in0, s1, op0, s2=None, op1=None):
        if op1 is None:
            self.S.op(eng, lambda e: e.tensor_single_scalar(out=out.ap, in_=in0.ap, scalar=self._a(s1), op=op0),
                      reads=self._rb(in0, s1), writes=out.bufs)
        else:
            self.S.op(eng, lambda e: e.tensor_scalar(out=out.ap, in0=in0.ap, scalar1=self._a(s1), scalar2=self._a(s2),
                                                     op0=op0, op1=op1),
                      reads=self._rb(in0, s1, s2), writes=out.bufs)

    def stt(self, eng, out, in0, scalar, in1, op0, op1):
        self.S.op(eng, lambda e: e.scalar_tensor_tensor(out=out.ap, in0=in0.ap, scalar=self._a(scalar), in1=in1.ap,
                                                        op0=op0, op1=op1),
                  reads=self._rb(in0, scalar, in1), writes=out.bufs)

    def recip(self, out, in_):
        self.S.op("dve", lambda e: e.reciprocal(out=out.ap, in_=in_.ap), reads=in_.bufs, writes=out.bufs)

    def memset(self, eng, out, val):
        self.S.op(eng, lambda e: e.memset(out.ap, val), writes=out.bufs)

    def scan(self, out, d0, d1, init):
        self.S.op("dve", lambda e: e.tensor_tensor_scan(out=out.ap, data0=d0.ap, data1=d1.ap, initial=self._a(init),
                                                        op0=ALU.mult, op1=ALU.add),
                  reads=self._rb(d0, d1, init), writes=out.bufs)

    def finish(self):
        self.S.wait_all("sp")
        sems = {k: self.st.enter_context(self.nc.semaphore(k)) for k in self.S.semkeys()}
        block = self.st.enter_context(self.nc.Block())
        self.S.emit(block, sems)
        self.st.close()
        return self.nc


def rev(v):
    return V(v.ap[:, ::-1], v.bufs)


TOK = 2304
BLOCKS = [(0, 512, 0), (512, 512, 0), (1024, 512, 0), (1536, 512, 0), (2048, 256, 1)]
EPS = 1e-6


def build_M():
    k = KB()
    cT = k.dram("cT", [128, 8, 2], F32, "ExternalInput")
    w_mod = k.dram("w_mod", [1024, 6144], F32, "ExternalInput")
    bmodT = k.dram("bmodT", [128, 48], F32, "ExternalInput")
    gn = k.dram("gn", [128, 16], F32, "ExternalInput")
    modT = k.dram("modT", [128, 48, 2], F32, "ExternalOutput")

    c_sb = k.sb("c_sb", [128, 8, 2], F32)
    cact = k.sb("cact", [128, 8, 2], F32)
    bm_sb = k.sb("bm_sb", [128, 48], F32)
    gn_sb = k.sb("gn_sb", [128, 16], F32)
    mod_sb = k.sb("mod_sb", [128, 48, 2], F32)
    wch = [k.sb(f"wch{i}", [128, 8, 512], F32) for i in range(3)]
    pss = [k.ps(f"psm{i}", [128, 2]) for i in range(4)]

    k.dma(c_sb[:, :, :], cT[:, :, :], q="sp")
    k.dma(bm_sb[:, :], bmodT[:, :], q="sp")
    k.dma(gn_sb[:, :], gn[:, :], q="sp")
    k.act(cact[:, :, :], c_sb[:, :, :], AF.Silu)
    wv = w_mod.h.rearrange("(k p) c -> p k c", p=128)
    for c in range(12):
        w = wch[c % 3]
        for kk in range(8):
            k.dma(w[:, kk, :], w_mod.view(wv[:, kk, c * 512:(c + 1) * 512]), q=("sp", "act")[kk % 2])
        for f in range(4):
            ft = c * 4 + f
            ps = pss[ft % 4]
            for kk in range(8):
                k.mm(ps[:, 0:2], w[:, kk, f * 128:(f + 1) * 128], cact[:, kk, :], kk == 0, kk == 7)
            k.ts("dve", mod_sb[:, ft, :], ps[:, 0:2], bm_sb[:, ft:ft + 1], ALU.add)
    for s, g0 in ((1, 0), (4, 8)):
        for n in range(2):
            k.stt("dve", mod_sb[:, s * 8:(s + 1) * 8, n], mod_sb[:, s * 8:(s + 1) * 8, n], 1.0,
                  gn_sb[:, g0:g0 + 8], ALU.add, ALU.mult)
    k.dma(modT[:, :, :], mod_sb[:, :, :], q="sp")
    return k.finish()


def rmsnorm_mod(k, x_sb, W, n, mod_sb, s_sh, s_gm, h_bf, xsq, ps_ss, rstd, tmp, ones_bf, cst):
    for kk in range(8):
        k.act(xsq[:, kk, 0:W], x_sb[:, kk, 0:W], AF.Square)
    for kk in range(8):
        k.mm(ps_ss[:, 0:W], ones_bf, xsq[:, kk, 0:W], kk == 0, kk == 7)
    k.act(rstd[:, 0:W], ps_ss[:, 0:W], AF.Sqrt, bias=cst[:, 0:1], scale=1.0 / 1024)
    k.recip(rstd[:, 0:W], rstd[:, 0:W])
    for kk in range(8):
        t = tmp[kk % 2]
        k.stt("dve", t[:, 0:W], x_sb[:, kk, 0:W], mod_sb[:, s_gm * 8 + kk, n:n + 1], rstd[:, 0:W], ALU.mult, ALU.mult)
        k.act(h_bf[:, kk, 0:W], t[:, 0:W], AF.Identity, bias=mod_sb[:, s_sh * 8 + kk, n:n + 1], scale=1.0)


def build_A(nb=5, upto=9):
    k = KB()
    xT = k.dram("xT", [1024, TOK], F32, "ExternalInput")
    modT = k.dram("modT", [128, 48, 2], F32, "ExternalInput")
    w_in = k.dram("w_in", [1024, 2048], F32, "ExternalInput")
    gqk = k.dram("gqk", [128, 2], F32, "ExternalInput")
    rope = k.dram("rope", [128, 2, TOK], F32, "ExternalInput")
    cmat = k.dram("cmat", [128, 3, 128], BF16, "ExternalInput")
    cs = k.dram("cs", [128, 2, 512], BF16, "ExternalInput")
    dcol = k.dram("dcol", [128, 2], F32, "ExternalInput")
    QT = k.dram("QT", [4, 128, TOK], BF16, "ExternalOutput")
    KT = k.dram("KT", [4, 128, TOK], BF16, "ExternalOutput")
    Vd = k.dram("V", [TOK, 512], BF16, "ExternalOutput")
    UT = k.dram("UT", [256, TOK], BF16, "ExternalOutput")
    PQ = k.dram("PQ", [TOK, 512], BF16, "ExternalOutput")
    DUT = k.dram("DUT", [256, TOK], F32, "ExternalOutput")

    w_bf = k.sb("w_bf", [128, 8, 2048], BF16)
    stage = [k.sb(f"stage{i}", [128, 8, 256], F32) for i in range(2)]
    mod_sb = k.sb("mod_sb", [128, 48, 2], F32)
    gqk_sb = k.sb("gqk_sb", [128, 2], F32)
    dcol_sb = k.sb("dcol_sb", [128, 2], F32)
    cmat_sb = k.sb("cmat_sb", [128, 3, 128], BF16)
    cs_sb = k.sb("cs_sb", [128, 2, 512], BF16)
    cst = k.sb("cst", [128, 2], F32)
    x_sb = [k.sb(f"x_sb{i}", [128, 8, 512], F32) for i in range(2)]
    rope_sb = [k.sb(f"rope_sb{i}", [128, 2, 512], F32) for i in range(2)]
    xsq = k.sb("xsq", [128, 8, 512], BF16)
    h_bf = [k.sb(f"h_bf{i}", [128, 8, 512], BF16) for i in range(2)]
    rstd = k.sb("rstd", [128, 512], F32)
    tmp = [k.sb(f"tmp{i}", [128, 512], F32) for i in range(2)]
    qg = [k.sb(f"qg{i}", [128, 512], BF16) for i in range(2)]
    sq = [k.sb(f"sq{i}", [128, 512], BF16) for i in range(2)]
    rs = [k.sb(f"rs{i}", [128, 512], F32) for i in range(2)]
    t1 = [k.sb(f"t1{i}", [128, 512], F32) for i in range(2)]
    t2 = [k.sb(f"t2{i}", [128, 512], F32) for i in range(2)]
    oq = [k.sb(f"oq{i}", [128, 512], BF16) for i in range(3)]
    vtok = [k.sb(f"vtok{i}", [128, 512], BF16) for i in range(2)]
    ftl = k.sb("ftl", [128, 2, 512], BF16)
    pqs = [k.sb(f"pqs{i}", [128, 512], BF16) for i in range(2)]
    ubf = [k.sb(f"ubf{i}", [128, 512], BF16) for i in range(2)]
    duf = [k.sb(f"duf{i}", [128, 512], F32) for i in range(2)]

    ps_ss = k.ps("ps_ss", [128, 512])
    ps_main = [k.ps(f"ps_main{i}", [128, 512]) for i in range(2)]
    ps_ss2 = k.ps("ps_ss2", [128, 512])
    ps_rot = k.ps("ps_rot", [128, 512])
    ps_v = k.ps("ps_v", [128, 512])
    ps_pq = k.ps("ps_pq", [128, 512])

    k.memset("dve", cst[:, 0:1], EPS)
    k.dma(mod_sb[:, :, :], modT[:, :, :], q="sp")
    k.dma(gqk_sb[:, :], gqk[:, :], q="sp")
    k.dma(dcol_sb[:, :], dcol[:, :], q="sp")
    k.dma(cmat_sb[:, :, :], cmat[:, :, :], q="sp")
    k.dma(cs_sb[:, :, :], cs[:, :, :], q="sp")
    wv = w_in.h.rearrange("(k p) c -> p k c", p=128)
    for c in range(8):
        sg = stage[c % 2]
        for kk in range(8):
            k.dma(sg[:, kk, :], w_in.view(wv[:, kk, c * 256:(c + 1) * 256]), q=("sp", "act")[kk % 2])
        k.copy(("dve", "act")[c % 2], w_bf[:, :, c * 256:(c + 1) * 256], sg[:, :, :])
    ones_bf = cmat_sb[:, 0, :]
    blk64 = cmat_sb[:, 1, :]
    RTm = cmat_sb[:, 2, :]
    xv = xT.h.rearrange("(k p) t -> p k t", p=128)
    mmi = 0
    qi = 0
    for bi, (c0, W, n) in enumerate(BLOCKS[:nb]):
        xs = x_sb[bi % 2]
        rp = rope_sb[bi % 2]
        hb = h_bf[bi % 2]
        for kk in range(8):
            k.dma(xs[:, kk, 0:W], xT.view(xv[:, kk, c0:c0 + W]), q="sp")
        k.dma(rp[:, :, 0:W], rope[:, :, c0:c0 + W], q="sp")
        if upto < 1: continue
        rmsnorm_mod(k, xs, W, n, mod_sb, 0, 1, hb, xsq, ps_ss, rstd, tmp, ones_bf, cst)
        if upto < 2: continue
        for idx in range(8):
            isq = idx < 4
            col = idx * 128
            pm = ps_main[mmi % 2]; mmi += 1
            for kk in range(8):
                k.mm(pm[:, 0:W], w_bf[:, kk, col:col + 128], hb[:, kk, 0:W], kk == 0, kk == 7)
            a = qi % 2; qi += 1
            gcol = gqk_sb[:, 0:1] if isq else gqk_sb[:, 1:2]
            k.act(qg[a][:, 0:W], pm[:, 0:W], AF.Identity, scale=gcol)
            k.act(sq[a][:, 0:W], pm[:, 0:W], AF.Square)
            k.mm(ps_ss2[:, 0:W], blk64, sq[a][:, 0:W], True, True)
            k.mm(ps_rot[:, 0:W], RTm, qg[a][:, 0:W], True, True)
            k.act(rs[a][:, 0:W], ps_ss2[:, 0:W], AF.Sqrt, bias=cst[:, 0:1], scale=1.0 / 64)
            k.recip(rs[a][:, 0:W], rs[a][:, 0:W])
            k.tt("dve", t1[a][:, 0:W], qg[a][:, 0:W], rp[:, 0, 0:W], ALU.mult)
            k.tt("dve", t2[a][:, 0:W], ps_rot[:, 0:W], rp[:, 1, 0:W], ALU.mult)
            k.tt("dve", t1[a][:, 0:W], t1[a][:, 0:W], t2[a][:, 0:W], ALU.add)
            o = oq[idx % 3]
            k.stt("dve", o[:, 0:W], t1[a][:, 0:W], 0.125 if isq else 1.0, rs[a][:, 0:W], ALU.mult, ALU.mult)
            dst = QT if isq else KT
            k.dma(dst[idx % 4, :, c0:c0 + W], o[:, 0:W], q="sp")
        if upto < 3.0: continue
        for m in range(2):
            pm = ps_main[mmi % 2]; mmi += 1
            col = 1792 + m * 128
            for kk in range(8):
                k.mm(pm[:, 0:W], w_bf[:, kk, col:col + 128], hb[:, kk, 0:W], kk == 0, kk == 7)
            k.act(ftl[:, m, 0:W], pm[:, 0:W], AF.Identity)
        if upto < 3.2: continue
        for m in range(2):
            pm = ps_main[mmi % 2]; mmi += 1
            col = 1536 + m * 128
            for kk in range(8):
                k.mm(pm[:, 0:W], w_bf[:, kk, col:col + 128], hb[:, kk, 0:W], kk == 0, kk == 7)
            k.act(ubf[m][:, 0:W], pm[:, 0:W], AF.Identity)
            if upto < 3.4: continue
            k.ts("dve", duf[m][:, 0:W], pm[:, 0:W], dcol_sb[:, m:m + 1], ALU.mult)
            if upto < 3.6: continue
            k.dma(UT[m * 128:(m + 1) * 128, c0:c0 + W], ubf[m][:, 0:W], q="sp")
            k.dma(DUT[m * 128:(m + 1) * 128, c0:c0 + W], duf[m][:, 0:W], q="sp")
        if upto < 4: continue
        for tt in range(W // 128):
            tsl = slice(tt * 128, (tt + 1) * 128)
            for kk in range(8):
                k.mm(ps_v[:, :], hb[:, kk, tsl], w_bf[:, kk, 1024:1536], kk == 0, kk == 7)
            vt = vtok[tt % 2]
            k.act(vt[:, :], ps_v[:, :], AF.Identity)
            k.dma(Vd[c0 + tt * 128:c0 + (tt + 1) * 128, :], vt[:, :], q="sp")
            for m in range(2):
                k.mm(ps_pq[:, :], ftl[:, m, tsl], cs_sb[:, m, :], m == 0, m == 1)
            pt = pqs[tt % 2]
            k.copy("dve", pt[:, :], ps_pq[:, :])
            k.dma(PQ[c0 + tt * 128:c0 + (tt + 1) * 128, :], pt[:, :], q="sp")
    return k.finish()


KEYT = [(s, t) for s in range(4) for t in range(16)] + [(0, 16), (0, 17)]
NY = 8448


def build_B(do_attn=True, do_s5=True, do_fft=True, nheads=4, nqb=5):
    k = KB()
    QT = k.dram("QT", [4, 128, TOK], BF16, "ExternalInput")
    KTall = k.dram("KTall", [4, 4, 128, TOK], BF16, "ExternalInput")
    Vall = k.dram("Vall", [4, TOK, 512], BF16, "ExternalInput")
    lamv = k.dram("lamv", [64, 4], F32, "ExternalInput")
    lconst = k.dram("lconst", [128, 2], F32, "ExternalInput")
    gsub = k.dram("gsub", [128, 1], F32, "ExternalInput")
    cmat = k.dram("cmat", [128, 3, 128], BF16, "ExternalInput")
    mixA = k.dram("mixA", [512, TOK], BF16, "ExternalOutput")
    UTall = k.dram("UTall", [4, 64, TOK], BF16, "ExternalInput")
    s5p = k.dram("s5p", [128, 4, 3], F32, "ExternalInput")
    BTd = k.dram("BT", [64, 4, 2, 128], F32, "ExternalInput")
    CTd = k.dram("CT", [128, 4, 2, 64], F32, "ExternalInput")
    tauAB = k.dram("tauAB", [128, 2, 512], F32, "ExternalInput")
    Yd = k.dram("Y", [64, NY], F32, "ExternalOutput")
    PQall = k.dram("PQall", [4, TOK, 512], BF16, "ExternalInput")
    krow = k.dram("krow", [128, TOK], F32, "ExternalInput")
    lcol = k.dram("lcol", [128, 66], F32, "ExternalInput")
    wfbd = k.dram("wfbd", [128, 2, 128], F32, "ExternalInput")
    mixF = k.dram("mixF", [256, TOK], BF16, "ExternalOutput")

    cmat_sb = k.sb("cmat_sb", [128, 3, 128], BF16)
    cst = k.sb("cst", [128, 4], F32)
    k.dma(cmat_sb[:, :, :], cmat[:, :, :])
    k.memset("dve", cst[:, 0:1], EPS)
    k.memset("dve", cst[:, 1:2], 1.0)
    ones_bf = cmat_sb[:, 0, :]
    pss = [k.ps(f"ps{i}") for i in range(8)]

    if do_attn:
      with k.scope():
          lam_sb = k.sb("lam_sb", [64, 4], F32)
          lprod = k.sb("lprod", [64, 2], F32)
          lc_sb = k.sb("lc_sb", [128, 2], F32)
          gs_sb = k.sb("gs_sb", [128, 1], F32)
          ones_f = k.sb("ones_f", [64, 128], F32)
          lam_e = k.sb("lam_e", [128, 2], F32)
          neglam = k.sb("neglam", [128, 1], F32)
          gfin = k.sb("gfin", [128, 1], F32)
          k.dma(lam_sb[:, :], lamv[:, :])
          k.dma(lc_sb[:, :], lconst[:, :])
          k.dma(gs_sb[:, :], gsub[:, :])
          k.memset("dve", ones_f[:, :], 1.0)
          k.tt("dve", lprod[:, 0:1], lam_sb[:, 0:1], lam_sb[:, 1:2], ALU.mult)
          k.tt("dve", lprod[:, 1:2], lam_sb[:, 2:3], lam_sb[:, 3:4], ALU.mult)
          k.mm(pss[0][:, 0:2], ones_f[:, :], lprod[:, :], True, True)
          k.act(lam_e[:, :], pss[0][:, 0:2], AF.Exp)
          k.tt("dve", neglam[:, :], lam_e[:, 1:2], lam_e[:, 0:1], ALU.subtract)
          k.tt("dve", neglam[:, :], neglam[:, :], lc_sb[:, 0:1], ALU.subtract)
          k.tt("dve", gfin[:, :], gs_sb[:, :], lc_sb[:, 1:2], ALU.mult)

          KT_sb = [k.sb(f"KT_sb{i}", [128, 4, TOK], BF16) for i in range(2)]
          V_sb = [k.sb(f"V_sb{i}", [128, 4, 18, 128], BF16) for i in range(2)]
          QT_sb = [k.sb(f"QT_sb{i}", [128, TOK], BF16) for i in range(2)]
          PT = [[k.sb(f"PT{m}{i}", [128, 512], BF16) for i in range(3)] for m in range(2)]
          rr = [k.sb(f"rr{m}", [128, 512], F32) for m in range(2)]
          racc = [k.sb(f"racc{m}", [128, 512], F32) for m in range(2)]
          ones_f32 = k.sb("ones_f32", [128, 128], F32)
          k.memset("dve", ones_f32[:, :], 1.0)
          o0 = k.sb("o0", [128, 512], F32)
          o1 = k.sb("o1", [128, 512], F32)
          osq = k.sb("osq", [128, 512], BF16)
          ors = k.sb("ors", [128, 512], F32)
          oout = [k.sb(f"oout{i}", [128, 512], BF16) for i in range(2)]
          ps_s = [[pss[0], pss[1]], [pss[2], pss[3]]]
          ps_o = [pss[4], pss[5]]
          ps_r = [pss[6], pss[7]]
          it = 0
          for h in range(nheads):
              kt = KT_sb[h % 2]; vs = V_sb[h % 2]; qs = QT_sb[h % 2]
              for s in range(4):
                  k.dma(kt[:, s, :], KTall[s, h, :, :], q="sp")
                  vv = Vall.h[s].rearrange("(t p) c -> p t c", p=128)
                  for half in range(2):
                      k.dma(vs[:, s, half * 9:(half + 1) * 9, :],
                            Vall.view(vv[:, half * 9:(half + 1) * 9, h * 128:(h + 1) * 128]), q="sp")
              k.dma(qs[:, :], QT[h, :, :], q="sp")
              for qb, (c0, W, n) in enumerate(BLOCKS[:nqb]):
                  keys = KEYT if n == 0 else KEYT[64:]
                  for ki, (s, t) in enumerate(keys):
                      first = ki == 0
                      last = ki == len(keys) - 1
                      for m in range(2):
                          pS = ps_s[m][it % 2]
                          k.mm(pS[:, 0:W], kt[m * 64:(m + 1) * 64, s, t * 128:(t + 1) * 128],
                               qs[m * 64:(m + 1) * 64, c0:c0 + W], True, True)
                      for m in range(2):
                          pS = ps_s[m][it % 2]
                          pt = PT[m][it % 3]
                          k.act(pt[:, 0:W], pS[:, 0:W], AF.Exp)
                          k.mm(ps_o[m][:, 0:W], vs[:, s, t, :], pt[:, 0:W], first, last)
                          re_ = ("dve", "pool")[m]
                          if first:
                              k.copy(re_, racc[m][:, 0:W], pt[:, 0:W])
                          else:
                              k.tt(re_, racc[m][:, 0:W], racc[m][:, 0:W], pt[:, 0:W], ALU.add)
                      it += 1
                  for m in range(2):
                      k.mm(ps_r[m][:, 0:W], ones_f32[:, :], racc[m][:, 0:W], True, True)
                      k.recip(rr[m][:, 0:W], ps_r[m][:, 0:W])
                  k.tt("dve", o0[:, 0:W], ps_o[0][:, 0:W], rr[0][:, 0:W], ALU.mult)
                  k.tt("dve", o1[:, 0:W], ps_o[1][:, 0:W], rr[1][:, 0:W], ALU.mult)
                  k.stt("dve", o0[:, 0:W], o1[:, 0:W], neglam[:, 0:1], o0[:, 0:W], ALU.mult, ALU.add)
                  k.act(osq[:, 0:W], o0[:, 0:W], AF.Square)
                  pe_ = ps_s[0][it % 2]
                  k.mm(pe_[:, 0:W], ones_bf, osq[:, 0:W], True, True)
                  k.act(ors[:, 0:W], pe_[:, 0:W], AF.Sqrt, bias=cst[:, 0:1], scale=1.0 / 128)
                  k.recip(ors[:, 0:W], ors[:, 0:W])
                  oo = oout[(h * 5 + qb) % 2]
                  k.stt("dve", oo[:, 0:W], o0[:, 0:W], gfin[:, 0:1], ors[:, 0:W], ALU.mult, ALU.mult)
                  k.dma(mixA[h * 128:(h + 1) * 128, c0:c0 + W], oo[:, 0:W], q="sp")

    if do_fft:
      with k.scope():
          TWO_PI = 2.0 * np.pi
          PQ_sb = k.sb("PQ_sb", [128, 4, 18, 512], BF16)
          for s in range(4):
              pv = PQall.h[s].rearrange("(t p) c -> p t c", p=128)
              for half in range(2):
                  k.dma(PQ_sb[:, s, half * 9:(half + 1) * 9, :], PQall.view(pv[:, half * 9:(half + 1) * 9, :]), q="sp")
          wf_f = k.sb("wf_f", [128, 2, 128], F32)
          wf_b = k.sb("wf_b", [128, 2, 128], BF16)
          k.dma(wf_f[:, :, :], wfbd[:, :, :])
          k.copy("dve", wf_b[:, :, :], wf_f[:, :, :])
          kr_sb = k.sb("kr_sb", [128, TOK], F32)
          lc_sb2 = k.sb("lc_sb2", [128, 66], F32)
          hp = k.sb("hp", [128, 1], F32)
          k.dma(kr_sb[:, :], krow[:, :])
          k.dma(lc_sb2[:, :], lcol[:, :])
          k.memset("dve", hp[:, :], np.pi / 2)
          gt = [k.sb(f"gt{i}", [128, 512], F32) for i in range(2)]
          gi = [k.sb(f"gi{i}", [128, 512], I32) for i in range(2)]
          gr = [k.sb(f"gr{i}", [128, 512], F32) for i in range(2)]
          ga = [k.sb(f"ga{i}", [128, 512], F32) for i in range(2)]
          tcos = [k.sb(f"tcos{i}", [128, 512], BF16) for i in range(3)]
          tsin = [k.sb(f"tsin{i}", [128, 512], BF16) for i in range(3)]
          zbf = [k.sb(f"zbf{i}", [128, 512], BF16) for i in range(2)]
          fo = [k.sb(f"fo{i}", [128, 512], BF16) for i in range(2)]
          gi_ = 0
          zi = 0
          for kb in range(5):
              acc = [pss[0], pss[1]]
              if kb < 4:
                  W, c0, lts, scale = 512, kb * 512, [(lt, lt // 16, lt % 16) for lt in range(64)], 1.0 / np.sqrt(8192.0)
              else:
                  W, c0, lts, scale = 256, 2048, [(64, 0, 16), (65, 0, 17)], 1.0 / 16.0
              for li_, (lt, s, t) in enumerate(lts):
                  a = gi_ % 2; b3 = gi_ % 3; gi_ += 1
                  k.ts("dve", gt[a][:, 0:W], kr_sb[:, c0:c0 + W], lc_sb2[:, lt:lt + 1], ALU.mult)
                  k.copy("dve", gi[a][:, 0:W], gt[a][:, 0:W])
                  k.copy("dve", gr[a][:, 0:W], gi[a][:, 0:W])
                  k.tt("dve", gt[a][:, 0:W], gt[a][:, 0:W], gr[a][:, 0:W], ALU.subtract)
                  k.act(ga[a][:, 0:W], gt[a][:, 0:W], AF.Abs)
                  k.act(tsin[b3][:, 0:W], gt[a][:, 0:W], AF.Sin, scale=-TWO_PI)
                  k.act(tcos[b3][:, 0:W], ga[a][:, 0:W], AF.Sin, bias=hp[:, 0:1], scale=-TWO_PI)
                  first = li_ == 0
                  last = li_ == len(lts) - 1
                  for m in range(2):
                      k.mm(acc[m][:, 0:W], PQ_sb[:, s, t, m * 128:(m + 1) * 128], tcos[b3][:, 0:W], first, False)
                      k.mm(acc[m][:, 0:W], PQ_sb[:, s, t, 256 + m * 128:256 + (m + 1) * 128], tsin[b3][:, 0:W], False, last,
                           sig=(last or m == 1))
              for m in range(2):
                  zb = zbf[zi % 2]
                  k.act(zb[:, 0:W], acc[m][:, 0:W], AF.Identity, scale=float(scale))
                  pw = pss[2 + zi % 2]
                  k.mm(pw[:, 0:W], wf_b[:, m, :], zb[:, 0:W], True, True)
                  f_ = fo[zi % 2]
                  k.copy("dve", f_[:, 0:W], pw[:, 0:W])
                  k.dma(mixF[m * 128:(m + 1) * 128, c0:c0 + W], f_[:, 0:W], q="sp")
                  zi += 1

    if do_s5:
      with k.scope():
          TW = 2.0 * np.pi
          u_sb = k.sb("u_sb", [64, 4, TOK], BF16)
          for s in range(4):
              k.dma(u_sb[:, s, :], UTall[s, 0:64, :], q="sp")
          p_sb = k.sb("p_sb", [128, 4, 3], F32)
          BT_f = k.sb("BT_f", [64, 4, 2, 128], F32)
          BT_b = k.sb("BT_b", [64, 4, 2, 128], BF16)
          CT_f = k.sb("CT_f", [128, 4, 2, 64], F32)
          CT_b = k.sb("CT_b", [128, 4, 2, 64], BF16)
          tau = k.sb("tau", [128, 2, 512], F32)
          k.dma(p_sb[:, :, :], s5p[:, :, :])
          k.dma(BT_f[:, :, :, :], BTd[:, :, :, :])
          k.dma(CT_f[:, :, :, :], CTd[:, :, :, :])
          k.dma(tau[:, :, :], tauAB[:, :, :])
          k.copy("dve", BT_b[:, :, :, :], BT_f[:, :, :, :])
          k.copy("dve", CT_b[:, :, 0, :], CT_f[:, :, 0, :])
          k.ts("dve", CT_b[:, :, 1, :], CT_f[:, :, 1, :], -1.0, ALU.mult)
          sc = k.sb("sc", [128, 4, 16], F32)
          Cc = [k.sb(f"Cc{i}", [128, 512], F32) for i in range(4)]
          Sn = [k.sb(f"Sn{i}", [128, 512], F32) for i in range(4)]
          R1r = [k.sb(f"R1r{i}", [128, 512], F32) for i in range(4)]
          R1i = [k.sb(f"R1i{i}", [128, 512], F32) for i in range(4)]
          magT = [k.sb(f"magT{i}", [128, 512], F32) for i in range(4)]
          ET = k.sb("ET", [128, 4, 2, 4], F32)
          tb = [k.sb(f"tb{i}", [128, 512], F32) for i in range(3)]
          ti = k.sb("ti", [128, 512], I32)
          onesT = k.sb("onesT", [128, 512], F32)
          k.memset("dve", onesT[:, :], 1.0)

          def frac_sin(out, turns):
              k.copy("dve", ti[:, :], turns)
              k.copy("dve", tb[2][:, :], ti[:, :])
              k.tt("dve", tb[2][:, :], turns, tb[2][:, :], ALU.subtract)
              k.act(out, tb[2][:, :], AF.Sin, scale=TW)

          for dq in range(4):
              c = lambda i: sc[:, dq, i:i + 1]
              ldt, are, aim = p_sb[:, dq, 0:1], p_sb[:, dq, 1:2], p_sb[:, dq, 2:3]
              k.act(c(0), ldt, AF.Exp)
              k.tt("dve", c(1), c(0), are, ALU.mult)
              k.act(c(2), c(1), AF.Exp)
              k.tt("dve", c(3), c(0), aim, ALU.mult)
              k.ts("dve", c(3), c(3), 1.0 / TW, ALU.mult)
              k.ts("dve", c(4), c(3), 32.0, ALU.mult)
              k.copy("dve", ti[:, 0:1], c(4))
              k.copy("dve", c(5), ti[:, 0:1])
              k.tt("dve", c(4), c(4), c(5), ALU.subtract)
              k.ts("dve", tb[0][:, :], tau[:, 0, :], c(4), ALU.mult)
              k.stt("dve", tb[0][:, :], tau[:, 1, :], c(3), tb[0][:, :], ALU.mult, ALU.add)
              frac_sin(Sn[dq][:, :], tb[0][:, :])
              k.ts("dve", tb[1][:, :], tb[0][:, :], 0.25, ALU.add)
              frac_sin(Cc[dq][:, :], tb[1][:, :])
              k.tt("dve", c(6), c(2), Cc[dq][:, 1:2], ALU.mult)
              k.tt("dve", c(7), c(2), Sn[dq][:, 1:2], ALU.mult)
              k.ts("dve", c(8), c(6), -1.0, ALU.add)
              k.tt("dve", c(9), are, are, ALU.mult)
              k.stt("dve", c(9), aim, aim, c(9), ALU.mult, ALU.add)
              k.recip(c(9), c(9))
              k.tt("dve", c(10), c(8), are, ALU.mult)
              k.stt("dve", c(10), c(7), aim, c(10), ALU.mult, ALU.add)
              k.tt("dve", c(10), c(10), c(9), ALU.mult)
              k.tt("dve", c(11), c(7), are, ALU.mult)
              k.tt("dve", c(12), c(8), aim, ALU.mult)
              k.tt("dve", c(11), c(11), c(12), ALU.subtract)
              k.tt("dve", c(11), c(11), c(9), ALU.mult)
              k.ts("dve", c(12), c(10), -1.0, ALU.mult)
              k.ts("dve", R1r[dq][:, :], Cc[dq][:, :], c(10), ALU.mult)
              k.stt("dve", R1r[dq][:, :], Sn[dq][:, :], c(11), R1r[dq][:, :], ALU.mult, ALU.add)
              k.ts("dve", R1i[dq][:, :], Cc[dq][:, :], c(11), ALU.mult)
              k.stt("dve", R1i[dq][:, :], Sn[dq][:, :], c(12), R1i[dq][:, :], ALU.mult, ALU.add)
              k.ts("dve", magT[dq][:, :], onesT[:, :], c(2), ALU.mult)
              for ti_, T in enumerate((256, 512)):
                  er, ei, eni = ET[:, dq, ti_, 0:1], ET[:, dq, ti_, 1:2], ET[:, dq, ti_, 2:3]
                  k.tt("dve", c(13), Cc[dq][:, T - 1:T], Cc[dq][:, 1:2], ALU.mult)
                  k.tt("dve", c(14), Sn[dq][:, T - 1:T], Sn[dq][:, 1:2], ALU.mult)
                  k.tt("dve", er, c(13), c(14), ALU.subtract)
                  k.tt("dve", c(13), Cc[dq][:, T - 1:T], Sn[dq][:, 1:2], ALU.mult)
                  k.tt("dve", c(14), Sn[dq][:, T - 1:T], Cc[dq][:, 1:2], ALU.mult)
                  k.tt("dve", ei, c(13), c(14), ALU.add)
                  k.ts("dve", eni, ei, -1.0, ALU.mult)

          Y_sb = k.sb("Y_sb", [64, NY], F32)
          zin = k.sb("zin", [128, 4, 2], F32)
          k.memset("dve", zin[:, :, :], 0.0)
          wre = [k.sb(f"wre{i}", [128, 512], F32) for i in range(2)]
          wim = [k.sb(f"wim{i}", [128, 512], F32) for i in range(2)]
          ta = [k.sb(f"ta{i}", [128, 512], F32) for i in range(2)]
          zre = [k.sb(f"zre{i}", [128, 512], F32) for i in range(2)]
          zim = [k.sb(f"zim{i}", [128, 512], F32) for i in range(2)]
          pa = [k.sb(f"pa{i}", [128, 512], F32) for i in range(2)]
          pb = [k.sb(f"pb{i}", [128, 512], F32) for i in range(2)]
          sre = [k.sb(f"sre{i}", [128, 512], BF16) for i in range(2)]
          sim_ = [k.sb(f"sim{i}", [128, 512], BF16) for i in range(2)]
          tcar = k.sb("tcar", [128, 2], F32)
          ps_br = [pss[4], pss[5]]
          ps_bi = [pss[6], pss[7]]
          ps_y = [pss[2], pss[3]]
          it = 0
          chunks_f = [(0, 2048, 256, 8192)] + [(c // 4, (c % 4) * 512, 512, c * 512) for c in range(16)]
          chunks_b = [(0, 2048, 256, 8192)] + [(c // 4, (c % 4) * 512, 512, c * 512) for c in reversed(range(16))]
          for d in range(2):
              chunks = chunks_f if d == 0 else chunks_b
              for ci_, (s, c0, T, y0) in enumerate(chunks):
                  ti_ = 0 if T == 256 else 1
                  uu = u_sb[0:64, s, c0:c0 + T]
                  if d == 1:
                      uu = rev(uu)
                  py = ps_y[ci_ % 2]
                  for q in range(2):
                      dq = d * 2 + q
                      a = it % 2; it += 1
                      br, bi_ = ps_br[a], ps_bi[a]
                      k.mm(br[:, 0:T], BT_b[:, dq, 0, :], uu, True, True)
                      k.mm(bi_[:, 0:T], BT_b[:, dq, 1, :], uu, True, True)
                      k.tt("dve", wre[a][:, 0:T], br[:, 0:T], R1r[dq][:, 0:T], ALU.mult)
                      k.tt("dve", ta[a][:, 0:T], bi_[:, 0:T], R1i[dq][:, 0:T], ALU.mult)
                      k.tt("dve", wre[a][:, 0:T], wre[a][:, 0:T], ta[a][:, 0:T], ALU.subtract)
                      k.tt("dve", wim[a][:, 0:T], bi_[:, 0:T], R1r[dq][:, 0:T], ALU.mult)
                      k.tt("dve", ta[a][:, 0:T], br[:, 0:T], R1i[dq][:, 0:T], ALU.mult)
                      k.tt("dve", wim[a][:, 0:T], wim[a][:, 0:T], ta[a][:, 0:T], ALU.add)
                      k.scan(zre[a][:, 0:T], magT[dq][:, 0:T], wre[a][:, 0:T], zin[:, dq, 0:1])
                      k.scan(zim[a][:, 0:T], magT[dq][:, 0:T], wim[a][:, 0:T], zin[:, dq, 1:2])
                      er, ei, eni = ET[:, dq, ti_, 0:1], ET[:, dq, ti_, 1:2], ET[:, dq, ti_, 2:3]
                      k.ts("dve", tcar[:, 0:1], zre[a][:, T - 1:T], er, ALU.mult)
                      k.stt("dve", zin[:, dq, 0:1], zim[a][:, T - 1:T], eni, tcar[:, 0:1], ALU.mult, ALU.add)
                      k.ts("dve", tcar[:, 1:2], zim[a][:, T - 1:T], er, ALU.mult)
                      k.stt("dve", zin[:, dq, 1:2], zre[a][:, T - 1:T], ei, tcar[:, 1:2], ALU.mult, ALU.add)
                      k.tt("pool", pa[a][:, 0:T], zre[a][:, 0:T], Cc[dq][:, 0:T], ALU.mult)
                      k.tt("pool", pb[a][:, 0:T], zim[a][:, 0:T], Sn[dq][:, 0:T], ALU.mult)
                      k.tt("pool", sre[a][:, 0:T], pa[a][:, 0:T], pb[a][:, 0:T], ALU.subtract)
                      k.tt("pool", pa[a][:, 0:T], zim[a][:, 0:T], Cc[dq][:, 0:T], ALU.mult)
                      k.tt("pool", pb[a][:, 0:T], zre[a][:, 0:T], Sn[dq][:, 0:T], ALU.mult)
                      k.tt("pool", sim_[a][:, 0:T], pa[a][:, 0:T], pb[a][:, 0:T], ALU.add)
                      k.mm(py[0:64, 0:T], CT_b[:, dq, 0, :], sre[a][:, 0:T], q == 0, False)
                      k.mm(py[0:64, 0:T], CT_b[:, dq, 1, :], sim_[a][:, 0:T], False, q == 1)
                  if d == 0:
                      k.copy("act", Y_sb[:, y0:y0 + T], py[0:64, 0:T])
                  else:
                      yv = rev(Y_sb[:, y0:y0 + T])
                      k.tt("dve", yv, py[0:64, 0:T], yv, ALU.add)
          for c in range(4):
              k.dma(Yd[:, c * 2112:(c + 1) * 2112], Y_sb[:, c * 2112:(c + 1) * 2112], q="sp")
    return k.finish()


CBLOCKS = [(c * 256, 256, 0) for c in range(8)] + [(2048, 256, 1)]


def build_C():
    k = KB()
    xT = k.dram("xT", [1024, TOK], F32, "ExternalInput")
    modT = k.dram("modT", [128, 48, 2], F32, "ExternalInput")
    mixA = k.dram("mixA", [512, TOK], BF16, "ExternalInput")
    mixF = k.dram("mixF", [256, TOK], BF16, "ExternalInput")
    Ysel = k.dram("Ysel", [4, 64, TOK], F32, "ExternalInput")
    DUT = k.dram("DUT", [256, TOK], F32, "ExternalInput")
    w_glu = k.dram("w_glu", [256, 256], F32, "ExternalInput")
    bglu = k.dram("bglu", [128, 2], F32, "ExternalInput")
    w_out = k.dram("w_out", [1024, 1024], F32, "ExternalInput")
    w_ff1 = k.dram("w_ff1", [1024, 4096], F32, "ExternalInput")
    w_ff2 = k.dram("w_ff2", [4096, 1024], F32, "ExternalInput")
    cmat = k.dram("cmat", [128, 3, 128], BF16, "ExternalInput")
    xo = k.dram("xTo", [1024, TOK], F32, "ExternalOutput")

    W = 256
    wo_bf = k.sb("wo_bf", [128, 8, 1024], BF16)
    wg_bf = k.sb("wg_bf", [128, 2, 256], BF16)
    w1_bf = k.sb("w1_bf", [128, 8, 4096], BF16)
    w2_bf = k.sb("w2_bf", [128, 32, 1024], BF16)
    stage = [k.sb(f"stage{i}", [128, 1024], F32) for i in range(2)]
    mod_sb = k.sb("mod_sb", [128, 48, 2], F32)
    bg_sb = k.sb("bg_sb", [128, 2], F32)
    cmat_sb = k.sb("cmat_sb", [128, 3, 128], BF16)
    cst = k.sb("cst", [128, 2], F32)
    x_sb = k.sb("x_sb", [128, 8, W], F32)
    mix_bf = k.sb("mix_bf", [128, 8, W], BF16, subaxis=1)
    h2_bf = k.sb("h2_bf", [128, 8, W], BF16)
    xsq = k.sb("xsq", [128, 8, W], BF16)
    act_bf = k.sb("act_bf", [128, 32, W], BF16, subaxis=1)
    rstd = k.sb("rstd", [128, W], F32)
    tmp = [k.sb(f"tmp{i}", [128, W], F32) for i in range(2)]
    y_sb = [k.sb(f"y_sb{i}", [128, W], F32) for i in range(2)]
    du_sb = [k.sb(f"du_sb{i}", [128, W], F32) for i in range(2)]
    u1 = [k.sb(f"u1{i}", [128, W], F32) for i in range(2)]
    sg = [k.sb(f"sg{i}", [128, W], F32) for i in range(2)]
    hg = [k.sb(f"hg{i}", [128, W], F32) for i in range(2)]
    hgb = k.sb("hgb", [128, 2, W], BF16)
    rl = [k.sb(f"rl{i}", [128, W], F32) for i in range(2)]
    ps_ss = k.ps("ps_ss")
    ps_g = k.ps("ps_g")
    pm = [k.ps(f"pm{i}") for i in range(4)]

    k.memset("dve", cst[:, 0:1], EPS)
    k.dma(mod_sb[:, :, :], modT[:, :, :])
    k.dma(bg_sb[:, :], bglu[:, :])
    k.dma(cmat_sb[:, :, :], cmat[:, :, :])
    ones_bf = cmat_sb[:, 0, :]
    si = 0

    def load_cast(dst, src):
        nonlocal si
        sg_ = stage[si % 2]
        ncol = dst.ap.shape[-1]
        k.dma(sg_[:, 0:ncol], src, q=("sp", "act")[si % 2])
        k.copy(("dve", "act")[si % 2], dst, sg_[:, 0:ncol])
        si += 1

    gv = w_glu.h.rearrange("(k p) c -> p k c", p=128)
    for kk in range(2):
        load_cast(wg_bf[:, kk, :], w_glu.view(gv[:, kk, :]))
    ov = w_out.h.rearrange("(k p) c -> p k c", p=128)
    for kk in range(8):
        load_cast(wo_bf[:, kk, :], w_out.view(ov[:, kk, :]))
    v1 = w_ff1.h.rearrange("(k p) c -> p k c", p=128)
    for kk in range(8):
        for cc in range(4):
            load_cast(w1_bf[:, kk, cc * 1024:(cc + 1) * 1024], w_ff1.view(v1[:, kk, cc * 1024:(cc + 1) * 1024]))
    v2 = w_ff2.h.rearrange("(k p) c -> p k c", p=128)
    for kk in range(32):
        load_cast(w2_bf[:, kk, :], w_ff2.view(v2[:, kk, :]))

    xv = xT.h.rearrange("(k p) t -> p k t", p=128)
    xov = xo.h.rearrange("(k p) t -> p k t", p=128)
    pi = 0
    for bi, (c0, _, n) in enumerate(CBLOCKS):
        k.dma(x_sb[:, :, :], xT.view(xv[:, :, c0:c0 + W]), q="sp")
        for h in range(4):
            k.dma(mix_bf[:, h, :], mixA[h * 128:(h + 1) * 128, c0:c0 + W], q="sp")
        for m in range(2):
            k.dma(mix_bf[:, 6 + m, :], mixF[m * 128:(m + 1) * 128, c0:c0 + W], q="sp")
        for m in range(2):
            k.dma(y_sb[m][0:64, :], Ysel[2 * m, :, c0:c0 + W], q="sp")
            k.dma(y_sb[m][64:128, :], Ysel[2 * m + 1, :, c0:c0 + W], q="sp")
            k.dma(du_sb[m][:, :], DUT[m * 128:(m + 1) * 128, c0:c0 + W], q="sp")
            k.tt("dve", y_sb[m][:, :], y_sb[m][:, :], du_sb[m][:, :], ALU.add)
            k.tt("dve", u1[m][:, :], y_sb[m][:, :], y_sb[m][:, :], ALU.mult)
            k.ts("dve", u1[m][:, :], u1[m][:, :], 0.044715, ALU.mult, 1.0, ALU.add)
            k.tt("dve", u1[m][:, :], u1[m][:, :], y_sb[m][:, :], ALU.mult)
            k.act(sg[m][:, :], u1[m][:, :], AF.Sigmoid, scale=1.5957691216057308)
            k.tt("dve", hg[m][:, :], y_sb[m][:, :], sg[m][:, :], ALU.mult)
            k.act(hgb[:, m, :], hg[m][:, :], AF.Identity)
        for m in range(2):
            for kt in range(2):
                k.mm(ps_g[:, 0:W], wg_bf[:, kt, m * 128:(m + 1) * 128], hgb[:, kt, :], kt == 0, kt == 1)
            k.act(sg[m][:, :], ps_g[:, 0:W], AF.Sigmoid, bias=bg_sb[:, m:m + 1], scale=1.0)
            k.tt("dve", mix_bf[:, 4 + m, :], hg[m][:, :], sg[m][:, :], ALU.mult)
        for dt in range(8):
            p_ = pm[pi % 4]; pi += 1
            for kt in range(8):
                k.mm(p_[:, 0:W], wo_bf[:, kt, dt * 128:(dt + 1) * 128], mix_bf[:, kt, :], kt == 0, kt == 7)
            k.stt("dve", x_sb[:, dt, :], p_[:, 0:W], mod_sb[:, 2 * 8 + dt, n:n + 1], x_sb[:, dt, :], ALU.mult, ALU.add)
        rmsnorm_mod(k, x_sb, W, n, mod_sb, 3, 4, h2_bf, xsq, ps_ss, rstd, tmp, ones_bf, cst)
        for ft in range(32):
            p_ = pm[pi % 4]; pi += 1
            for kt in range(8):
                k.mm(p_[:, 0:W], w1_bf[:, kt, ft * 128:(ft + 1) * 128], h2_bf[:, kt, :], kt == 0, kt == 7)
            r_ = rl[ft % 2]
            k.act(r_[:, :], p_[:, 0:W], AF.Relu)
            k.tt("dve", act_bf[:, ft, :], r_[:, :], r_[:, :], ALU.mult)
        for dt in range(8):
            p_ = pm[pi % 4]; pi += 1
            for kt in range(32):
                k.mm(p_[:, 0:W], w2_bf[:, kt, dt * 128:(dt + 1) * 128], act_bf[:, kt, :], kt == 0, kt == 31)
            k.stt("dve", x_sb[:, dt, :], p_[:, 0:W], mod_sb[:, 5 * 8 + dt, n:n + 1], x_sb[:, dt, :], ALU.mult, ALU.add)
        k.dma(xo.view(xov[:, :, c0:c0 + W]), x_sb[:, :, :], q="sp")
    return k.finish()


L_SEQ = 8192
CTX = 256
TOKC = 2304


def const_cmat():
    ones = np.ones((128, 128), np.float32)
    p = np.arange(128)
    blk = (p[:, None] // 64 == p[None, :] // 64).astype(np.float32)
    RT = np.zeros((128, 128), np.float32)
    for m in range(128):
        if m % 64 < 32:
            RT[m + 32, m] = -1.0
        else:
            RT[m - 32, m] = 1.0
    return np.stack([ones, blk, RT], axis=1).astype(NPBF)


def const_cs():
    CS = np.zeros((256, 512), np.float64)
    c = np.arange(64)[:, None]
    d = np.arange(64)[None, :]
    ang = 2 * np.pi * c * d / 64.0
    for g in range(4):
        CS[g * 64:(g + 1) * 64, g * 64:(g + 1) * 64] = np.cos(ang) / 8.0
        CS[g * 64:(g + 1) * 64, 256 + g * 64:256 + (g + 1) * 64] = np.sin(ang) / 8.0
    return CS.reshape(2, 128, 512).transpose(1, 0, 2).astype(NPBF)


def const_rope(j):
    t = np.arange(j * 2048, (j + 1) * 2048)
    row = (t // 64).astype(np.float32)
    col = (t % 64).astype(np.float32)
    inv = np.power(np.float32(10000.0), -np.arange(16, dtype=np.float32) / np.float32(16)).astype(np.float32)
    ang = np.concatenate([row[:, None] * inv, col[:, None] * inv], axis=-1).astype(np.float32)
    cos = np.cos(ang).astype(np.float32)
    sin = np.sin(ang).astype(np.float32)
    out = np.zeros((128, 2, TOKC), np.float32)
    pidx = (np.arange(128) % 64) % 32
    out[:, 0, :2048] = cos[:, pidx].T
    out[:, 1, :2048] = sin[:, pidx].T
    out[:, 0, 2048:] = 1.0
    return out


def fm(v, ntile):
    return np.ascontiguousarray(np.asarray(v).reshape(ntile, 128).T)


def const_tauAB():
    tau = np.arange(512)
    out = np.zeros((128, 2, 512), np.float32)
    out[:, 0, :] = (tau // 32)[None, :]
    out[:, 1, :] = (tau % 32)[None, :]
    return out


_DFT_CACHE = {}


def const_dft(j):
    if j in _DFT_CACHE:
        return _DFT_CACHE[j]
    L = 8192
    l = np.arange(L, dtype=np.int64).reshape(64, 128)
    k = (2048 * j + np.arange(2048, dtype=np.int64)).reshape(4, 512)
    kl = (l[None, :, :, None] * k[:, None, None, :]) % L
    ang = kl.astype(np.float64) * (2 * np.pi / L)
    sc = 1.0 / np.sqrt(L)
    out = np.empty((4, 64, 128, 1024), NPBF)
    out[..., :512] = (np.cos(ang) * sc).astype(NPBF)
    out[..., 512:] = (-np.sin(ang) * sc).astype(NPBF)
    _DFT_CACHE[j] = out
    return out


def const_dftc():
    L = 256
    l = np.arange(L, dtype=np.int64).reshape(2, 128)
    k = np.arange(L, dtype=np.int64)
    ang = ((l[:, :, None] * k[None, None, :]) % L).astype(np.float64) * (2 * np.pi / L)
    out = np.empty((2, 128, 512), NPBF)
    out[..., :256] = (np.cos(ang) / 16.0).astype(NPBF)
    out[..., 256:] = (-np.sin(ang) / 16.0).astype(NPBF)
    return out


def s5_layout(inp, li, j):
    s5p = np.zeros((128, 4, 3), np.float32)
    BT = np.zeros((64, 4, 2, 128), np.float32)
    CT = np.zeros((128, 4, 2, 64), np.float32)
    for d in range(2):
        for q in range(2):
            dq = d * 2 + q
            for gl in range(2):
                g = 4 * j + 2 * q + gl
                ps = slice(gl * 64, (gl + 1) * 64)
                s5p[ps, dq, 0] = inp["ssm_log_dt"][li, d, g]
                s5p[ps, dq, 1] = inp["ssm_a_re"][li, d, g]
                s5p[ps, dq, 2] = inp["ssm_a_im"][li, d, g]
                chs = slice((2 * q + gl) * 16, (2 * q + gl + 1) * 16)
                BT[chs, dq, 0, ps] = inp["ssm_b_re"][li, d, g].T
                BT[chs, dq, 1, ps] = inp["ssm_b_im"][li, d, g].T
                CT[ps, dq, 0, chs] = inp["ssm_c_re"][li, d, g].T
                CT[ps, dq, 1, chs] = inp["ssm_c_im"][li, d, g].T
    return s5p, BT, CT


def wfbd_layout(w_fnet_l):
    out = np.zeros((128, 2, 128), np.float32)
    for m in range(2):
        for gl in range(2):
            out[gl * 64:(gl + 1) * 64, m, gl * 64:(gl + 1) * 64] = w_fnet_l[2 * m + gl]
    return out


def const_krow(j):
    out = np.zeros((128, TOKC), np.float32)
    out[:, :2048] = ((2048 * j + np.arange(2048)) / 8192.0).astype(np.float32)[None, :]
    out[:, 2048:] = (np.arange(256) / 256.0).astype(np.float32)[None, :]
    return out


def const_lcol():
    out = np.zeros((128, 66), np.float32)
    p = np.arange(128)
    for lt in range(64):
        out[:, lt] = 128 * lt + p
    out[:, 64] = p
    out[:, 65] = 128 + p
    return out


_NC_CACHE = {}


def _prog(name, builder):
    if name not in _NC_CACHE:
        _NC_CACHE[name] = builder()
    return _NC_CACHE[name]


def _launch(name, builder, in_maps):
    nc = _prog(name, builder)
    res = run_bass_kernel_spmd(nc, in_maps, core_ids=list(range(8)))
    return [{k: np.asarray(v) for k, v in r.items()} for r in res.results]


def kernel(**inputs):
    inp = {k: np.asarray(v) for k, v in inputs.items()}
    x, c, ctx, c_ctx = inp["x"], inp["c"], inp["ctx"], inp["c_ctx"]
    depth = inp["w_in"].shape[0]
    f32 = np.float32
    cm = const_cmat(); csm = const_cs(); tab = const_tauAB(); lc = const_lcol()
    in_maps = []
    for core in range(8):
        b, j = core // 4, core % 4
        cp = np.stack([c[b], c_ctx], -1).astype(f32)
        in_maps.append({
            "cT": np.ascontiguousarray(cp.reshape(8, 128, 2).transpose(1, 0, 2)),
            "w_mod": inp["w_mod"][j], "bmodT": fm(inp["b_mod"][j], 48),
            "gn": np.concatenate([fm(inp["g_norm1"][j], 8), fm(inp["g_norm2"][j], 8)], axis=1),
        })
    modT = [r["modT"] for r in _launch("M", build_M, in_maps)]
    xT = []
    for core in range(8):
        b, j = core // 4, core % 4
        xT.append(np.ascontiguousarray(np.concatenate([x[b, j * 2048:(j + 1) * 2048].T, ctx[b].T], axis=1)))
    ropes = [const_rope(j) for j in range(4)]
    krows = [const_krow(j) for j in range(4)]
    for li in range(depth):
        lam_init = 0.8 - 0.6 * math.exp(-0.3 * li)
        in_maps = []
        for core in range(8):
            b, j = core // 4, core % 4
            in_maps.append({
                "xT": xT[core], "modT": modT[b * 4 + li], "w_in": inp["w_in"][li],
                "gqk": np.stack([np.tile(inp["g_qnorm"][li], 2), np.tile(inp["g_knorm"][li], 2)], -1).astype(f32),
                "rope": ropes[j], "cmat": cm, "cs": csm, "dcol": fm(inp["ssm_d"][li], 2),
            })
        outA = _launch("A", build_A, in_maps)
        in_maps = []
        for core in range(8):
            b, j = core // 4, core % 4
            grp = [outA[b * 4 + s] for s in range(4)]
            s5p, BT, CT = s5_layout(inp, li, j)
            in_maps.append({
                "QT": outA[core]["QT"], "KTall": np.stack([g["KT"] for g in grp]),
                "Vall": np.stack([g["V"] for g in grp]),
                "lamv": np.stack([inp["lam_q1"][li], inp["lam_k1"][li], inp["lam_q2"][li], inp["lam_k2"][li]],
                                 -1).astype(f32),
                "lconst": np.tile(np.array([[lam_init, 1 - lam_init]], f32), (128, 1)),
                "gsub": inp["g_subln"][li].reshape(128, 1).astype(f32), "cmat": cm,
                "UTall": np.stack([g["UT"][64 * j:64 * (j + 1)] for g in grp]),
                "s5p": s5p, "BT": BT, "CT": CT, "tauAB": tab,
                "PQall": np.stack([g["PQ"] for g in grp]), "krow": krows[j], "lcol": lc,
                "wfbd": wfbd_layout(inp["w_fnet"][li]),
            })
        outB = _launch("B", build_B, in_maps)
        in_maps = []
        for core in range(8):
            b, j = core // 4, core % 4
            Ys = [outB[b * 4 + s]["Y"] for s in range(4)]
            ysel = np.stack([np.concatenate([y[:, 2048 * j:2048 * (j + 1)], y[:, 8192:]], 1) for y in Ys])
            in_maps.append({
                "xT": xT[core], "modT": modT[b * 4 + li], "mixA": outB[core]["mixA"], "mixF": outB[core]["mixF"],
                "Ysel": ysel, "DUT": outA[core]["DUT"], "w_glu": inp["w_glu"][li], "bglu": fm(inp["b_glu"][li], 2),
                "w_out": inp["w_out"][li], "w_ff1": inp["w_ff1"][li], "w_ff2": inp["w_ff2"][li], "cmat": cm,
            })
        outC = _launch("C", build_C, in_maps)
        xT = [r["xTo"] for r in outC]
    out = np.empty(x.shape, f32)
    for core in range(8):
        b, j = core // 4, core % 4
        out[b, j * 2048:(j + 1) * 2048, :] = xT[core][:, :2048].T
    return out
```

```python
import math
import contextlib
import numpy as np
import ml_dtypes
import concourse.bass as bass
import concourse.mybir as mybir
from concourse.bass_utils import run_bass_kernel_spmd

F32 = mybir.dt.float32
BF16 = mybir.dt.bfloat16
I32 = mybir.dt.int32
AF = mybir.ActivationFunctionType
ALU = mybir.AluOpType
NPBF = ml_dtypes.bfloat16


class Buf:
    __slots__ = ("name", "w", "r", "psum")

    def __init__(self, name, psum=False):
        self.name = name
        self.w = None
        self.r = {}
        self.psum = psum


class V:
    __slots__ = ("ap", "bufs")

    def __init__(self, ap, bufs):
        self.ap = ap
        self.bufs = bufs


class T:
    def __init__(self, handle, name, shape, subaxis=None, psum=False):
        self.h = handle
        self.name = name
        self.shape = list(shape)
        self.subaxis = subaxis
        n = shape[subaxis] if subaxis is not None else 1
        self.bufs = [Buf(f"{name}.{i}", psum) for i in range(n)]

    def __getitem__(self, idx):
        if not isinstance(idx, tuple):
            idx = (idx,)
        ap = self.h[idx]
        bufs = self.bufs
        if self.subaxis is not None and len(idx) > self.subaxis:
            ix = idx[self.subaxis]
            if isinstance(ix, int):
                bufs = [self.bufs[ix]]
            elif isinstance(ix, slice):
                rng = range(*ix.indices(self.shape[self.subaxis]))
                bufs = [self.bufs[i] for i in rng]
        return V(ap, bufs)

    def view(self, ap, sub=None):
        bufs = self.bufs if sub is None else [self.bufs[i] for i in sub]
        return V(ap, bufs)


class Sched:
    ENGS = ("pe", "act", "dve", "pool", "sp")

    def __init__(self, n_dma_sems=8):
        self.prog = {e: [] for e in self.ENGS}
        self.cnt = {}
        self.seen = {e: {} for e in self.ENGS}
        self.n_dma_sems = n_dma_sems
        self.dma_rr = {e: 0 for e in self.ENGS}
        self.ninst = 0

    def semkeys(self):
        keys = list(self.ENGS)
        for e in ("sp", "act", "pool"):
            for i in range(self.n_dma_sems):
                keys.append(f"d_{e}{i}")
        return keys

    def _wait(self, e, dep):
        k, n = dep
        if self.seen[e].get(k, 0) >= n:
            return
        self.seen[e][k] = n
        self.prog[e].append(("wait", k, n))

    def _deps(self, e, reads, writes):
        deps = []
        for b in reads:
            if b.w is not None:
                deps.append(b.w)
            if b.psum:
                for k, n in b.r.items():
                    if k != e:
                        deps.append((k, n))
        for b in writes:
            if b.w is not None:
                deps.append(b.w)
            for k, n in b.r.items():
                deps.append((k, n))
        for d in deps:
            if d[0] == "pe" and e == "pe":
                continue
            self._wait(e, d)

    def _mark(self, key, n, reads, writes):
        for b in reads:
            if b.r.get(key, 0) < n:
                b.r[key] = n
        for b in writes:
            b.w = (key, n)
            b.r = {}

    def op(self, e, fn, reads=(), writes=(), sig=True):
        self._deps(e, reads, writes)
        n = self.cnt.get(e, 0) + 1
        if sig:
            self.cnt[e] = n
            self.prog[e].append(("op", fn, e, 1))
        else:
            self.prog[e].append(("op", fn, None, 0))
        self._mark(e, n, reads, writes)
        self.ninst += 1

    def dma(self, e, fn, reads=(), writes=()):
        i = self.dma_rr[e]
        self.dma_rr[e] = (i + 1) % self.n_dma_sems
        key = f"d_{e}{i}"
        uses = self.cnt.get(key, 0)
        if uses > 0:
            self._wait(e, (key, uses))
        self._deps(e, reads, writes)
        n = uses + 16
        self.cnt[key] = n
        self.prog[e].append(("op", fn, key, 16))
        self._mark(key, n, reads, writes)
        self.ninst += 1

    def wait_all(self, e):
        for k, n in list(self.cnt.items()):
            self._wait(e, (k, n))

    def barrier(self):
        for e in self.ENGS:
            self.wait_all(e)

    def emit(self, block, sems):
        engobj = {"pe": "tensor", "act": "scalar", "dve": "vector", "pool": "gpsimd", "sp": "sync"}

        def mk(e):
            def body(eng):
                for item in self.prog[e]:
                    if item[0] == "wait":
                        eng.wait_ge(sems[item[1]], item[2])
                    else:
                        _, fn, key, inc = item
                        if key is None:
                            fn(eng)
                        else:
                            fn(eng).then_inc(sems[key], inc)
            return body

        for e in self.ENGS:
            getattr(block, engobj[e])(mk(e))


class KB:
    def __init__(self):
        self.nc = bass.Bass("TRN2", target_bir_lowering=False)
        self.S = Sched()
        self.st = contextlib.ExitStack()
        self.dq = 0
        self.sb_off = (self.nc.sbuf_base + 63) // 64 * 64
        self.sb_top = self.nc.sbuf_top
        self.nalloc = 0

    @contextlib.contextmanager
    def scope(self):
        mark = self.sb_off
        yield
        self.S.barrier()
        self.sb_off = mark

    def dram(self, name, shape, dt, kind, subaxis=None):
        h = self.nc.dram_tensor(name, list(shape), dt, kind=kind).ap()
        return T(h, name, shape, subaxis)

    def sb(self, name, shape, dt, subaxis=None):
        nbytes = int(np.prod(shape[1:])) * mybir.dt.size(dt)
        nbytes = (nbytes + 63) // 64 * 64
        assert self.sb_off + nbytes <= self.sb_top, f"SBUF overflow allocating {name}: {self.sb_off}+{nbytes}>{self.sb_top}"
        self.nalloc += 1
        h = self.nc.alloc_sbuf_tensor_at(f"{name}_{self.nalloc}", list(shape), dt, offset=self.sb_off)
        self.sb_off += nbytes
        return T(h, name, shape, subaxis)

    def ps(self, name, shape=None, dt=F32):
        shape = [128, 512] if dt == F32 else [128, 1024]
        h = self.st.enter_context(self.nc.psum_tensor(name, list(shape), dt))
        return T(h, name, shape, None, psum=True)

    @staticmethod
    def _rb(*vs):
        out = []
        for v in vs:
            if isinstance(v, V):
                out.extend(v.bufs)
        return out

    @staticmethod
    def _a(v):
        return v.ap if isinstance(v, V) else v

    def dma(self, out, in_, q=None):
        if q is None:
            q = "sp"
            self.dq += 1
        self.S.dma(q, lambda e: e.dma_start(out=out.ap, in_=in_.ap), reads=in_.bufs, writes=out.bufs)

    def mm(self, out, lhsT, rhs, start, stop, sig=None):
        self.S.op("pe", lambda e: e.matmul(out.ap, lhsT=lhsT.ap, rhs=rhs.ap, start=start, stop=stop),
                  reads=self._rb(lhsT, rhs), writes=out.bufs, sig=(stop if sig is None else sig))

    def act(self, out, in_, func, bias=None, scale=None, eng="act"):
        kw = {}
        if bias is not None:
            kw["bias"] = self._a(bias)
        if scale is not None:
            kw["scale"] = self._a(scale)
        self.S.op(eng, lambda e: e.activation(out=out.ap, in_=in_.ap, func=func, **kw),
                  reads=self._rb(in_, bias, scale), writes=out.bufs)

    def copy(self, eng, out, in_):
        if eng == "act":
            self.S.op(eng, lambda e: e.copy(out=out.ap, in_=in_.ap), reads=in_.bufs, writes=out.bufs)
        else:
            self.S.op(eng, lambda e: e.tensor_copy(out=out.ap, in_=in_.ap), reads=in_.bufs, writes=out.bufs)

    def tt(self, eng, out, in0, in1, op):
        self.S.op(eng, lambda e: e.tensor_tensor(out=out.ap, in0=in0.ap, in1=in1.ap, op=op),
                  reads=self._rb(in0, in1), writes=out.bufs)

    def ts(self, eng, out, in0, s1, op0, s2=None, op1=None):
        if op1 is None:
            self.S.op(eng, lambda e: e.tensor_single_scalar(out=out.ap, in_=in0.ap, scalar=self._a(s1), op=op0),
                      reads=self._rb(in0, s1), writes=out.bufs)
        else:
            self.S.op(eng, lambda e: e.tensor_scalar(out=out.ap, in0=in0.ap, scalar1=self._a(s1), scalar2=self._a(s2),
                                                     op0=op0, op1=op1),
                      reads=self._rb(in0, s1, s2), writes=out.bufs)

    def stt(self, eng, out, in0, scalar, in1, op0, op1):
        self.S.op(eng, lambda e: e.scalar_tensor_tensor(out=out.ap, in0=in0.ap, scalar=self._a(scalar), in1=in1.ap,
                                                        op0=op0, op1=op1),
                  reads=self._rb(in0, scalar, in1), writes=out.bufs)

    def recip(self, out, in_):
        self.S.op("dve", lambda e: e.reciprocal(out=out.ap, in_=in_.ap), reads=in_.bufs, writes=out.bufs)

    def memset(self, eng, out, val):
        self.S.op(eng, lambda e: e.memset(out.ap, val), writes=out.bufs)

    def scan(self, out, d0, d1, init):
        self.S.op("dve", lambda e: e.tensor_tensor_scan(out=out.ap, data0=d0.ap, data1=d1.ap, initial=self._a(init),
                                                        op0=ALU.mult, op1=ALU.add),
                  reads=self._rb(d0, d1, init), writes=out.bufs)

    def finish(self):
        self.S.wait_all("sp")
        sems = {k: self.st.enter_context(self.nc.semaphore(k)) for k in self.S.semkeys()}
        block = self.st.enter_context(self.nc.Block())
        self.S.emit(block, sems)
        self.st.close()
        return self.nc


def rev(v):
    return V(v.ap[:, ::-1], v.bufs)


TOK = 2304
BLOCKS = [(0, 512, 0), (512, 512, 0), (1024, 512, 0), (1536, 512, 0), (2048, 256, 1)]
EPS = 1e-6


def build_M():
    k = KB()
    cT = k.dram("cT", [128, 8, 2], F32, "ExternalInput")
    w_mod = k.dram("w_mod", [1024, 6144], F32, "ExternalInput")
    bmodT = k.dram("bmodT", [128, 48], F32, "ExternalInput")
    gn = k.dram("gn", [128, 16], F32, "ExternalInput")
    modT = k.dram("modT", [128, 48, 2], F32, "ExternalOutput")

    c_sb = k.sb("c_sb", [128, 8, 2], F32)
    cact = k.sb("cact", [128, 8, 2], F32)
    bm_sb = k.sb("bm_sb", [128, 48], F32)
    gn_sb = k.sb("gn_sb", [128, 16], F32)
    mod_sb = k.sb("mod_sb", [128, 48, 2], F32)
    wch = [k.sb(f"wch{i}", [128, 8, 512], F32) for i in range(3)]
    pss = [k.ps(f"psm{i}", [128, 2]) for i in range(4)]

    k.dma(c_sb[:, :, :], cT[:, :, :], q="sp")
    k.dma(bm_sb[:, :], bmodT[:, :], q="sp")
    k.dma(gn_sb[:, :], gn[:, :], q="sp")
    k.act(cact[:, :, :], c_sb[:, :, :], AF.Silu)
    wv = w_mod.h.rearrange("(k p) c -> p k c", p=128)
    for c in range(12):
        w = wch[c % 3]
        for kk in range(8):
            k.dma(w[:, kk, :], w_mod.view(wv[:, kk, c * 512:(c + 1) * 512]), q=("sp", "act")[kk % 2])
        for f in range(4):
            ft = c * 4 + f
            ps = pss[ft % 4]
            for kk in range(8):
                k.mm(ps[:, 0:2], w[:, kk, f * 128:(f + 1) * 128], cact[:, kk, :], kk == 0, kk == 7)
            k.ts("dve", mod_sb[:, ft, :], ps[:, 0:2], bm_sb[:, ft:ft + 1], ALU.add)
    for s, g0 in ((1, 0), (4, 8)):
        for n in range(2):
            k.stt("dve", mod_sb[:, s * 8:(s + 1) * 8, n], mod_sb[:, s * 8:(s + 1) * 8, n], 1.0,
                  gn_sb[:, g0:g0 + 8], ALU.add, ALU.mult)
    k.dma(modT[:, :, :], mod_sb[:, :, :], q="sp")
    return k.finish()


def rmsnorm_mod(k, x_sb, W, n, mod_sb, s_sh, s_gm, h_bf, xsq, ps_ss, rstd, tmp, ones_bf, cst):
    for kk in range(8):
        k.act(xsq[:, kk, 0:W], x_sb[:, kk, 0:W], AF.Square)
    for kk in range(8):
        k.mm(ps_ss[:, 0:W], ones_bf, xsq[:, kk, 0:W], kk == 0, kk == 7)
    k.act(rstd[:, 0:W], ps_ss[:, 0:W], AF.Sqrt, bias=cst[:, 0:1], scale=1.0 / 1024)
    k.recip(rstd[:, 0:W], rstd[:, 0:W])
    for kk in range(8):
        t = tmp[kk % 2]
        k.stt("dve", t[:, 0:W], x_sb[:, kk, 0:W], mod_sb[:, s_gm * 8 + kk, n:n + 1], rstd[:, 0:W], ALU.mult, ALU.mult)
        k.act(h_bf[:, kk, 0:W], t[:, 0:W], AF.Identity, bias=mod_sb[:, s_sh * 8 + kk, n:n + 1], scale=1.0)


def build_A(nb=5, upto=9):
    k = KB()
    xT = k.dram("xT", [1024, TOK], F32, "ExternalInput")
    modT = k.dram("modT", [128, 48, 2], F32, "ExternalInput")
    w_in = k.dram("w_in", [1024, 2048], F32, "ExternalInput")
    gqk = k.dram("gqk", [128, 2], F32, "ExternalInput")
    rope = k.dram("rope", [128, 2, TOK], F32, "ExternalInput")
    cmat = k.dram("cmat", [128, 3, 128], BF16, "ExternalInput")
    cs = k.dram("cs", [128, 2, 512], BF16, "ExternalInput")
    dcol = k.dram("dcol", [128, 2], F32, "ExternalInput")
    QT = k.dram("QT", [4, 128, TOK], BF16, "ExternalOutput")
    KT = k.dram("KT", [4, 128, TOK], BF16, "ExternalOutput")
    Vd = k.dram("V", [TOK, 512], BF16, "ExternalOutput")
    UT = k.dram("UT", [256, TOK], BF16, "ExternalOutput")
    PQ = k.dram("PQ", [TOK, 512], BF16, "ExternalOutput")
    DUT = k.dram("DUT", [256, TOK], F32, "ExternalOutput")

    w_bf = k.sb("w_bf", [128, 8, 2048], BF16)
    stage = [k.sb(f"stage{i}", [128, 8, 256], F32) for i in range(2)]
    mod_sb = k.sb("mod_sb", [128, 48, 2], F32)
    gqk_sb = k.sb("gqk_sb", [128, 2], F32)
    dcol_sb = k.sb("dcol_sb", [128, 2], F32)
    cmat_sb = k.sb("cmat_sb", [128, 3, 128], BF16)
    cs_sb = k.sb("cs_sb", [128, 2, 512], BF16)
    cst = k.sb("cst", [128, 2], F32)
    x_sb = [k.sb(f"x_sb{i}", [128, 8, 512], F32) for i in range(2)]
    rope_sb = [k.sb(f"rope_sb{i}", [128, 2, 512], F32) for i in range(2)]
    xsq = k.sb("xsq", [128, 8, 512], BF16)
    h_bf = [k.sb(f"h_bf{i}", [128, 8, 512], BF16) for i in range(2)]
    rstd = k.sb("rstd", [128, 512], F32)
    tmp = [k.sb(f"tmp{i}", [128, 512], F32) for i in range(2)]
    qg = [k.sb(f"qg{i}", [128, 512], BF16) for i in range(2)]
    sq = [k.sb(f"sq{i}", [128, 512], BF16) for i in range(2)]
    rs = [k.sb(f"rs{i}", [128, 512], F32) for i in range(2)]
    t1 = [k.sb(f"t1{i}", [128, 512], F32) for i in range(2)]
    t2 = [k.sb(f"t2{i}", [128, 512], F32) for i in range(2)]
    oq = [k.sb(f"oq{i}", [128, 512], BF16) for i in range(3)]
    vtok = [k.sb(f"vtok{i}", [128, 512], BF16) for i in range(2)]
    ftl = k.sb("ftl", [128, 2, 512], BF16)
    pqs = [k.sb(f"pqs{i}", [128, 512], BF16) for i in range(2)]
    ubf = [k.sb(f"ubf{i}", [128, 512], BF16) for i in range(2)]
    duf = [k.sb(f"duf{i}", [128, 512], F32) for i in range(2)]

    ps_ss = k.ps("ps_ss", [128, 512])
    ps_main = [k.ps(f"ps_main{i}", [128, 512]) for i in range(2)]
    ps_ss2 = k.ps("ps_ss2", [128, 512])
    ps_rot = k.ps("ps_rot", [128, 512])
    ps_v = k.ps("ps_v", [128, 512])
    ps_pq = k.ps("ps_pq", [128, 512])

    k.memset("dve", cst[:, 0:1], EPS)
    k.dma(mod_sb[:, :, :], modT[:, :, :], q="sp")
    k.dma(gqk_sb[:, :], gqk[:, :], q="sp")
    k.dma(dcol_sb[:, :], dcol[:, :], q="sp")
    k.dma(cmat_sb[:, :, :], cmat[:, :, :], q="sp")
    k.dma(cs_sb[:, :, :], cs[:, :, :], q="sp")
    wv = w_in.h.rearrange("(k p) c -> p k c", p=128)
    for c in range(8):
        sg = stage[c % 2]
        for kk in range(8):
            k.dma(sg[:, kk, :], w_in.view(wv[:, kk, c * 256:(c + 1) * 256]), q=("sp", "act")[kk % 2])
        k.copy(("dve", "act")[c % 2], w_bf[:, :, c * 256:(c + 1) * 256], sg[:, :, :])
    ones_bf = cmat_sb[:, 0, :]
    blk64 = cmat_sb[:, 1, :]
    RTm = cmat_sb[:, 2, :]
    xv = xT.h.rearrange("(k p) t -> p k t", p=128)
    mmi = 0
    qi = 0
    for bi, (c0, W, n) in enumerate(BLOCKS[:nb]):
        xs = x_sb[bi % 2]
        rp = rope_sb[bi % 2]
        hb = h_bf[bi % 2]
        for kk in range(8):
            k.dma(xs[:, kk, 0:W], xT.view(xv[:, kk, c0:c0 + W]), q="sp")
        k.dma(rp[:, :, 0:W], rope[:, :, c0:c0 + W], q="sp")
        if upto < 1: continue
        rmsnorm_mod(k, xs, W, n, mod_sb, 0, 1, hb, xsq, ps_ss, rstd, tmp, ones_bf, cst)
        if upto < 2: continue
        for idx in range(8):
            isq = idx < 4
            col = idx * 128
            pm = ps_main[mmi % 2]; mmi += 1
            for kk in range(8):
                k.mm(pm[:, 0:W], w_bf[:, kk, col:col + 128], hb[:, kk, 0:W], kk == 0, kk == 7)
            a = qi % 2; qi += 1
            gcol = gqk_sb[:, 0:1] if isq else gqk_sb[:, 1:2]
            k.act(qg[a][:, 0:W], pm[:, 0:W], AF.Identity, scale=gcol)
            k.act(sq[a][:, 0:W], pm[:, 0:W], AF.Square)
            k.mm(ps_ss2[:, 0:W], blk64, sq[a][:, 0:W], True, True)
            k.mm(ps_rot[:, 0:W], RTm, qg[a][:, 0:W], True, True)
            k.act(rs[a][:, 0:W], ps_ss2[:, 0:W], AF.Sqrt, bias=cst[:, 0:1], scale=1.0 / 64)
            k.recip(rs[a][:, 0:W], rs[a][:, 0:W])
            k.tt("dve", t1[a][:, 0:W], qg[a][:, 0:W], rp[:, 0, 0:W], ALU.mult)
            k.tt("dve", t2[a][:, 0:W], ps_rot[:, 0:W], rp[:, 1, 0:W], ALU.mult)
            k.tt("dve", t1[a][:, 0:W], t1[a][:, 0:W], t2[a][:, 0:W], ALU.add)
            o = oq[idx % 3]
            k.stt("dve", o[:, 0:W], t1[a][:, 0:W], 0.125 if isq else 1.0, rs[a][:, 0:W], ALU.mult, ALU.mult)
            dst = QT if isq else KT
            k.dma(dst[idx % 4, :, c0:c0 + W], o[:, 0:W], q="sp")
        if upto < 3.0: continue
        for m in range(2):
            pm = ps_main[mmi % 2]; mmi += 1
            col = 1792 + m * 128
            for kk in range(8):
                k.mm(pm[:, 0:W], w_bf[:, kk, col:col + 128], hb[:, kk, 0:W], kk == 0, kk == 7)
            k.act(ftl[:, m, 0:W], pm[:, 0:W], AF.Identity)
        if upto < 3.2: continue
        for m in range(2):
            pm = ps_main[mmi % 2]; mmi += 1
            col = 1536 + m * 128
            for kk in range(8):
                k.mm(pm[:, 0:W], w_bf[:, kk, col:col + 128], hb[:, kk, 0:W], kk == 0, kk == 7)
            k.act(ubf[m][:, 0:W], pm[:, 0:W], AF.Identity)
            if upto < 3.4: continue
            k.ts("dve", duf[m][:, 0:W], pm[:, 0:W], dcol_sb[:, m:m + 1], ALU.mult)
            if upto < 3.6: continue
            k.dma(UT[m * 128:(m + 1) * 128, c0:c0 + W], ubf[m][:, 0:W], q="sp")
            k.dma(DUT[m * 128:(m + 1) * 128, c0:c0 + W], duf[m][:, 0:W], q="sp")
        if upto < 4: continue
        for tt in range(W // 128):
            tsl = slice(tt * 128, (tt + 1) * 128)
            for kk in range(8):
                k.mm(ps_v[:, :], hb[:, kk, tsl], w_bf[:, kk, 1024:1536], kk == 0, kk == 7)
            vt = vtok[tt % 2]
            k.act(vt[:, :], ps_v[:, :], AF.Identity)
            k.dma(Vd[c0 + tt * 128:c0 + (tt + 1) * 128, :], vt[:, :], q="sp")
            for m in range(2):
                k.mm(ps_pq[:, :], ftl[:, m, tsl], cs_sb[:, m, :], m == 0, m == 1)
            pt = pqs[tt % 2]
            k.copy("dve", pt[:, :], ps_pq[:, :])
            k.dma(PQ[c0 + tt * 128:c0 + (tt + 1) * 128, :], pt[:, :], q="sp")
    return k.finish()


KEYT = [(s, t) for s in range(4) for t in range(16)] + [(0, 16), (0, 17)]
NY = 8448


def build_B(do_attn=True, do_s5=True, do_fft=True, nheads=4, nqb=5):
    k = KB()
    QT = k.dram("QT", [4, 128, TOK], BF16, "ExternalInput")
    KTall = k.dram("KTall", [4, 4, 128, TOK], BF16, "ExternalInput")
    Vall = k.dram("Vall", [4, TOK, 512], BF16, "ExternalInput")
    lamv = k.dram("lamv", [64, 4], F32, "ExternalInput")
    lconst = k.dram("lconst", [128, 2], F32, "ExternalInput")
    gsub = k.dram("gsub", [128, 1], F32, "ExternalInput")
    cmat = k.dram("cmat", [128, 3, 128], BF16, "ExternalInput")
    mixA = k.dram("mixA", [512, TOK], BF16, "ExternalOutput")
    UTall = k.dram("UTall", [4, 64, TOK], BF16, "ExternalInput")
    s5p = k.dram("s5p", [128, 4, 3], F32, "ExternalInput")
    BTd = k.dram("BT", [64, 4, 2, 128], F32, "ExternalInput")
    CTd = k.dram("CT", [128, 4, 2, 64], F32, "ExternalInput")
    tauAB = k.dram("tauAB", [128, 2, 512], F32, "ExternalInput")
    Yd = k.dram("Y", [64, NY], F32, "ExternalOutput")
    PQall = k.dram("PQall", [4, TOK, 512], BF16, "ExternalInput")
    krow = k.dram("krow", [128, TOK], F32, "ExternalInput")
    lcol = k.dram("lcol", [128, 66], F32, "ExternalInput")
    wfbd = k.dram("wfbd", [128, 2, 128], F32, "ExternalInput")
    mixF = k.dram("mixF", [256, TOK], BF16, "ExternalOutput")

    cmat_sb = k.sb("cmat_sb", [128, 3, 128], BF16)
    cst = k.sb("cst", [128, 4], F32)
    k.dma(cmat_sb[:, :, :], cmat[:, :, :])
    k.memset("dve", cst[:, 0:1], EPS)
    k.memset("dve", cst[:, 1:2], 1.0)
    ones_bf = cmat_sb[:, 0, :]
    pss = [k.ps(f"ps{i}") for i in range(8)]

    if do_attn:
      with k.scope():
          lam_sb = k.sb("lam_sb", [64, 4], F32)
          lprod = k.sb("lprod", [64, 2], F32)
          lc_sb = k.sb("lc_sb", [128, 2], F32)
          gs_sb = k.sb("gs_sb", [128, 1], F32)
          ones_f = k.sb("ones_f", [64, 128], F32)
          lam_e = k.sb("lam_e", [128, 2], F32)
          neglam = k.sb("neglam", [128, 1], F32)
          gfin = k.sb("gfin", [128, 1], F32)
          k.dma(lam_sb[:, :], lamv[:, :])
          k.dma(lc_sb[:, :], lconst[:, :])
          k.dma(gs_sb[:, :], gsub[:, :])
          k.memset("dve", ones_f[:, :], 1.0)
          k.tt("dve", lprod[:, 0:1], lam_sb[:, 0:1], lam_sb[:, 1:2], ALU.mult)
          k.tt("dve", lprod[:, 1:2], lam_sb[:, 2:3], lam_sb[:, 3:4], ALU.mult)
          k.mm(pss[0][:, 0:2], ones_f[:, :], lprod[:, :], True, True)
          k.act(lam_e[:, :], pss[0][:, 0:2], AF.Exp)
          k.tt("dve", neglam[:, :], lam_e[:, 1:2], lam_e[:, 0:1], ALU.subtract)
          k.tt("dve", neglam[:, :], neglam[:, :], lc_sb[:, 0:1], ALU.subtract)
          k.tt("dve", gfin[:, :], gs_sb[:, :], lc_sb[:, 1:2], ALU.mult)

          KT_sb = [k.sb(f"KT_sb{i}", [128, 4, TOK], BF16) for i in range(2)]
          V_sb = [k.sb(f"V_sb{i}", [128, 4, 18, 128], BF16) for i in range(2)]
          QT_sb = [k.sb(f"QT_sb{i}", [128, TOK], BF16) for i in range(2)]
          PT = [[k.sb(f"PT{m}{i}", [128, 512], BF16) for i in range(3)] for m in range(2)]
          rr = [k.sb(f"rr{m}", [128, 512], F32) for m in range(2)]
          racc = [k.sb(f"racc{m}", [128, 512], F32) for m in range(3)]
          ones_f32 = k.sb("ones_f32", [128, 128], F32)
          k.memset("dve", ones_f32[:, :], 1.0)
          o0 = k.sb("o0", [128, 512], F32)
          o1 = k.sb("o1", [128, 512], F32)
          osq = k.sb("osq", [128, 512], BF16)
          ors = k.sb("ors", [128, 512], F32)
          oout = [k.sb(f"oout{i}", [128, 512], BF16) for i in range(2)]
          ps_s = [[pss[0], pss[1]], [pss[2], pss[3]]]
          ps_o = [pss[4], pss[5]]
          ps_r = [pss[6], pss[7]]
          it = 0
          for h in range(nheads):
              kt = KT_sb[h % 2]; vs = V_sb[h % 2]; qs = QT_sb[h % 2]
              for s in range(4):
                  k.dma(kt[:, s, :], KTall[s, h, :, :], q="sp")
                  vv = Vall.h[s].rearrange("(t p) c -> p t c", p=128)
                  for half in range(2):
                      k.dma(vs[:, s, half * 9:(half + 1) * 9, :],
                            Vall.view(vv[:, half * 9:(half + 1) * 9, h * 128:(h + 1) * 128]), q="sp")
              k.dma(qs[:, :], QT[h, :, :], q="sp")
              seq = []
              for qb, (c0, W, n) in enumerate(BLOCKS[:nqb]):
                  keys = KEYT if n == 0 else KEYT[64:]
                  for ki, (s, t) in enumerate(keys):
                      seq.append((qb, c0, W, s, t, ki == 0, ki == len(keys) - 1, ki))

              def issue_qk(i):
                  qb, c0, W, s, t, first, last, ki_ = seq[i]
                  for m in range(2):
                      pS = ps_s[m][(it0 + i) % 2]
                      k.mm(pS[:, 0:W], kt[m * 64:(m + 1) * 64, s, t * 128:(t + 1) * 128],
                           qs[m * 64:(m + 1) * 64, c0:c0 + W], True, True)

              it0 = it
              issue_qk(0)
              for i in range(len(seq)):
                  qb, c0, W, s, t, first, last, ki_ = seq[i]
                  if i + 1 < len(seq):
                      issue_qk(i + 1)
                  for m in range(2):
                      pS = ps_s[m][(it0 + i) % 2]
                      pt = PT[m][(it0 + i) % 3]
                      k.act(pt[:, 0:W], pS[:, 0:W], AF.Exp)
                      k.mm(ps_o[m][:, 0:W], vs[:, s, t, :], pt[:, 0:W], first, last)
                      if m == 0:
                          re_, ra_, init_ = "dve", racc[0], ki_ == 0
                      elif ki_ % 2 == 0:
                          re_, ra_, init_ = "dve", racc[1], ki_ == 0
                      else:
                          re_, ra_, init_ = "pool", racc[2], ki_ == 1
                      if init_:
                          k.copy(re_, ra_[:, 0:W], pt[:, 0:W])
                      else:
                          k.tt(re_, ra_[:, 0:W], ra_[:, 0:W], pt[:, 0:W], ALU.add)
                  if not last:
                      continue
                  k.mm(ps_r[0][:, 0:W], ones_f32[:, :], racc[0][:, 0:W], True, True)
                  k.mm(ps_r[1][:, 0:W], ones_f32[:, :], racc[1][:, 0:W], True, False)
                  k.mm(ps_r[1][:, 0:W], ones_f32[:, :], racc[2][:, 0:W], False, True)
                  for m in range(2):
                      k.recip(rr[m][:, 0:W], ps_r[m][:, 0:W])
                  k.tt("dve", o0[:, 0:W], ps_o[0][:, 0:W], rr[0][:, 0:W], ALU.mult)
                  k.tt("dve", o1[:, 0:W], ps_o[1][:, 0:W], rr[1][:, 0:W], ALU.mult)
                  k.stt("dve", o0[:, 0:W], o1[:, 0:W], neglam[:, 0:1], o0[:, 0:W], ALU.mult, ALU.add)
                  k.act(osq[:, 0:W], o0[:, 0:W], AF.Square)
                  pe_ = ps_r[0]
                  k.mm(pe_[:, 0:W], ones_bf, osq[:, 0:W], True, True)
                  k.act(ors[:, 0:W], pe_[:, 0:W], AF.Sqrt, bias=cst[:, 0:1], scale=1.0 / 128)
                  k.recip(ors[:, 0:W], ors[:, 0:W])
                  oo = oout[(h * 5 + qb) % 2]
                  k.stt("dve", oo[:, 0:W], o0[:, 0:W], gfin[:, 0:1], ors[:, 0:W], ALU.mult, ALU.mult)
                  k.dma(mixA[h * 128:(h + 1) * 128, c0:c0 + W], oo[:, 0:W], q="sp")
              it += len(seq)

    if do_fft:
      with k.scope():
          TWO_PI = 2.0 * np.pi
          PQ_sb = k.sb("PQ_sb", [128, 4, 18, 512], BF16)
          for s in range(4):
              pv = PQall.h[s].rearrange("(t p) c -> p t c", p=128)
              for half in range(2):
                  k.dma(PQ_sb[:, s, half * 9:(half + 1) * 9, :], PQall.view(pv[:, half * 9:(half + 1) * 9, :]), q="sp")
          wf_f = k.sb("wf_f", [128, 2, 128], F32)
          wf_b = k.sb("wf_b", [128, 2, 128], BF16)
          k.dma(wf_f[:, :, :], wfbd[:, :, :])
          k.copy("dve", wf_b[:, :, :], wf_f[:, :, :])
          kr_sb = k.sb("kr_sb", [128, TOK], F32)
          lc_sb2 = k.sb("lc_sb2", [128, 66], F32)
          hp = k.sb("hp", [128, 1], F32)
          k.dma(kr_sb[:, :], krow[:, :])
          k.dma(lc_sb2[:, :], lcol[:, :])
          k.memset("dve", hp[:, :], np.pi / 2)
          gt = [k.sb(f"gt{i}", [128, 512], F32) for i in range(2)]
          gi = [k.sb(f"gi{i}", [128, 512], I32) for i in range(2)]
          gr = [k.sb(f"gr{i}", [128, 512], F32) for i in range(2)]
          ga = [k.sb(f"ga{i}", [128, 512], F32) for i in range(2)]
          tcos = [k.sb(f"tcos{i}", [128, 512], BF16) for i in range(3)]
          tsin = [k.sb(f"tsin{i}", [128, 512], BF16) for i in range(3)]
          zbf = [k.sb(f"zbf{i}", [128, 512], BF16) for i in range(2)]
          fo = [k.sb(f"fo{i}", [128, 512], BF16) for i in range(2)]
          gi_ = 0
          zi = 0
          for kb in range(5):
              acc = [pss[0], pss[1]]
              if kb < 4:
                  W, c0, lts, scale = 512, kb * 512, [(lt, lt // 16, lt % 16) for lt in range(64)], 1.0 / np.sqrt(8192.0)
              else:
                  W, c0, lts, scale = 256, 2048, [(64, 0, 16), (65, 0, 17)], 1.0 / 16.0
              for li_, (lt, s, t) in enumerate(lts):
                  a = gi_ % 2; b3 = gi_ % 3; gi_ += 1
                  k.ts("dve", gt[a][:, 0:W], kr_sb[:, c0:c0 + W], lc_sb2[:, lt:lt + 1], ALU.mult)
                  k.copy("dve", gi[a][:, 0:W], gt[a][:, 0:W])
                  k.copy("dve", gr[a][:, 0:W], gi[a][:, 0:W])
                  k.tt("dve", gt[a][:, 0:W], gt[a][:, 0:W], gr[a][:, 0:W], ALU.subtract)
                  k.act(ga[a][:, 0:W], gt[a][:, 0:W], AF.Abs)
                  k.act(tsin[b3][:, 0:W], gt[a][:, 0:W], AF.Sin, scale=-TWO_PI)
                  k.act(tcos[b3][:, 0:W], ga[a][:, 0:W], AF.Sin, bias=hp[:, 0:1], scale=-TWO_PI)
                  first = li_ == 0
                  last = li_ == len(lts) - 1
                  for m in range(2):
                      k.mm(acc[m][:, 0:W], PQ_sb[:, s, t, m * 128:(m + 1) * 128], tcos[b3][:, 0:W], first, False)
                      k.mm(acc[m][:, 0:W], PQ_sb[:, s, t, 256 + m * 128:256 + (m + 1) * 128], tsin[b3][:, 0:W], False, last,
                           sig=(last or m == 1))
              for m in range(2):
                  zb = zbf[zi % 2]
                  k.act(zb[:, 0:W], acc[m][:, 0:W], AF.Identity, scale=float(scale))
                  pw = pss[2 + zi % 2]
                  k.mm(pw[:, 0:W], wf_b[:, m, :], zb[:, 0:W], True, True)
                  f_ = fo[zi % 2]
                  k.copy("dve", f_[:, 0:W], pw[:, 0:W])
                  k.dma(mixF[m * 128:(m + 1) * 128, c0:c0 + W], f_[:, 0:W], q="sp")
                  zi += 1

    if do_s5:
      with k.scope():
          TW = 2.0 * np.pi
          u_sb = k.sb("u_sb", [64, 4, TOK], BF16)
          for s in range(4):
              k.dma(u_sb[:, s, :], UTall[s, 0:64, :], q="sp")
          p_sb = k.sb("p_sb", [128, 4, 3], F32)
          BT_f = k.sb("BT_f", [64, 4, 2, 128], F32)
          BT_b = k.sb("BT_b", [64, 4, 2, 128], BF16)
          CT_f = k.sb("CT_f", [128, 4, 2, 64], F32)
          CT_b = k.sb("CT_b", [128, 4, 2, 64], BF16)
          tau = k.sb("tau", [128, 2, 512], F32)
          k.dma(p_sb[:, :, :], s5p[:, :, :])
          k.dma(BT_f[:, :, :, :], BTd[:, :, :, :])
          k.dma(CT_f[:, :, :, :], CTd[:, :, :, :])
          k.dma(tau[:, :, :], tauAB[:, :, :])
          k.copy("dve", BT_b[:, :, :, :], BT_f[:, :, :, :])
          k.copy("dve", CT_b[:, :, 0, :], CT_f[:, :, 0, :])
          k.ts("dve", CT_b[:, :, 1, :], CT_f[:, :, 1, :], -1.0, ALU.mult)
          sc = k.sb("sc", [128, 4, 16], F32)
          Cc = [k.sb(f"Cc{i}", [128, 512], F32) for i in range(4)]
          Sn = [k.sb(f"Sn{i}", [128, 512], F32) for i in range(4)]
          R1r = [k.sb(f"R1r{i}", [128, 512], F32) for i in range(4)]
          R1i = [k.sb(f"R1i{i}", [128, 512], F32) for i in range(4)]
          magT = [k.sb(f"magT{i}", [128, 512], F32) for i in range(4)]
          ET = k.sb("ET", [128, 4, 2, 4], F32)
          tb = [k.sb(f"tb{i}", [128, 512], F32) for i in range(3)]
          ti = k.sb("ti", [128, 512], I32)
          onesT = k.sb("onesT", [128, 512], F32)
          k.memset("dve", onesT[:, :], 1.0)

          def frac_sin(out, turns):
              k.copy("dve", ti[:, :], turns)
              k.copy("dve", tb[2][:, :], ti[:, :])
              k.tt("dve", tb[2][:, :], turns, tb[2][:, :], ALU.subtract)
              k.act(out, tb[2][:, :], AF.Sin, scale=TW)

          for dq in range(4):
              c = lambda i: sc[:, dq, i:i + 1]
              ldt, are, aim = p_sb[:, dq, 0:1], p_sb[:, dq, 1:2], p_sb[:, dq, 2:3]
              k.act(c(0), ldt, AF.Exp)
              k.tt("dve", c(1), c(0), are, ALU.mult)
              k.act(c(2), c(1), AF.Exp)
              k.tt("dve", c(3), c(0), aim, ALU.mult)
              k.ts("dve", c(3), c(3), 1.0 / TW, ALU.mult)
              k.ts("dve", c(4), c(3), 32.0, ALU.mult)
              k.copy("dve", ti[:, 0:1], c(4))
              k.copy("dve", c(5), ti[:, 0:1])
              k.tt("dve", c(4), c(4), c(5), ALU.subtract)
              k.ts("dve", tb[0][:, :], tau[:, 0, :], c(4), ALU.mult)
              k.stt("dve", tb[0][:, :], tau[:, 1, :], c(3), tb[0][:, :], ALU.mult, ALU.add)
              frac_sin(Sn[dq][:, :], tb[0][:, :])
              k.ts("dve", tb[1][:, :], tb[0][:, :], 0.25, ALU.add)
              frac_sin(Cc[dq][:, :], tb[1][:, :])
              k.tt("dve", c(6), c(2), Cc[dq][:, 1:2], ALU.mult)
              k.tt("dve", c(7), c(2), Sn[dq][:, 1:2], ALU.mult)
              k.ts("dve", c(8), c(6), -1.0, ALU.add)
              k.tt("dve", c(9), are, are, ALU.mult)
              k.stt("dve", c(9), aim, aim, c(9), ALU.mult, ALU.add)
              k.recip(c(9), c(9))
              k.tt("dve", c(10), c(8), are, ALU.mult)
              k.stt("dve", c(10), c(7), aim, c(10), ALU.mult, ALU.add)
              k.tt("dve", c(10), c(10), c(9), ALU.mult)
              k.tt("dve", c(11), c(7), are, ALU.mult)
              k.tt("dve", c(12), c(8), aim, ALU.mult)
              k.tt("dve", c(11), c(11), c(12), ALU.subtract)
              k.tt("dve", c(11), c(11), c(9), ALU.mult)
              k.ts("dve", c(12), c(10), -1.0, ALU.mult)
              k.ts("dve", R1r[dq][:, :], Cc[dq][:, :], c(10), ALU.mult)
              k.stt("dve", R1r[dq][:, :], Sn[dq][:, :], c(11), R1r[dq][:, :], ALU.mult, ALU.add)
              k.ts("dve", R1i[dq][:, :], Cc[dq][:, :], c(11), ALU.mult)
              k.stt("dve", R1i[dq][:, :], Sn[dq][:, :], c(12), R1i[dq][:, :], ALU.mult, ALU.add)
              k.ts("dve", magT[dq][:, :], onesT[:, :], c(2), ALU.mult)
              for ti_, T in enumerate((256, 512)):
                  er, ei, eni = ET[:, dq, ti_, 0:1], ET[:, dq, ti_, 1:2], ET[:, dq, ti_, 2:3]
                  k.tt("dve", c(13), Cc[dq][:, T - 1:T], Cc[dq][:, 1:2], ALU.mult)
                  k.tt("dve", c(14), Sn[dq][:, T - 1:T], Sn[dq][:, 1:2], ALU.mult)
                  k.tt("dve", er, c(13), c(14), ALU.subtract)
                  k.tt("dve", c(13), Cc[dq][:, T - 1:T], Sn[dq][:, 1:2], ALU.mult)
                  k.tt("dve", c(14), Sn[dq][:, T - 1:T], Cc[dq][:, 1:2], ALU.mult)
                  k.tt("dve", ei, c(13), c(14), ALU.add)
                  k.ts("dve", eni, ei, -1.0, ALU.mult)

          Y_sb = k.sb("Y_sb", [64, NY], F32)
          zin = k.sb("zin", [128, 4, 2], F32)
          k.memset("dve", zin[:, :, :], 0.0)
          wre = [k.sb(f"wre{i}", [128, 512], F32) for i in range(2)]
          wim = [k.sb(f"wim{i}", [128, 512], F32) for i in range(2)]
          ta = [k.sb(f"ta{i}", [128, 512], F32) for i in range(2)]
          zre = [k.sb(f"zre{i}", [128, 512], F32) for i in range(2)]
          zim = [k.sb(f"zim{i}", [128, 512], F32) for i in range(2)]
          pa = [k.sb(f"pa{i}", [128, 512], F32) for i in range(2)]
          pb = [k.sb(f"pb{i}", [128, 512], F32) for i in range(2)]
          sre = [k.sb(f"sre{i}", [128, 512], BF16) for i in range(2)]
          sim_ = [k.sb(f"sim{i}", [128, 512], BF16) for i in range(2)]
          tcar = k.sb("tcar", [128, 2], F32)
          ps_br = [pss[4], pss[5]]
          ps_bi = [pss[6], pss[7]]
          ps_y = [pss[2], pss[3]]
          it = 0
          chunks_f = [(0, 2048, 256, 8192)] + [(c // 4, (c % 4) * 512, 512, c * 512) for c in range(16)]
          chunks_b = [(0, 2048, 256, 8192)] + [(c // 4, (c % 4) * 512, 512, c * 512) for c in reversed(range(16))]
          for d in range(2):
              chunks = chunks_f if d == 0 else chunks_b
              for ci_, (s, c0, T, y0) in enumerate(chunks):
                  ti_ = 0 if T == 256 else 1
                  uu = u_sb[0:64, s, c0:c0 + T]
                  if d == 1:
                      uu = rev(uu)
                  py = ps_y[ci_ % 2]
                  for q in range(2):
                      dq = d * 2 + q
                      a = it % 2; it += 1
                      br, bi_ = ps_br[a], ps_bi[a]
                      k.mm(br[:, 0:T], BT_b[:, dq, 0, :], uu, True, True)
                      k.mm(bi_[:, 0:T], BT_b[:, dq, 1, :], uu, True, True)
                      k.tt("dve", wre[a][:, 0:T], br[:, 0:T], R1r[dq][:, 0:T], ALU.mult)
                      k.tt("dve", ta[a][:, 0:T], bi_[:, 0:T], R1i[dq][:, 0:T], ALU.mult)
                      k.tt("dve", wre[a][:, 0:T], wre[a][:, 0:T], ta[a][:, 0:T], ALU.subtract)
                      k.tt("dve", wim[a][:, 0:T], bi_[:, 0:T], R1r[dq][:, 0:T], ALU.mult)
                      k.tt("dve", ta[a][:, 0:T], br[:, 0:T], R1i[dq][:, 0:T], ALU.mult)
                      k.tt("dve", wim[a][:, 0:T], wim[a][:, 0:T], ta[a][:, 0:T], ALU.add)
                      k.scan(zre[a][:, 0:T], magT[dq][:, 0:T], wre[a][:, 0:T], zin[:, dq, 0:1])
                      k.scan(zim[a][:, 0:T], magT[dq][:, 0:T], wim[a][:, 0:T], zin[:, dq, 1:2])
                      er, ei, eni = ET[:, dq, ti_, 0:1], ET[:, dq, ti_, 1:2], ET[:, dq, ti_, 2:3]
                      k.ts("dve", tcar[:, 0:1], zre[a][:, T - 1:T], er, ALU.mult)
                      k.stt("dve", zin[:, dq, 0:1], zim[a][:, T - 1:T], eni, tcar[:, 0:1], ALU.mult, ALU.add)
                      k.ts("dve", tcar[:, 1:2], zim[a][:, T - 1:T], er, ALU.mult)
                      k.stt("dve", zin[:, dq, 1:2], zre[a][:, T - 1:T], ei, tcar[:, 1:2], ALU.mult, ALU.add)
                      k.tt("pool", pa[a][:, 0:T], zre[a][:, 0:T], Cc[dq][:, 0:T], ALU.mult)
                      k.tt("pool", pb[a][:, 0:T], zim[a][:, 0:T], Sn[dq][:, 0:T], ALU.mult)
                      k.tt("pool", sre[a][:, 0:T], pa[a][:, 0:T], pb[a][:, 0:T], ALU.subtract)
                      k.tt("pool", pa[a][:, 0:T], zim[a][:, 0:T], Cc[dq][:, 0:T], ALU.mult)
                      k.tt("pool", pb[a][:, 0:T], zre[a][:, 0:T], Sn[dq][:, 0:T], ALU.mult)
                      k.tt("pool", sim_[a][:, 0:T], pa[a][:, 0:T], pb[a][:, 0:T], ALU.add)
                      k.mm(py[0:64, 0:T], CT_b[:, dq, 0, :], sre[a][:, 0:T], q == 0, False)
                      k.mm(py[0:64, 0:T], CT_b[:, dq, 1, :], sim_[a][:, 0:T], False, q == 1)
                  if d == 0:
                      k.copy("act", Y_sb[:, y0:y0 + T], py[0:64, 0:T])
                  else:
                      yv = rev(Y_sb[:, y0:y0 + T])
                      k.tt("dve", yv, py[0:64, 0:T], yv, ALU.add)
          for c in range(4):
              k.dma(Yd[:, c * 2112:(c + 1) * 2112], Y_sb[:, c * 2112:(c + 1) * 2112], q="sp")
    return k.finish()


CBLOCKS = [(c * 256, 256, 0) for c in range(8)] + [(2048, 256, 1)]


def build_C():
    k = KB()
    xT = k.dram("xT", [1024, TOK], F32, "ExternalInput")
    modT = k.dram("modT", [128, 48, 2], F32, "ExternalInput")
    mixA = k.dram("mixA", [512, TOK], BF16, "ExternalInput")
    mixF = k.dram("mixF", [256, TOK], BF16, "ExternalInput")
    Ysel = k.dram("Ysel", [4, 64, TOK], F32, "ExternalInput")
    DUT = k.dram("DUT", [256, TOK], F32, "ExternalInput")
    w_glu = k.dram("w_glu", [256, 256], F32, "ExternalInput")
    bglu = k.dram("bglu", [128, 2], F32, "ExternalInput")
    w_out = k.dram("w_out", [1024, 1024], F32, "ExternalInput")
    w_ff1 = k.dram("w_ff1", [1024, 4096], F32, "ExternalInput")
    w_ff2 = k.dram("w_ff2", [4096, 1024], F32, "ExternalInput")
    cmat = k.dram("cmat", [128, 3, 128], BF16, "ExternalInput")
    xo = k.dram("xTo", [1024, TOK], F32, "ExternalOutput")

    W = 256
    wo_bf = k.sb("wo_bf", [128, 8, 1024], BF16)
    wg_bf = k.sb("wg_bf", [128, 2, 256], BF16)
    w1_bf = k.sb("w1_bf", [128, 8, 4096], BF16)
    w2_bf = k.sb("w2_bf", [128, 32, 1024], BF16)
    stage = [k.sb(f"stage{i}", [128, 1024], F32) for i in range(2)]
    mod_sb = k.sb("mod_sb", [128, 48, 2], F32)
    bg_sb = k.sb("bg_sb", [128, 2], F32)
    cmat_sb = k.sb("cmat_sb", [128, 3, 128], BF16)
    cst = k.sb("cst", [128, 2], F32)
    x_sb = k.sb("x_sb", [128, 8, W], F32)
    mix_bf = k.sb("mix_bf", [128, 8, W], BF16, subaxis=1)
    h2_bf = k.sb("h2_bf", [128, 8, W], BF16)
    xsq = k.sb("xsq", [128, 8, W], BF16)
    act_bf = k.sb("act_bf", [128, 32, W], BF16, subaxis=1)
    rstd = k.sb("rstd", [128, W], F32)
    tmp = [k.sb(f"tmp{i}", [128, W], F32) for i in range(2)]
    y_sb = [k.sb(f"y_sb{i}", [128, W], F32) for i in range(2)]
    du_sb = [k.sb(f"du_sb{i}", [128, W], F32) for i in range(2)]
    u1 = [k.sb(f"u1{i}", [128, W], F32) for i in range(2)]
    sg = [k.sb(f"sg{i}", [128, W], F32) for i in range(2)]
    hg = [k.sb(f"hg{i}", [128, W], F32) for i in range(2)]
    hgb = k.sb("hgb", [128, 2, W], BF16)
    rl = [k.sb(f"rl{i}", [128, W], F32) for i in range(2)]
    ps_ss = k.ps("ps_ss")
    ps_g = k.ps("ps_g")
    pm = [k.ps(f"pm{i}") for i in range(4)]

    k.memset("dve", cst[:, 0:1], EPS)
    k.dma(mod_sb[:, :, :], modT[:, :, :])
    k.dma(bg_sb[:, :], bglu[:, :])
    k.dma(cmat_sb[:, :, :], cmat[:, :, :])
    ones_bf = cmat_sb[:, 0, :]
    si = 0

    def load_cast(dst, src):
        nonlocal si
        sg_ = stage[si % 2]
        ncol = dst.ap.shape[-1]
        k.dma(sg_[:, 0:ncol], src, q=("sp", "act")[si % 2])
        k.copy(("dve", "act")[si % 2], dst, sg_[:, 0:ncol])
        si += 1

    gv = w_glu.h.rearrange("(k p) c -> p k c", p=128)
    for kk in range(2):
        load_cast(wg_bf[:, kk, :], w_glu.view(gv[:, kk, :]))
    ov = w_out.h.rearrange("(k p) c -> p k c", p=128)
    for kk in range(8):
        load_cast(wo_bf[:, kk, :], w_out.view(ov[:, kk, :]))
    v1 = w_ff1.h.rearrange("(k p) c -> p k c", p=128)
    for kk in range(8):
        for cc in range(4):
            load_cast(w1_bf[:, kk, cc * 1024:(cc + 1) * 1024], w_ff1.view(v1[:, kk, cc * 1024:(cc + 1) * 1024]))
    v2 = w_ff2.h.rearrange("(k p) c -> p k c", p=128)
    for kk in range(32):
        load_cast(w2_bf[:, kk, :], w_ff2.view(v2[:, kk, :]))

    xv = xT.h.rearrange("(k p) t -> p k t", p=128)
    xov = xo.h.rearrange("(k p) t -> p k t", p=128)
    pi = 0
    for bi, (c0, _, n) in enumerate(CBLOCKS):
        k.dma(x_sb[:, :, :], xT.view(xv[:, :, c0:c0 + W]), q="sp")
        for h in range(4):
            k.dma(mix_bf[:, h, :], mixA[h * 128:(h + 1) * 128, c0:c0 + W], q="sp")
        for m in range(2):
            k.dma(mix_bf[:, 6 + m, :], mixF[m * 128:(m + 1) * 128, c0:c0 + W], q="sp")
        for m in range(2):
            k.dma(y_sb[m][0:64, :], Ysel[2 * m, :, c0:c0 + W], q="sp")
            k.dma(y_sb[m][64:128, :], Ysel[2 * m + 1, :, c0:c0 + W], q="sp")
            k.dma(du_sb[m][:, :], DUT[m * 128:(m + 1) * 128, c0:c0 + W], q="sp")
            k.tt("dve", y_sb[m][:, :], y_sb[m][:, :], du_sb[m][:, :], ALU.add)
            k.tt("dve", u1[m][:, :], y_sb[m][:, :], y_sb[m][:, :], ALU.mult)
            k.ts("dve", u1[m][:, :], u1[m][:, :], 0.044715, ALU.mult, 1.0, ALU.add)
            k.tt("dve", u1[m][:, :], u1[m][:, :], y_sb[m][:, :], ALU.mult)
            k.act(sg[m][:, :], u1[m][:, :], AF.Sigmoid, scale=1.5957691216057308)
            k.tt("dve", hg[m][:, :], y_sb[m][:, :], sg[m][:, :], ALU.mult)
            k.act(hgb[:, m, :], hg[m][:, :], AF.Identity)
        for m in range(2):
            for kt in range(2):
                k.mm(ps_g[:, 0:W], wg_bf[:, kt, m * 128:(m + 1) * 128], hgb[:, kt, :], kt == 0, kt == 1)
            k.act(sg[m][:, :], ps_g[:, 0:W], AF.Sigmoid, bias=bg_sb[:, m:m + 1], scale=1.0)
            k.tt("dve", mix_bf[:, 4 + m, :], hg[m][:, :], sg[m][:, :], ALU.mult)
        for dt in range(8):
            p_ = pm[pi % 4]; pi += 1
            for kt in range(8):
                k.mm(p_[:, 0:W], wo_bf[:, kt, dt * 128:(dt + 1) * 128], mix_bf[:, kt, :], kt == 0, kt == 7)
            k.stt("dve", x_sb[:, dt, :], p_[:, 0:W], mod_sb[:, 2 * 8 + dt, n:n + 1], x_sb[:, dt, :], ALU.mult, ALU.add)
        rmsnorm_mod(k, x_sb, W, n, mod_sb, 3, 4, h2_bf, xsq, ps_ss, rstd, tmp, ones_bf, cst)
        for ft in range(32):
            p_ = pm[pi % 4]; pi += 1
            for kt in range(8):
                k.mm(p_[:, 0:W], w1_bf[:, kt, ft * 128:(ft + 1) * 128], h2_bf[:, kt, :], kt == 0, kt == 7)
            r_ = rl[ft % 2]
            k.act(r_[:, :], p_[:, 0:W], AF.Relu)
            k.tt("dve", act_bf[:, ft, :], r_[:, :], r_[:, :], ALU.mult)
        for dt in range(8):
            p_ = pm[pi % 4]; pi += 1
            for kt in range(32):
                k.mm(p_[:, 0:W], w2_bf[:, kt, dt * 128:(dt + 1) * 128], act_bf[:, kt, :], kt == 0, kt == 31)
            k.stt("dve", x_sb[:, dt, :], p_[:, 0:W], mod_sb[:, 5 * 8 + dt, n:n + 1], x_sb[:, dt, :], ALU.mult, ALU.add)
        k.dma(xo.view(xov[:, :, c0:c0 + W]), x_sb[:, :, :], q="sp")
    return k.finish()


L_SEQ = 8192
CTX = 256
TOKC = 2304


def const_cmat():
    ones = np.ones((128, 128), np.float32)
    p = np.arange(128)
    blk = (p[:, None] // 64 == p[None, :] // 64).astype(np.float32)
    RT = np.zeros((128, 128), np.float32)
    for m in range(128):
        if m % 64 < 32:
            RT[m + 32, m] = -1.0
        else:
            RT[m - 32, m] = 1.0
    return np.stack([ones, blk, RT], axis=1).astype(NPBF)


def const_cs():
    CS = np.zeros((256, 512), np.float64)
    c = np.arange(64)[:, None]
    d = np.arange(64)[None, :]
    ang = 2 * np.pi * c * d / 64.0
    for g in range(4):
        CS[g * 64:(g + 1) * 64, g * 64:(g + 1) * 64] = np.cos(ang) / 8.0
        CS[g * 64:(g + 1) * 64, 256 + g * 64:256 + (g + 1) * 64] = np.sin(ang) / 8.0
    return CS.reshape(2, 128, 512).transpose(1, 0, 2).astype(NPBF)


def const_rope(j):
    t = np.arange(j * 2048, (j + 1) * 2048)
    row = (t // 64).astype(np.float32)
    col = (t % 64).astype(np.float32)
    inv = np.power(np.float32(10000.0), -np.arange(16, dtype=np.float32) / np.float32(16)).astype(np.float32)
    ang = np.concatenate([row[:, None] * inv, col[:, None] * inv], axis=-1).astype(np.float32)
    cos = np.cos(ang).astype(np.float32)
    sin = np.sin(ang).astype(np.float32)
    out = np.zeros((128, 2, TOKC), np.float32)
    pidx = (np.arange(128) % 64) % 32
    out[:, 0, :2048] = cos[:, pidx].T
    out[:, 1, :2048] = sin[:, pidx].T
    out[:, 0, 2048:] = 1.0
    return out


def fm(v, ntile):
    return np.ascontiguousarray(np.asarray(v).reshape(ntile, 128).T)


def const_tauAB():
    tau = np.arange(512)
    out = np.zeros((128, 2, 512), np.float32)
    out[:, 0, :] = (tau // 32)[None, :]
    out[:, 1, :] = (tau % 32)[None, :]
    return out


_DFT_CACHE = {}


def const_dft(j):
    if j in _DFT_CACHE:
        return _DFT_CACHE[j]
    L = 8192
    l = np.arange(L, dtype=np.int64).reshape(64, 128)
    k = (2048 * j + np.arange(2048, dtype=np.int64)).reshape(4, 512)
    kl = (l[None, :, :, None] * k[:, None, None, :]) % L
    ang = kl.astype(np.float64) * (2 * np.pi / L)
    sc = 1.0 / np.sqrt(L)
    out = np.empty((4, 64, 128, 1024), NPBF)
    out[..., :512] = (np.cos(ang) * sc).astype(NPBF)
    out[..., 512:] = (-np.sin(ang) * sc).astype(NPBF)
    _DFT_CACHE[j] = out
    return out


def const_dftc():
    L = 256
    l = np.arange(L, dtype=np.int64).reshape(2, 128)
    k = np.arange(L, dtype=np.int64)
    ang = ((l[:, :, None] * k[None, None, :]) % L).astype(np.float64) * (2 * np.pi / L)
    out = np.empty((2, 128, 512), NPBF)
    out[..., :256] = (np.cos(ang) / 16.0).astype(NPBF)
    out[..., 256:] = (-np.sin(ang) / 16.0).astype(NPBF)
    return out


def s5_layout(inp, li, j):
    s5p = np.zeros((128, 4, 3), np.float32)
    BT = np.zeros((64, 4, 2, 128), np.float32)
    CT = np.zeros((128, 4, 2, 64), np.float32)
    for d in range(2):
        for q in range(2):
            dq = d * 2 + q
            for gl in range(2):
                g = 4 * j + 2 * q + gl
                ps = slice(gl * 64, (gl + 1) * 64)
                s5p[ps, dq, 0] = inp["ssm_log_dt"][li, d, g]
                s5p[ps, dq, 1] = inp["ssm_a_re"][li, d, g]
                s5p[ps, dq, 2] = inp["ssm_a_im"][li, d, g]
                chs = slice((2 * q + gl) * 16, (2 * q + gl + 1) * 16)
                BT[chs, dq, 0, ps] = inp["ssm_b_re"][li, d, g].T
                BT[chs, dq, 1, ps] = inp["ssm_b_im"][li, d, g].T
                CT[ps, dq, 0, chs] = inp["ssm_c_re"][li, d, g].T
                CT[ps, dq, 1, chs] = inp["ssm_c_im"][li, d, g].T
    return s5p, BT, CT


def wfbd_layout(w_fnet_l):
    out = np.zeros((128, 2, 128), np.float32)
    for m in range(2):
        for gl in range(2):
            out[gl * 64:(gl + 1) * 64, m, gl * 64:(gl + 1) * 64] = w_fnet_l[2 * m + gl]
    return out


def const_krow(j):
    out = np.zeros((128, TOKC), np.float32)
    out[:, :2048] = ((2048 * j + np.arange(2048)) / 8192.0).astype(np.float32)[None, :]
    out[:, 2048:] = (np.arange(256) / 256.0).astype(np.float32)[None, :]
    return out


def const_lcol():
    out = np.zeros((128, 66), np.float32)
    p = np.arange(128)
    for lt in range(64):
        out[:, lt] = 128 * lt + p
    out[:, 64] = p
    out[:, 65] = 128 + p
    return out


_NC_CACHE = {}


def _prog(name, builder):
    if name not in _NC_CACHE:
        _NC_CACHE[name] = builder()
    return _NC_CACHE[name]


def _launch(name, builder, in_maps):
    nc = _prog(name, builder)
    res = run_bass_kernel_spmd(nc, in_maps, core_ids=list(range(8)))
    return [{k: np.asarray(v) for k, v in r.items()} for r in res.results]


def kernel(**inputs):
    inp = {k: np.asarray(v) for k, v in inputs.items()}
    x, c, ctx, c_ctx = inp["x"], inp["c"], inp["ctx"], inp["c_ctx"]
    depth = inp["w_in"].shape[0]
    f32 = np.float32
    cm = const_cmat(); csm = const_cs(); tab = const_tauAB(); lc = const_lcol()
    in_maps = []
    for core in range(8):
        b, j = core // 4, core % 4
        cp = np.stack([c[b], c_ctx], -1).astype(f32)
        in_maps.append({
            "cT": np.ascontiguousarray(cp.reshape(8, 128, 2).transpose(1, 0, 2)),
            "w_mod": inp["w_mod"][j], "bmodT": fm(inp["b_mod"][j], 48),
            "gn": np.concatenate([fm(inp["g_norm1"][j], 8), fm(inp["g_norm2"][j], 8)], axis=1),
        })
    modT = [r["modT"] for r in _launch("M", build_M, in_maps)]
    xT = []
    for core in range(8):
        b, j = core // 4, core % 4
        xT.append(np.ascontiguousarray(np.concatenate([x[b, j * 2048:(j + 1) * 2048].T, ctx[b].T], axis=1)))
    ropes = [const_rope(j) for j in range(4)]
    krows = [const_krow(j) for j in range(4)]
    for li in range(depth):
        lam_init = 0.8 - 0.6 * math.exp(-0.3 * li)
        in_maps = []
        for core in range(8):
            b, j = core // 4, core % 4
            in_maps.append({
                "xT": xT[core], "modT": modT[b * 4 + li], "w_in": inp["w_in"][li],
                "gqk": np.stack([np.tile(inp["g_qnorm"][li], 2), np.tile(inp["g_knorm"][li], 2)], -1).astype(f32),
                "rope": ropes[j], "cmat": cm, "cs": csm, "dcol": fm(inp["ssm_d"][li], 2),
            })
        outA = _launch("A", build_A, in_maps)
        in_maps = []
        for core in range(8):
            b, j = core // 4, core % 4
            grp = [outA[b * 4 + s] for s in range(4)]
            s5p, BT, CT = s5_layout(inp, li, j)
            in_maps.append({
                "QT": outA[core]["QT"], "KTall": np.stack([g["KT"] for g in grp]),
                "Vall": np.stack([g["V"] for g in grp]),
                "lamv": np.stack([inp["lam_q1"][li], inp["lam_k1"][li], inp["lam_q2"][li], inp["lam_k2"][li]],
                                 -1).astype(f32),
                "lconst": np.tile(np.array([[lam_init, 1 - lam_init]], f32), (128, 1)),
                "gsub": inp["g_subln"][li].reshape(128, 1).astype(f32), "cmat": cm,
                "UTall": np.stack([g["UT"][64 * j:64 * (j + 1)] for g in grp]),
                "s5p": s5p, "BT": BT, "CT": CT, "tauAB": tab,
                "PQall": np.stack([g["PQ"] for g in grp]), "krow": krows[j], "lcol": lc,
                "wfbd": wfbd_layout(inp["w_fnet"][li]),
            })
        outB = _launch("B", build_B, in_maps)
        in_maps = []
        for core in range(8):
            b, j = core // 4, core % 4
            Ys = [outB[b * 4 + s]["Y"] for s in range(4)]
            ysel = np.stack([np.concatenate([y[:, 2048 * j:2048 * (j + 1)], y[:, 8192:]], 1) for y in Ys])
            in_maps.append({
                "xT": xT[core], "modT": modT[b * 4 + li], "mixA": outB[core]["mixA"], "mixF": outB[core]["mixF"],
                "Ysel": ysel, "DUT": outA[core]["DUT"], "w_glu": inp["w_glu"][li], "bglu": fm(inp["b_glu"][li], 2),
                "w_out": inp["w_out"][li], "w_ff1": inp["w_ff1"][li], "w_ff2": inp["w_ff2"][li], "cmat": cm,
            })
        outC = _launch("C", build_C, in_maps)
        xT = [r["xTo"] for r in outC]
    out = np.empty(x.shape, f32)
    for core in range(8):
        b, j = core // 4, core % 4
        out[b, j * 2048:(j + 1) * 2048, :] = xT[core][:, :2048].T
    return out
```
